# Optimizing a Trainium2 kernel written in Bass

```python
import jax, jax.numpy as jnp
from jax import lax
import numpy as np

D_MODEL = 1024
BATCH = 32
SEQ = 2048
DEPTH = 4

GRID_W = 64
CTX_LEN = 256
HEAD_DIM = 64
ATTN_W = D_MODEL // 2
N_Q_HEADS = ATTN_W // HEAD_DIM
N_KV_HEADS = N_Q_HEADS // 4
Q_PER_KV = N_Q_HEADS // N_KV_HEADS
KV_W = N_KV_HEADS * HEAD_DIM
POOL_W = D_MODEL // 4
POOL_WINDOWS = (2, 4, 8, 16)
N_POOL_GROUPS = len(POOL_WINDOWS)
POOL_GROUP_W = POOL_W // N_POOL_GROUPS
FFT_W = D_MODEL // 4
N_FFT_GROUPS = 4
FFT_GROUP_W = FFT_W // N_FFT_GROUPS
MIX_W = ATTN_W + POOL_W + FFT_W
IN_W = ATTN_W + 2 * KV_W + POOL_W + FFT_W
D_FF = ((8 * D_MODEL // 3 + 127) // 128) * 128
ROPE_THETA = 10000.0
Q_BLOCK = 128
N_MOD = 9
EPS = 1e-6
ATTN_SCALE = HEAD_DIM ** -0.5

kernel_name = 'hybrid_attn_pool_fourier_dit_block'


def rms_norm(x):
    xf = x.astype(jnp.float32)
    return (xf * lax.rsqrt(jnp.mean(xf * xf, axis=-1, keepdims=True) + EPS)).astype(x.dtype)


def modulate(h, shift, scale):
    return h * (1 + scale[:, None, :]) + shift[:, None, :]


def swiglu(h, w_gu, w_down):
    g, u = jnp.split(h @ w_gu, 2, axis=-1)
    return (jax.nn.silu(g) * u) @ w_down


def axial_rope_tables(length):
    rows = length // GRID_W
    row = jnp.repeat(jnp.arange(rows), GRID_W).astype(jnp.float32)
    col = jnp.tile(jnp.arange(GRID_W), rows).astype(jnp.float32)
    n_freq = HEAD_DIM // 4
    inv_freq = ROPE_THETA ** (-jnp.arange(n_freq, dtype=jnp.float32) / n_freq)
    ang = jnp.concatenate([row[:, None] * inv_freq, col[:, None] * inv_freq], axis=-1)
    return jnp.cos(ang), jnp.sin(ang)


def apply_rope(x, cos, sin):
    shape = (1, cos.shape[0]) + (1,) * (x.ndim - 3) + (cos.shape[1],)
    cos = cos.reshape(shape).astype(x.dtype)
    sin = sin.reshape(shape).astype(x.dtype)
    half = HEAD_DIM // 2
    x1, x2 = x[..., :half], x[..., half:]
    return jnp.concatenate([x1 * cos - x2 * sin, x1 * sin + x2 * cos], axis=-1)


def split_in(z):
    b = [ATTN_W, ATTN_W + KV_W, ATTN_W + 2 * KV_W, ATTN_W + 2 * KV_W + POOL_W]
    return jnp.split(z, b, axis=-1)


def q_heads(q, gain):
    q = q.reshape(q.shape[:2] + (N_KV_HEADS, Q_PER_KV, HEAD_DIM))
    return rms_norm(q) * gain


def kv_heads(t):
    return t.reshape(t.shape[:2] + (N_KV_HEADS, HEAD_DIM))


def attend_latent(q, k, v, k_ctx, v_ctx):
    k_all = jnp.concatenate([k, k_ctx], axis=1)
    v_all = jnp.concatenate([v, v_ctx], axis=1)
    bsz, length = q.shape[:2]
    n_blk = length // Q_BLOCK
    qb = q.reshape((bsz, n_blk, Q_BLOCK) + q.shape[2:]).transpose(1, 0, 2, 3, 4, 5)

    def one_block(q_blk):
        s = jnp.einsum('bqkgd,bskd->bkgqs', q_blk, k_all).astype(jnp.float32) * ATTN_SCALE
        p = jax.nn.softmax(s, axis=-1).astype(v_all.dtype)
        return jnp.einsum('bkgqs,bskd->bqkgd', p, v_all)

    o = lax.map(one_block, qb)
    return o.transpose(1, 0, 2, 3, 4, 5).reshape(bsz, length, ATTN_W)


def attend_context(q, k, v):
    s = jnp.einsum('bqkgd,bskd->bkgqs', q, k).astype(jnp.float32) * ATTN_SCALE
    p = jax.nn.softmax(s, axis=-1).astype(v.dtype)
    o = jnp.einsum('bkgqs,bskd->bqkgd', p, v)
    return o.reshape(q.shape[0], q.shape[1], ATTN_W)


def pool_minus_self(u):
    bsz, length, _ = u.shape
    ug = u.reshape(bsz, length, N_POOL_GROUPS, POOL_GROUP_W).astype(jnp.float32)
    cs = jnp.cumsum(ug, axis=1)
    cs = jnp.concatenate([jnp.zeros_like(cs[:, :1]), cs], axis=1)
    t = jnp.arange(length)
    means = []
    for g, w in enumerate(POOL_WINDOWS):
        lo = jnp.clip(t - w // 2, 0, length)
        hi = jnp.clip(t + w - w // 2, 0, length)
        cs_g = cs[:, :, g]
        s = jnp.take(cs_g, hi, axis=1) - jnp.take(cs_g, lo, axis=1)
        cnt = (hi - lo).astype(jnp.float32)[None, :, None]
        means.append(s / cnt)
    mean = jnp.stack(means, axis=2)
    return (mean - ug).astype(u.dtype)


def fourier_2d(u):
    bsz, length, _ = u.shape
    ug = u.reshape(bsz, length, N_FFT_GROUPS, FFT_GROUP_W).astype(jnp.float32)
    f = jnp.fft.fft2(ug, axes=(1, 3), norm='ortho').real
    return f.reshape(bsz, length, FFT_W).astype(u.dtype)


def merge_mixers(attn_out, u_pool, u_fft, pool_w_l, pool_scale_l, fft_w_l, w_out_l):
    bsz, length, _ = u_pool.shape
    yp = jnp.einsum('blgc,gcd->blgd', pool_minus_self(u_pool), pool_w_l).reshape(bsz, length, POOL_W) * pool_scale_l
    yf = fourier_2d(u_fft) @ fft_w_l
    return jnp.concatenate([attn_out, yp, yf], axis=-1) @ w_out_l


def setup_inputs(seed: int = 0) -> dict:
    key = jax.random.key(seed)
    ks = jax.random.split(key, 20)

    def nrm(k, shape, std):
        return jax.random.normal(k, shape, jnp.float32) * std

    return {
        'x': nrm(ks[0], (BATCH, SEQ, D_MODEL), 1.0),
        'c': nrm(ks[1], (BATCH, D_MODEL), 1.0),
        'ctx': nrm(ks[2], (BATCH, CTX_LEN, D_MODEL), 1.0),
        'c_ctx': nrm(ks[3], (D_MODEL,), 1.0),
        'ada_w': nrm(ks[4], (DEPTH, D_MODEL, N_MOD * D_MODEL), 0.5 * D_MODEL ** -0.5),
        'ada_b': nrm(ks[5], (DEPTH, N_MOD * D_MODEL), 0.01),
        'ffn1_w_gu': nrm(ks[6], (DEPTH, D_MODEL, 2 * D_FF), D_MODEL ** -0.5),
        'ffn1_w_down': nrm(ks[7], (DEPTH, D_FF, D_MODEL), D_FF ** -0.5),
        'w_in': nrm(ks[8], (DEPTH, D_MODEL, IN_W), D_MODEL ** -0.5),
        'q_gain': 1.0 + nrm(ks[9], (DEPTH, HEAD_DIM), 0.1),
        'k_gain': 1.0 + nrm(ks[10], (DEPTH, HEAD_DIM), 0.1),
        'pool_w': nrm(ks[11], (DEPTH, N_POOL_GROUPS, POOL_GROUP_W, POOL_GROUP_W), POOL_GROUP_W ** -0.5),
        'pool_scale': 1.0 + nrm(ks[12], (DEPTH, POOL_W), 0.1),
        'fft_w': nrm(ks[13], (DEPTH, FFT_W, FFT_W), FFT_W ** -0.5),
        'w_out': nrm(ks[14], (DEPTH, MIX_W, D_MODEL), MIX_W ** -0.5),
        'ffn2_w_gu': nrm(ks[15], (DEPTH, D_MODEL, 2 * D_FF), D_MODEL ** -0.5),
        'ffn2_w_down': nrm(ks[16], (DEPTH, D_FF, D_MODEL), D_FF ** -0.5),
        'final_gain': 1.0 + nrm(ks[17], (D_MODEL,), 0.1),
    }


def reference(x, c, ctx, c_ctx, ada_w, ada_b, ffn1_w_gu, ffn1_w_down, w_in, q_gain, k_gain,
              pool_w, pool_scale, fft_w, w_out, ffn2_w_gu, ffn2_w_down, final_gain):
    cos, sin = axial_rope_tables(x.shape[1])
    for layer in range(DEPTH):
        last = layer == DEPTH - 1
        mx = jnp.split(jax.nn.silu(c) @ ada_w[layer] + ada_b[layer], N_MOD, axis=-1)
        mc = jnp.split((jax.nn.silu(c_ctx) @ ada_w[layer] + ada_b[layer])[None, :], N_MOD, axis=-1)

        x = x + 0.5 * mx[2][:, None] * swiglu(modulate(rms_norm(x), mx[0], mx[1]), ffn1_w_gu[layer], ffn1_w_down[layer])
        ctx = ctx + 0.5 * mc[2][:, None] * swiglu(modulate(rms_norm(ctx), mc[0], mc[1]), ffn1_w_gu[layer], ffn1_w_down[layer])

        hx = modulate(rms_norm(x), mx[3], mx[4])
        hc = modulate(rms_norm(ctx), mc[3], mc[4])
        if last:
            kc, vc = jnp.split(hc @ w_in[layer, :, ATTN_W:ATTN_W + 2 * KV_W], 2, axis=-1)
        else:
            qc, kc, vc, upc, ufc = split_in(hc @ w_in[layer])
        kc = rms_norm(kv_heads(kc)) * k_gain[layer]
        vc = kv_heads(vc)

        qx, kx, vx, upx, ufx = split_in(hx @ w_in[layer])
        qx = apply_rope(q_heads(qx, q_gain[layer]), cos, sin)
        kx = apply_rope(rms_norm(kv_heads(kx)) * k_gain[layer], cos, sin)
        vx = kv_heads(vx)
        ax = attend_latent(qx, kx, vx, kc, vc)
        x = x + mx[5][:, None] * merge_mixers(ax, upx, ufx, pool_w[layer], pool_scale[layer], fft_w[layer], w_out[layer])

        x = x + 0.5 * mx[8][:, None] * swiglu(modulate(rms_norm(x), mx[6], mx[7]), ffn2_w_gu[layer], ffn2_w_down[layer])

        if not last:
            ac = attend_context(q_heads(qc, q_gain[layer]), kc, vc)
            ctx = ctx + mc[5][:, None] * merge_mixers(ac, upc, ufc, pool_w[layer], pool_scale[layer], fft_w[layer], w_out[layer])
            ctx = ctx + 0.5 * mc[8][:, None] * swiglu(modulate(rms_norm(ctx), mc[6], mc[7]), ffn2_w_gu[layer], ffn2_w_down[layer])

    return rms_norm(x) * final_gain
```

```python
import numpy as np
import concourse.bass as bass
import concourse.mybir as mybir
from concourse.bass_utils import run_bass_kernel_spmd

F32 = mybir.dt.float32
BF16 = mybir.dt.bfloat16
AF = mybir.ActivationFunctionType
ALU = mybir.AluOpType

D = 1024
SEQ = 2048
CTX = 256
DEPTH = 4
DFF = 2816
NFC = 22
INW = 1280
EPS = 1e-6
NB_CORE = 4
NCORES = 8
ENGS = ('pe', 'act', 'dve', 'pool', 'sp')


class _Op:
    __slots__ = ('id', 'eng', 'fn', 'deps', 'signal', 'sem', 'val', 'dma', 'epoch')


class _Res:
    __slots__ = ('w', 'r', 'rd')

    def __init__(self):
        self.w = None
        self.r = {}
        self.rd = []


class DmaSem:
    def __init__(self, sem):
        self.sem = sem
        self.count = 0


class Sched:
    def __init__(self, nc):
        self.nc = nc
        self.ops = {e: [] for e in ENGS}
        self.all = []
        self.res = {}
        self.epoch = 0
        self.pending_barrier = {e: set() for e in ENGS}
        self.dma_since_barrier = []
        self.engsem = {}
        self.dmasems = []

    def dma_sem(self, name):
        s = DmaSem(self.nc.alloc_semaphore(name))
        self.dmasems.append(s)
        return s

    def new_epoch(self):
        self.epoch += 1

    def op(self, eng, fn, reads=(), writes=(), dma=None):
        o = _Op()
        o.id = len(self.all)
        o.eng = eng
        o.fn = fn
        o.signal = False
        o.dma = dma
        o.epoch = self.epoch
        o.sem = None
        o.val = None
        deps = set(self.pending_barrier[eng])
        self.pending_barrier[eng] = set()
        for r in reads:
            st = self.res.get(r)
            if st is not None and st.w is not None:
                deps.add(st.w)
        for w in writes:
            st = self.res.get(w)
            if st is not None:
                if st.w is not None:
                    deps.add(st.w)
                deps.update(st.r.values())
                deps.update(st.rd)
        for r in reads:
            st = self.res.setdefault(r, _Res())
            if dma is not None:
                st.rd.append(o.id)
            else:
                st.r[eng] = o.id
        for w in writes:
            st = self.res.setdefault(w, _Res())
            st.w = o.id
            st.r = {}
            st.rd = []
        best = {}
        keep = set()
        for d in deps:
            p = self.all[d]
            if p.dma is not None:
                keep.add(d)
            else:
                k = (p.eng, p.epoch)
                if k not in best or best[k] < d:
                    best[k] = d
        keep.update(best.values())
        o.deps = keep
        if dma is not None:
            dma.count += 16
            o.sem = dma.sem
            o.val = dma.count
            self.dma_since_barrier.append(o.id)
        self.all.append(o)
        self.ops[eng].append(o)
        return o

    def barrier(self):
        last = set()
        for e in ENGS:
            for o in reversed(self.ops[e]):
                if o.dma is None:
                    last.add(o.id)
                    break
        last.update(self.dma_since_barrier)
        self.dma_since_barrier = []
        for e in ENGS:
            self.pending_barrier[e].update(last)

    def emit(self, block):
        nc = self.nc
        for o in self.all:
            for d in o.deps:
                self.all[d].signal = True
        nep = self.epoch + 1
        for e in ENGS:
            if e == 'sp':
                continue
            self.engsem[e] = [nc.alloc_semaphore(f"s_{e}_{i}") for i in range(nep)]
        for e in ENGS:
            cnt = {}
            for o in self.ops[e]:
                if o.dma is None and o.signal:
                    if e == 'sp':
                        raise RuntimeError("non-dma op on sp cannot signal")
                    cnt[o.epoch] = cnt.get(o.epoch, 0) + 1
                    o.val = cnt[o.epoch]
                    o.sem = self.engsem[e][o.epoch]
        allops = self.all

        def run(name, h):
            waited = {}
            for o in self.ops[name]:
                for d in sorted(o.deps):
                    p = allops[d]
                    key = id(p.sem)
                    if waited.get(key, 0) >= p.val:
                        continue
                    h.wait_ge(p.sem, p.val)
                    waited[key] = p.val
                ins = o.fn(h)
                if o.dma is not None:
                    ins.then_inc(o.sem, 16)
                elif o.signal:
                    ins.then_inc(o.sem, 1)

        @block.tensor
        def _(h):
            run('pe', h)

        @block.scalar
        def _(h):
            run('act', h)

        @block.vector
        def _(h):
            run('dve', h)

        @block.gpsimd
        def _(h):
            run('pool', h)

        @block.sync
        def _(h):
            run('sp', h)


class Stream:
    def __init__(self, S, name, slots, units):
        self.S = S
        self.name = name
        self.slots = slots
        self.units = units
        self.issued = 0
        self.got = 0

    def _issue(self):
        i = self.issued
        ap, sem = self.slots[i % len(self.slots)]
        src = self.units[i]
        self.S.op('sp', lambda h, ap=ap, src=src: h.dma_start(out=ap, in_=src),
                  writes=[(self.name, i % len(self.slots))], dma=sem)
        self.issued += 1

    def start(self):
        while self.issued < min(len(self.units), len(self.slots)):
            self._issue()

    def get(self):
        i = self.got
        while self.issued < min(len(self.units), i + len(self.slots)):
            self._issue()
        self.got += 1
        return self.slots[i % len(self.slots)][0], (self.name, i % len(self.slots))


def build_nc(cfg):
    nb = cfg.get('nb', NB_CORE)
    depth = cfg.get('depth', DEPTH)
    stages = cfg.get('stages', 'full')
    final_norm = cfg.get('final_norm', True)
    mixdbg = cfg.get('mixdbg', 9)
    pmdbg = cfg.get('pmdbg', 9)

    nc = bass.Bass("TRN2", target_bir_lowering=False)
    S = Sched(nc)

    def dram_in(name, shape, dt=F32):
        return nc.dram_tensor(name, list(shape), dt, kind="ExternalInput").ap()

    x_d = dram_in("x", [nb, SEQ, D])
    c_d = dram_in("c", [nb, D])
    ctx_d = dram_in("ctx", [nb, CTX, D])
    cctx_d = dram_in("c_ctx", [D])
    adaw_d = dram_in("ada_w", [DEPTH, D, 9 * D])
    adab_d = dram_in("ada_b", [DEPTH, 9 * D])
    wgu_d = [dram_in("ffn1_w_gu", [DEPTH, D, 2 * DFF]), dram_in("ffn2_w_gu", [DEPTH, D, 2 * DFF])]
    wdn_d = [dram_in("ffn1_w_down", [DEPTH, DFF, D]), dram_in("ffn2_w_down", [DEPTH, DFF, D])]
    win_d = dram_in("w_in", [DEPTH, D, INW])
    qg_d = dram_in("q_gain", [DEPTH, 64])
    kg_d = dram_in("k_gain", [DEPTH, 64])
    poolw_d = dram_in("pool_w", [DEPTH, 4, 64, 64])
    pools_d = dram_in("pool_scale", [DEPTH, 256])
    fftw_d = dram_in("fft_w", [DEPTH, 256, 256])
    wout_d = dram_in("w_out", [DEPTH, D, D])
    fg_d = dram_in("final_gain", [D])
    ident_d = dram_in("k_ident", [128, 128])
    kdft_d = dram_in("k_dft", [32, 128, 2048])
    kcdft_d = dram_in("k_cdft", [128, 1024])
    kbd_d = dram_in("k_bd", [2, 128, 128])
    krope_d = dram_in("k_rope", [2, 128, SEQ])
    kperm_d = dram_in("k_perm", [128, 128])
    kswap_d = dram_in("k_swap", [128, 128])
    kinvw_d = dram_in("k_invw", [128, 2])
    krcnt_d = dram_in("k_rcnt", [128, 32])
    krcf_d = dram_in("k_rcf", [3, 128, 1024])
    out_d = nc.dram_tensor("out", [nb, SEQ, D], F32, kind="ExternalOutput").ap()
    win_s = nc.dram_tensor("win_s", [DEPTH, 128, 10 * 8 * 128], BF16, kind="Internal").ap()
    wout_s = nc.dram_tensor("wout_s", [DEPTH, 128, 8 * 8 * 128], BF16, kind="Internal").ap()
    ab_s = nc.dram_tensor("ab_s", [DEPTH, 128, 2 * 512], BF16, kind="Internal").ap()
    dft_s = nc.dram_tensor("dft_s", [32, 128, 2048], BF16, kind="Internal").ap()

    wgu_s = [nc.dram_tensor(f"wgu_s{f}", [DEPTH, 128, NFC * 8 * 256], BF16, kind="Internal").ap() for f in range(2)]
    wdn_s = [nc.dram_tensor(f"wdn_s{f}", [DEPTH, 128, 8 * NFC * 128], BF16, kind="Internal").ap() for f in range(2)]

    xT = nc.alloc_sbuf_tensor("xT", [128, 8, SEQ], F32)
    cT = nc.alloc_sbuf_tensor("cT", [128, 8, CTX], F32)
    mod = nc.alloc_sbuf_tensor("mod", [128, DEPTH, 72, 5], F32)
    ident = nc.alloc_sbuf_tensor("ident", [128, 128], F32)
    ones_bf = nc.alloc_sbuf_tensor("ones_bf", [128, 128], BF16)
    fgain = nc.alloc_sbuf_tensor("fgain", [128, 8], F32)
    perm_bf = nc.alloc_sbuf_tensor("perm_bf", [128, 128], BF16)
    bdones_bf = nc.alloc_sbuf_tensor("bdones_bf", [128, 128], BF16)
    esel_bf = nc.alloc_sbuf_tensor("esel_bf", [128, 2, 128], BF16)
    swap_f = nc.alloc_sbuf_tensor("swap_f", [128, 128], F32)
    cdft_bf = nc.alloc_sbuf_tensor("cdft_bf", [128, 2, 2, 256], BF16)
    poolw_bf = nc.alloc_sbuf_tensor("poolw_bf", [128, DEPTH, 2, 128], BF16)
    pscale = nc.alloc_sbuf_tensor("pscale", [128, DEPTH, 2], F32)
    qg = nc.alloc_sbuf_tensor("qg", [128, DEPTH], F32)
    kg = nc.alloc_sbuf_tensor("kg", [128, DEPTH], F32)
    invw = nc.alloc_sbuf_tensor("invw", [128, 2], F32)
    rcnt = nc.alloc_sbuf_tensor("rcnt", [128, 2, 16], F32)
    WORK_BYTES = 120 * 1024
    work = nc.alloc_sbuf_tensor("work", [128, WORK_BYTES // 2], BF16)

    class Arena:
        def __init__(self):
            self.off = 0

        def alloc(self, shape_free, dt):
            n = int(np.prod(shape_free))
            esz = 4 if dt == F32 else 2
            nbytes = (n * esz + 63) // 64 * 64
            assert self.off + nbytes <= WORK_BYTES, (self.off, nbytes)
            v = work[:, self.off // 2:(self.off + n * esz) // 2]
            self.off += nbytes
            if dt == F32:
                v = v.bitcast(F32)
            if len(shape_free) == 2:
                v = v.rearrange("p (a b) -> p a b", a=shape_free[0])
            elif len(shape_free) == 3:
                v = v.rearrange("p (a b c) -> p a b c", a=shape_free[0], b=shape_free[1])
            return v

    pp = [nc.alloc_psum_tensor(f"pp{i}", [128, 1024], F32) for i in range(4)]
    ps = [pp[i // 2][:, (i % 2) * 512:(i % 2 + 1) * 512] for i in range(8)]

    sem_misc = S.dma_sem("misc")
    S.op('sp', lambda h: h.dma_start(out=ident[:], in_=ident_d[:, :]), writes=['ident'], dma=sem_misc)
    S.op('dve', lambda h: h.memset(ones_bf[:], 1.0), writes=['ones_bf'])
    with nc.allow_non_contiguous_dma(reason="tiny one-time vector loads"):
        pass
    S.op('sp', lambda h: h.dma_start(out=fgain[:], in_=fg_d.rearrange("(k p) -> p k", p=128),
                                     allow_slow_non_contiguous=True), writes=['fgain'], dma=S.dma_sem("fg"))

    def prologue_ffn():
        A = Arena()
        stage = [A.alloc([DFF], F32) for _ in range(2)]
        stsem = [S.dma_sem(f"pst{i}") for i in range(2)]
        big = A.alloc([NFC * 8 * 256], BF16)
        bigsem = S.dma_sem("pbig")
        cnt = 0

        def cp(eng, dst, src, reads, writes):
            if eng == 'act':
                S.op('act', lambda h: h.copy(out=dst, in_=src), reads=reads, writes=writes)
            else:
                S.op('dve', lambda h: h.tensor_copy(out=dst, in_=src), reads=reads, writes=writes)

        for l in range(depth):
            for f in range(2):
                bigv = big.rearrange("p (j k h c) -> p j k h c", j=NFC, k=8, h=2)
                for kc in range(8):
                    for hh in range(2):
                        st = stage[cnt % 2]
                        rs = ('pstage', cnt % 2)
                        S.op('sp', lambda h, st=st, l=l, f=f, kc=kc, hh=hh: h.dma_start(
                            out=st, in_=wgu_d[f][l, kc * 128:(kc + 1) * 128, hh * DFF:(hh + 1) * DFF]),
                            writes=[rs], dma=stsem[cnt % 2])
                        cp('act' if cnt % 2 == 0 else 'dve', bigv[:, :, kc, hh, :],
                           st.rearrange("p (j c) -> p j c", j=NFC), [rs], [('pbig', cnt % 2)])
                        cnt += 1
                S.op('sp', lambda h, l=l, f=f: h.dma_start(out=wgu_s[f][l], in_=big),
                     reads=[('pbig', 0), ('pbig', 1)], writes=[('wgu_s', f, l), ('pbig', 0), ('pbig', 1)], dma=bigsem)
                bigd = big[:, 0:8 * NFC * 128].rearrange("p (d j c) -> p d j c", d=8, j=NFC)
                for q in range(11):
                    st = stage[cnt % 2]
                    rs = ('pstage', cnt % 2)
                    stv = st[:, 0:2048].rearrange("p (j d) -> p j d", j=2)
                    S.op('sp', lambda h, stv=stv, l=l, f=f, q=q: h.dma_start(
                        out=stv, in_=wdn_d[f][l, q * 256:(q + 1) * 256, :].rearrange("(j p) d -> p j d", p=128)),
                        writes=[rs], dma=stsem[cnt % 2])
                    cp('act' if cnt % 2 == 0 else 'dve', bigd[:, :, q * 2:(q + 1) * 2, :],
                       stv.rearrange("p j (d c) -> p d j c", d=8), [rs], [('pbig', cnt % 2)])
                    cnt += 1
                S.op('sp', lambda h, l=l, f=f: h.dma_start(out=wdn_s[f][l], in_=big[:, 0:8 * NFC * 128]),
                     reads=[('pbig', 0), ('pbig', 1)], writes=[('wdn_s', f, l), ('pbig', 0), ('pbig', 1)], dma=bigsem)
        S.barrier()

    def prologue_mod():
        A = Arena()
        ccT = A.alloc([8, 5], F32)
        adab = A.alloc([DEPTH, 72], F32)
        wst = [A.alloc([8, 1152], F32) for _ in range(2)]
        wsem = [S.dma_sem(f"adaw{i}") for i in range(2)]
        sm = S.dma_sem("modmisc")
        for b in range(nb):
            S.op('sp', lambda h, b=b: h.dma_start(out=ccT[:, :, b:b + 1], in_=c_d[b].rearrange("(k p o) -> p k o", p=128, o=1),
                                                  allow_slow_non_contiguous=True), writes=[('ccT', b)], dma=sm)
        S.op('sp', lambda h: h.dma_start(out=ccT[:, :, 4:5], in_=cctx_d.rearrange("(k p o) -> p k o", p=128, o=1),
                                         allow_slow_non_contiguous=True), writes=[('ccT', 4)], dma=sm)
        if nb < 4:
            for b in range(nb, 4):
                S.op('dve', lambda h, b=b: h.memset(ccT[:, :, b:b + 1], 0.0), writes=[('ccT', b)])
        S.op('sp', lambda h: h.dma_start(out=adab, in_=adab_d.rearrange("l (f p) -> p l f", p=128),
                                         allow_slow_non_contiguous=True), writes=['adab'], dma=sm)
        S.op('act', lambda h: h.activation(out=ccT, in_=ccT, func=AF.Silu),
             reads=[('ccT', b) for b in range(5)], writes=['ccS'])
        cnt = 0
        for l in range(depth):
            for piece in range(8):
                st = wst[cnt % 2]
                rs = ('adawst', cnt % 2)
                S.op('sp', lambda h, st=st, l=l, piece=piece: h.dma_start(
                    out=st, in_=adaw_d[l, :, piece * 1152:(piece + 1) * 1152].rearrange("(k p) f -> p k f", p=128)),
                    writes=[rs], dma=wsem[cnt % 2])
                bank = ps[cnt % 2]

                def mm(h, st=st, bank=bank):
                    ins = None
                    for fcl in range(9):
                        for kc in range(8):
                            ins = h.matmul(bank[:, fcl * 8:fcl * 8 + 5], lhsT=st[:, kc, fcl * 128:(fcl + 1) * 128],
                                           rhs=ccT[:, kc, :], start=(kc == 0), stop=(kc == 7))
                    return ins
                S.op('pe', mm, reads=[rs, 'ccS'], writes=[('psb', cnt % 2)])
                S.op('dve', lambda h, bank=bank, l=l, piece=piece: h.tensor_tensor(
                    out=mod[:, l, piece * 9:(piece + 1) * 9, :],
                    in0=bank[:, 0:72].rearrange("p (f e) -> p f e", e=8)[:, :, 0:5],
                    in1=adab[:, l, piece * 9:(piece + 1) * 9].unsqueeze(2).broadcast_to([128, 9, 5]), op=ALU.add),
                    reads=[('psb', cnt % 2), 'adab'], writes=[('mod', l)])
                cnt += 1
            for m in (1, 4, 7):
                S.op('dve', lambda h, l=l, m=m: h.tensor_scalar_add(out=mod[:, l, m * 8:(m + 1) * 8, :], in0=mod[:, l, m * 8:(m + 1) * 8, :], scalar1=1.0),
                     reads=[('mod', l)], writes=[('mod', l)])
            for m in (2, 8):
                S.op('dve', lambda h, l=l, m=m: h.tensor_scalar_mul(out=mod[:, l, m * 8:(m + 1) * 8, :], in0=mod[:, l, m * 8:(m + 1) * 8, :], scalar1=0.5),
                     reads=[('mod', l)], writes=[('mod', l)])
        S.barrier()

    def modv(l, m, kc, b):
        return mod[:, l, m * 8 + kc, b:b + 1]

    class FFNBufs:
        def __init__(self):
            A = Arena()
            self.hT = [A.alloc([8, 512], BF16) for _ in range(2)]
            self.sq = A.alloc([8, 512], BF16)
            self.t = A.alloc([8, 512], F32)
            self.rs = A.alloc([512], F32)
            self.sg = [A.alloc([512], F32) for _ in range(2)]
            self.act = A.alloc([NFC, 512], BF16)
            self.wgu = [(A.alloc([8, 256], BF16), S.dma_sem(f"wgu{i}")) for i in range(4)]
            self.wdn = [(A.alloc([NFC, 128], BF16), S.dma_sem(f"wdn{i}")) for i in range(2)]
            self.end = A.off
            self.tile_i = 0

    class IOBufs:
        def __init__(self, base):
            A = Arena()
            A.off = base
            self.stg = [(A.alloc([1024], F32), S.dma_sem(f"io{i}")) for i in range(2)]
            self.i = 0

    ffnb = None
    iob = None

    def rms_stats(xv, T, scale, key_in, bufs):
        S.op('act', lambda h: h.activation(out=bufs.sq[:, :, :T], in_=xv, func=AF.Square),
             reads=[key_in], writes=['sq'])

        def mm(h):
            ins = None
            for kc in range(8):
                ins = h.matmul(ps[6][:, :T], lhsT=ones_bf[:], rhs=bufs.sq[:, kc, :T], start=(kc == 0), stop=(kc == 7))
            return ins
        S.op('pe', mm, reads=['sq', 'ones_bf'], writes=['ps6'])
        S.op('act', lambda h: h.activation(out=bufs.rs[:, :T], in_=ps[6][:, :T], func=AF.Sqrt, scale=scale, bias=EPS),
             reads=['ps6'], writes=['rs'])
        S.op('dve', lambda h: h.reciprocal(out=bufs.rs[:, :T], in_=bufs.rs[:, :T]), reads=['rs'], writes=['rs'])

    def ffn_tile(l, f, xv, T, b, xkey):
        B = ffnb
        ti = B.tile_i
        B.tile_i += 1
        hT = B.hT[ti % 2]
        hkey = ('hT', ti % 2)
        m0 = 0 if f == 0 else 6
        rms_stats(xv, T, 1.0 / D, xkey, B)
        for kc in range(8):
            S.op('dve', lambda h, kc=kc: h.scalar_tensor_tensor(
                out=B.t[:, kc, :T], in0=xv[:, kc, :], scalar=modv(l, m0 + 1, kc, b), in1=B.rs[:, :T],
                op0=ALU.mult, op1=ALU.mult), reads=[xkey, 'rs'], writes=[('t', kc)])
            S.op('act', lambda h, kc=kc: h.activation(
                out=hT[:, kc, :T], in_=B.t[:, kc, :T], func=AF.Identity, bias=modv(l, m0, kc, b), scale=1.0),
                reads=[('t', kc)], writes=[hkey])
        gu_units = [wgu_s[f][l][:, j * 2048:(j + 1) * 2048].rearrange("p (k c) -> p k c", k=8) for j in range(NFC)]
        st_gu = Stream(S, 'wgu', B.wgu, gu_units)
        dn_units = [wdn_s[f][l][:, dc * NFC * 128:(dc + 1) * NFC * 128].rearrange("p (j c) -> p j c", j=NFC) for dc in range(8)]
        st_dn = Stream(S, 'wdn', B.wdn, dn_units)
        st_gu.start()
        for j in range(NFC):
            w, wkey = st_gu.get()
            pg, pu = ps[j % 2], ps[2 + j % 2]

            def mmg(h, w=w, pg=pg):
                ins = None
                for kc in range(8):
                    ins = h.matmul(pg[:, :T], lhsT=w[:, kc, 0:128], rhs=hT[:, kc, :T], start=(kc == 0), stop=(kc == 7))
                return ins

            def mmu(h, w=w, pu=pu):
                ins = None
                for kc in range(8):
                    ins = h.matmul(pu[:, :T], lhsT=w[:, kc, 128:256], rhs=hT[:, kc, :T], start=(kc == 0), stop=(kc == 7))
                return ins
            S.op('pe', mmg, reads=[wkey, hkey], writes=[('ps', j % 2)])
            S.op('pe', mmu, reads=[wkey, hkey], writes=[('ps', 2 + j % 2)])
            sg = B.sg[j % 2]
            S.op('act', lambda h, sg=sg, pg=pg: h.activation(out=sg[:, :T], in_=pg[:, :T], func=AF.Silu),
                 reads=[('ps', j % 2)], writes=[('sg', j % 2)])
            S.op('dve', lambda h, sg=sg, pu=pu, j=j: h.tensor_tensor(out=B.act[:, j, :T], in0=pu[:, :T], in1=sg[:, :T], op=ALU.mult),
                 reads=[('ps', 2 + j % 2), ('sg', j % 2)], writes=[('act', j)])
            if j == NFC - 4:
                st_dn.start()
        for dc in range(8):
            w, wkey = st_dn.get()
            pd = ps[4 + dc % 2]

            def mmd(h, w=w, pd=pd):
                ins = None
                for fc in range(NFC):
                    ins = h.matmul(pd[:, :T], lhsT=w[:, fc, :], rhs=B.act[:, fc, :T], start=(fc == 0), stop=(fc == NFC - 1))
                return ins
            S.op('pe', mmd, reads=[wkey] + [('act', j) for j in range(NFC)], writes=[('ps', 4 + dc % 2)])
            S.op('dve', lambda h, dc=dc, pd=pd: h.scalar_tensor_tensor(
                out=xv[:, dc, :], in0=pd[:, :T], scalar=modv(l, m0 + 2, dc, b), in1=xv[:, dc, :],
                op0=ALU.mult, op1=ALU.add), reads=[('ps', 4 + dc % 2), xkey], writes=[xkey])

    def load_tokens(src_rows, dstT, ntok, key):
        for tt in range(ntok // 128):
            st, sem = iob.stg[iob.i % 2]
            skey = ('iostg', iob.i % 2)
            iob.i += 1
            S.op('sp', lambda h, st=st, tt=tt: h.dma_start(out=st, in_=src_rows[tt * 128:(tt + 1) * 128, :]),
                 writes=[skey], dma=sem)
            for hf in range(2):
                bank = ps[hf]

                def tr(h, st=st, bank=bank, hf=hf):
                    ins = None
                    for q in range(4):
                        kc = hf * 4 + q
                        ins = h.transpose(out=bank[:, q * 128:(q + 1) * 128], in_=st[:, kc * 128:(kc + 1) * 128], identity=ident[:])
                    return ins
                S.op('pe', tr, reads=[skey, 'ident'], writes=[('ps', hf)])
                eng = 'act' if hf == 0 else 'dve'
                dst = dstT[:, hf * 4:(hf + 1) * 4, tt * 128:(tt + 1) * 128]
                srcv = bank[:, :].rearrange("p (q t) -> p q t", q=4)
                if eng == 'act':
                    S.op('act', lambda h, dst=dst, srcv=srcv: h.copy(out=dst, in_=srcv), reads=[('ps', hf)], writes=[key])
                else:
                    S.op('dve', lambda h, dst=dst, srcv=srcv: h.tensor_copy(out=dst, in_=srcv), reads=[('ps', hf)], writes=[key])

    def store_out(b):
        B = ffnb
        for tq in range(SEQ // 512):
            xv = xT[:, :, tq * 512:(tq + 1) * 512]
            if final_norm:
                rms_stats(xv, 512, 1.0 / D, 'xT', B)
                for kc in range(8):
                    S.op('dve', lambda h, kc=kc, xv=xv: h.scalar_tensor_tensor(
                        out=B.t[:, kc, :], in0=xv[:, kc, :], scalar=fgain[:, kc:kc + 1], in1=B.rs[:, :],
                        op0=ALU.mult, op1=ALU.mult), reads=['xT', 'rs', 'fgain'], writes=[('t', kc)])
                srcT = B.t
                skeys = [('t', kc) for kc in range(8)]
            else:
                srcT = xv
                skeys = ['xT']
            for tt in range(4):
                st, sem = iob.stg[iob.i % 2]
                skey = ('iostg', iob.i % 2)
                iob.i += 1
                for hf in range(2):
                    bank = ps[hf]

                    def tr(h, bank=bank, hf=hf, tt=tt, srcT=srcT):
                        ins = None
                        for q in range(4):
                            kc = hf * 4 + q
                            ins = h.transpose(out=bank[:, q * 128:(q + 1) * 128], in_=srcT[:, kc, tt * 128:(tt + 1) * 128], identity=ident[:])
                        return ins
                    S.op('pe', tr, reads=skeys + ['ident'], writes=[('ps', hf)])
                    dst = st[:, hf * 512:(hf + 1) * 512]
                    if hf == 0:
                        S.op('act', lambda h, dst=dst, bank=bank: h.copy(out=dst, in_=bank[:, :]), reads=[('ps', hf)], writes=[(skey, hf)])
                    else:
                        S.op('dve', lambda h, dst=dst, bank=bank: h.tensor_copy(out=dst, in_=bank[:, :]), reads=[('ps', hf)], writes=[(skey, hf)])
                r0 = tq * 512 + tt * 128
                S.op('sp', lambda h, st=st, r0=r0: h.dma_start(out=out_d[b, r0:r0 + 128, :], in_=st),
                     reads=[(skey, 0), (skey, 1)], writes=[skey], dma=sem)


    def prologue_mix():
        A = Arena()
        st32 = [A.alloc([2048], F32) for _ in range(2)]
        stsem = [S.dma_sem(f"pm{i}") for i in range(2)]
        big = A.alloc([10 * 8 * 128], BF16)
        bigsem = S.dma_sem("pmbig")
        bfr = [A.alloc([2048], BF16) for _ in range(2)]
        bfsem = [S.dma_sem(f"pmbf{i}") for i in range(2)]
        bd32 = A.alloc([2, 128], F32)
        pw32 = A.alloc([2, 128], F32)
        abt = A.alloc([2, 512], BF16)
        sm = S.dma_sem("pmmisc")
        state = {'c': 0}

        def stage_load(dmas):
            i = state['c'] % 2
            state['c'] += 1
            st = st32[i]
            key = ('pmst', i)
            if len(dmas) == 1:
                dv, src = dmas[0]
                S.op('sp', lambda h, d=dv(st), src=src: h.dma_start(out=d, in_=src), writes=[key], dma=stsem[i])
                return st, key
            S.op('dve', lambda h: h.nop(), reads=[], writes=[key])
            keys = []
            for j, (dv, src) in enumerate(dmas):
                S.op('sp', lambda h, d=dv(st), src=src: h.dma_start(out=d, in_=src), reads=[key], writes=[(key, j)], dma=stsem[i])
                keys.append((key, j))
            return st, keys

        def cp(eng, dst, src, reads, writes):
            if eng == 'act':
                S.op('act', lambda h: h.copy(out=dst, in_=src), reads=reads, writes=writes)
            else:
                S.op('dve', lambda h: h.tensor_copy(out=dst, in_=src), reads=reads, writes=writes)

        st, key = stage_load([(lambda st: st[:, 0:128], kperm_d[:, :])])
        cp('dve', perm_bf[:], st[:, 0:128], [key], ['perm_bf'])
        S.op('sp', lambda h: h.dma_start(out=swap_f[:], in_=kswap_d[:, :]), writes=['swap_f'], dma=sm)
        S.op('sp', lambda h: h.dma_start(out=invw[:], in_=kinvw_d[:, :]), writes=['invw'], dma=sm)
        S.op('sp', lambda h: h.dma_start(out=rcnt[:].rearrange("p a b -> p (a b)"), in_=krcnt_d[:, :]), writes=['rcnt'], dma=sm)
        S.op('sp', lambda h: h.dma_start(out=bd32, in_=kbd_d.rearrange("a p c -> p a c")), writes=['bd32'], dma=sm)
        S.op('dve', lambda h: h.memset(esel_bf[:].rearrange('p a b -> p (a b)'), 0.0), writes=['esel'])
        S.op('dve', lambda h: h.memset(esel_bf[:, 0, 0:64], 1.0), reads=['esel'], writes=['esel'])
        S.op('dve', lambda h: h.memset(esel_bf[:, 1, 64:128], 1.0), reads=['esel'], writes=['esel'])
        S.op('dve', lambda h: h.memset(bdones_bf[:], 0.0), writes=['bdones'])
        S.op('dve', lambda h: h.memset(bdones_bf[0:64, 0:64], 1.0), reads=['bdones'], writes=['bdones'])
        S.op('dve', lambda h: h.memset(bdones_bf[64:128, 64:128], 1.0), reads=['bdones'], writes=['bdones'])
        st, key = stage_load([(lambda st: st[:, 0:1024], kcdft_d[:, :])])
        cp('act', cdft_bf[:].rearrange("p a b c -> p (a b c)"), st[:, 0:1024], [key], ['cdft'])
        for hh in range(2):
            S.op('sp', lambda h, hh=hh: h.dma_start(out=qg[hh * 64:(hh + 1) * 64, :], in_=qg_d.rearrange("l e -> e l"),
                                                   allow_slow_non_contiguous=True), writes=[('qg', hh)], dma=sm)
            S.op('sp', lambda h, hh=hh: h.dma_start(out=kg[hh * 64:(hh + 1) * 64, :], in_=kg_d.rearrange("l e -> e l"),
                                                   allow_slow_non_contiguous=True), writes=[('kg', hh)], dma=sm)
        S.op('sp', lambda h: h.dma_start(out=pscale[:], in_=pools_d.rearrange("l (m p) -> p l m", p=128),
                                         allow_slow_non_contiguous=True), writes=['pscale'], dma=sm)
        S.barrier()
        for u in range(32 if pmdbg >= 2 else 0):
            st, key = stage_load([(lambda st: st, kdft_d[u])])
            bf = bfr[u % 2]
            bkey = ('pmbf', u % 2)
            cp('act' if u % 2 == 0 else 'dve', bf, st, [key], [bkey])
            S.op('sp', lambda h, bf=bf, u=u: h.dma_start(out=dft_s[u], in_=bf), reads=[bkey], writes=[bkey, 'dft_s'], dma=bfsem[u % 2])
        for l in range(depth if pmdbg >= 3 else 0):
            S.op('dve', lambda h: h.memset(pw32, 0.0), writes=['pw32'])
            for g in range(4):
                S.op('sp', lambda h, l=l, g=g: h.dma_start(
                    out=pw32[(g % 2) * 64:(g % 2 + 1) * 64, g // 2, (g % 2) * 64:(g % 2 + 1) * 64], in_=poolw_d[l, g]),
                    reads=['pw32'], writes=[('pw32d', g)], dma=sm)
            cp('dve', poolw_bf[:, l], pw32, ['pw32'] + [('pw32d', g) for g in range(4)], [('poolw', l), 'pw32'] + [('pw32d', g) for g in range(4)])
            for k2 in range(2 if pmdbg >= 4 else 0):
                st, key = stage_load([(lambda st: st[:, 0:256], fftw_d[l, k2 * 128:(k2 + 1) * 128, :])])

                def mm(h, st=st):
                    h.matmul(ps[0][:, 0:256], lhsT=bd32[:, 0, :], rhs=st[:, 0:256], start=True, stop=True)
                    return h.matmul(ps[0][:, 256:512], lhsT=bd32[:, 1, :], rhs=st[:, 0:256], start=True, stop=True)
                S.op('pe', mm, reads=[key, 'bd32'], writes=[('ps', 0)])
                cp('act', abt[:, k2, :], ps[0], [('ps', 0)], [('abt', k2)])
            if pmdbg >= 4:
                S.op('sp', lambda h, l=l: h.dma_start(out=ab_s[l], in_=abt.rearrange("p a b -> p (a b)")),
                     reads=[('abt', 0), ('abt', 1)], writes=[('abt', 0), ('abt', 1), ('ab_s', l)], dma=sm)
            if pmdbg < 5:
                continue
            bigv = big.rearrange("p (o k c) -> p o k c", o=10, k=8)
            for kc in range(8):
                qsrc = win_d[l, kc * 128:(kc + 1) * 128, 0:512].rearrange("p (t c e) -> p c t e", t=2, c=4)
                dl = [((lambda st, c=c: st[:, c * 128:(c + 1) * 128].rearrange("p (t e) -> p t e", t=2)), qsrc[:, c]) for c in range(4)]
                dl.append((lambda st: st[:, 512:INW], win_d[l, kc * 128:(kc + 1) * 128, 512:INW]))
                st, keys = stage_load(dl)
                cp('act' if kc % 2 == 0 else 'dve', bigv[:, :, kc, :], st[:, 0:INW].rearrange("p (o c) -> p o c", o=10),
                   keys, [('pmbig', kc % 2), ('pmst', (state['c'] - 1) % 2)])
            S.op('sp', lambda h, l=l: h.dma_start(out=win_s[l], in_=big), reads=[('pmbig', 0), ('pmbig', 1)],
                 writes=[('pmbig', 0), ('pmbig', 1), ('win_s', l)], dma=bigsem)
            if pmdbg < 6:
                continue
            bigo = big[:, 0:8 * 8 * 128].rearrange("p (d m c) -> p d m c", d=8, m=8)
            for mc in range(8):
                if mc < 4:
                    st, key = stage_load([(lambda st: st[0:64, 0:1024], wout_d[l, mc * 64:(mc + 1) * 64, :]),
                                          (lambda st: st[64:128, 0:1024], wout_d[l, (mc + 4) * 64:(mc + 5) * 64, :])])
                else:
                    st, key = stage_load([(lambda st: st[:, 0:1024], wout_d[l, mc * 128:(mc + 1) * 128, :])])
                klist = key if isinstance(key, list) else [key]
                cp('act' if mc % 2 == 0 else 'dve', bigo[:, :, mc, :], st[:, 0:1024].rearrange("p (d c) -> p d c", d=8),
                   klist, [('pmbig', mc % 2), ('pmst', (state['c'] - 1) % 2)])
            S.op('sp', lambda h, l=l: h.dma_start(out=wout_s[l], in_=big[:, 0:8 * 8 * 128]), reads=[('pmbig', 0), ('pmbig', 1)],
                 writes=[('pmbig', 0), ('pmbig', 1), ('wout_s', l)], dma=bigsem)
        S.barrier()

    class MixBufs:
        def __init__(self):
            A = Arena()
            self.qT = A.alloc([4, SEQ], BF16)
            self.qcT = A.alloc([4, CTX], BF16)
            self.kT = A.alloc([SEQ + CTX], BF16)
            self.Vx = A.alloc([18, 128], BF16)
            self.upT = A.alloc([2, SEQ + 16], F32)
            self.upc = A.alloc([2, CTX + 16], F32)
            self.uab = A.alloc([18, 512], BF16)
            base = A.off
            self.hT = A.alloc([8, 512], BF16)
            self.sq = A.alloc([8, 512], BF16)
            self.t = [A.alloc([512], F32) for _ in range(2)]
            self.rs = A.alloc([512], F32)
            self.sq2 = A.alloc([512], BF16)
            self.rq = A.alloc([512], F32)
            self.qn = A.alloc([512], F32)
            self.qnb = A.alloc([512], BF16)
            self.t1 = A.alloc([512], F32)
            self.ufT = A.alloc([2, 512], BF16)
            self.rope = [(A.alloc([2, 512], F32), S.dma_sem(f"rope{i}")) for i in range(2)]
            self.win = [(A.alloc([8, 128], BF16), S.dma_sem(f"win{i}")) for i in range(3)]
            self.AB = A.alloc([2, 512], BF16)
            self.ABsem = S.dma_sem("ab")
            self.end1 = A.off
            A.off = base
            self.P = [A.alloc([2, 512], BF16) for _ in range(3)]
            self.rr = A.alloc([512], F32)
            self.rc = A.alloc([512], F32)
            self.mix = [A.alloc([8, 512], BF16) for _ in range(2)]
            self.a2 = A.alloc([544], F32)
            self.a4 = A.alloc([544], F32)
            self.a8 = A.alloc([544], F32)
            self.a16 = A.alloc([544], F32)
            self.pm = A.alloc([2, 512], BF16)
            self.rcf = A.alloc([2, 512], F32)
            self.rcfsem = S.dma_sem('rcf')
            self.dft = [(A.alloc([4, 512], BF16), S.dma_sem(f"dft{i}")) for i in range(2)]
            self.wo = [(A.alloc([8, 128], BF16), S.dma_sem(f"wo{i}")) for i in range(2)]
            self.end2 = A.off
            self.mix_i = 0
            self.p_i = 0

    def qk_norm_rope(M, pq, T, gain_ap, dst, cs, cskey, tag):
        S.op('act', lambda h: h.activation(out=M.sq2[:, :T], in_=pq[:, :T], func=AF.Square), reads=[tag], writes=['sq2'])
        S.op('pe', lambda h: h.matmul(ps[6][:, :T], lhsT=bdones_bf[:], rhs=M.sq2[:, :T], start=True, stop=True),
             reads=['sq2', 'bdones'], writes=['ps6'])
        S.op('act', lambda h: h.activation(out=M.rq[:, :T], in_=ps[6][:, :T], func=AF.Sqrt, scale=1.0 / 64, bias=EPS),
             reads=['ps6'], writes=['rq'])
        S.op('dve', lambda h: h.reciprocal(out=M.rq[:, :T], in_=M.rq[:, :T]), reads=['rq'], writes=['rq'])
        S.op('dve', lambda h: h.scalar_tensor_tensor(out=M.qn[:, :T], in0=pq[:, :T], scalar=gain_ap, in1=M.rq[:, :T],
                                                     op0=ALU.mult, op1=ALU.mult), reads=[tag, 'rq'], writes=['qn'])
        if cs is None:
            S.op('act', lambda h: h.copy(out=dst, in_=M.qn[:, :T]), reads=['qn'], writes=['qkdst'])
            return
        S.op('act', lambda h: h.copy(out=M.qnb[:, :T], in_=M.qn[:, :T]), reads=['qn'], writes=['qnb'])
        S.op('pe', lambda h: h.matmul(ps[7][:, :T], lhsT=perm_bf[:], rhs=M.qnb[:, :T], start=True, stop=True),
             reads=['qnb', 'perm_bf'], writes=['ps7'])
        S.op('dve', lambda h: h.tensor_tensor(out=M.t1[:, :T], in0=M.qn[:, :T], in1=cs[:, 0, :T], op=ALU.mult),
             reads=['qn', cskey], writes=['t1'])
        S.op('dve', lambda h: h.tensor_tensor(out=M.qn[:, :T], in0=ps[7][:, :T], in1=cs[:, 1, :T], op=ALU.mult),
             reads=['ps7', cskey, 'qn'], writes=['qn2'])
        S.op('dve', lambda h: h.tensor_tensor(out=dst, in0=M.qn[:, :T], in1=M.t1[:, :T], op=ALU.add),
             reads=['qn2', 't1'], writes=['qkdst', 'qn'])

    def mixer_m1(M, l, b, last):
        S.op('sp', lambda h: h.dma_start(out=M.AB.rearrange("p a b -> p (a b)"), in_=ab_s[l]), reads=[('ab_s', l)], writes=['AB'], dma=M.ABsem)
        S.op('dve', lambda h: h.memset(M.upT.rearrange('p a b -> p (a b)'), 0.0), writes=['upT'])
        S.op('dve', lambda h: h.memset(M.upc.rearrange('p a b -> p (a b)'), 0.0), writes=['upc'])
        tiles = [(xT[:, :, tq * 512:(tq + 1) * 512], 512, b, 'xT', tq) for tq in range(4)] + [(cT[:, :, :], CTX, 4, 'cT', 4)]
        rope_i = 0
        t_i = 0
        for (xv, T, bb, xkey, tq) in tiles:
            is_ctx = (tq == 4)
            rms_stats(xv, T, 1.0 / D, xkey, M)
            for kc in range(8):
                tt = M.t[t_i % 2]
                tk = ('mt', t_i % 2)
                t_i += 1
                S.op('dve', lambda h, kc=kc, tt=tt, xv=xv, T=T, bb=bb: h.scalar_tensor_tensor(
                    out=tt[:, :T], in0=xv[:, kc, :], scalar=modv(l, 4, kc, bb), in1=M.rs[:, :T],
                    op0=ALU.mult, op1=ALU.mult), reads=[xkey, 'rs'], writes=[tk])
                S.op('act', lambda h, kc=kc, tt=tt, T=T, bb=bb: h.activation(
                    out=M.hT[:, kc, :T], in_=tt[:, :T], func=AF.Identity, bias=modv(l, 3, kc, bb), scale=1.0),
                    reads=[tk], writes=['mhT'])
            if not is_ctx:
                cs, csem = M.rope[rope_i % 2]
                cskey = ('rope', rope_i % 2)
                rope_i += 1
                S.op('sp', lambda h, cs=cs, tq=tq: h.dma_start(out=cs, in_=krope_d[:, :, tq * 512:(tq + 1) * 512].rearrange("a p t -> p a t")),
                     writes=[cskey], dma=csem)
            else:
                cs, cskey = None, None
            if is_ctx and last:
                chunks = [4, 5]
            else:
                chunks = list(range(10))
            units = [win_s[l][:, oc * 1024:(oc + 1) * 1024].rearrange("p (k c) -> p k c", k=8) for oc in chunks]
            st = Stream(S, 'win', M.win, units)
            for oc in chunks:
                w, wkey = st.get()
                if oc == 5:
                    for tc in range(T // 128):
                        ch = (tq * 4 + tc) if not is_ctx else 16 + tc

                        def mmv(h, w=w, tc=tc):
                            ins = None
                            for kc in range(8):
                                ins = h.matmul(ps[1][:, 0:128], lhsT=M.hT[:, kc, tc * 128:(tc + 1) * 128], rhs=w[:, kc, :],
                                               start=(kc == 0), stop=(kc == 7))
                            return ins
                        S.op('pe', mmv, reads=[wkey, 'mhT'], writes=[('ps', 1)])
                        S.op('act', lambda h, ch=ch: h.activation(out=M.Vx[:, ch, :], in_=ps[1][:, 0:128], func=AF.Identity), reads=[('ps', 1)], writes=[('Vxa', ch)])
                    continue
                pq = ps[0] if oc % 2 == 0 else ps[2]
                ptag = ('ps', 0) if oc % 2 == 0 else ('ps', 2)

                def mmq(h, w=w, pq=pq, T=T):
                    ins = None
                    for kc in range(8):
                        ins = h.matmul(pq[:, :T], lhsT=w[:, kc, :], rhs=M.hT[:, kc, :T], start=(kc == 0), stop=(kc == 7))
                    return ins
                S.op('pe', mmq, reads=[wkey, 'mhT'], writes=[ptag])
                if oc < 4:
                    dst = M.qT[:, oc, tq * 512:(tq + 1) * 512] if not is_ctx else M.qcT[:, oc, :]
                    qk_norm_rope(M, pq, T, qg[:, l:l + 1], dst, cs, cskey, ptag)
                elif oc == 4:
                    dst = M.kT[:, tq * 512:(tq + 1) * 512] if not is_ctx else M.kT[:, SEQ:SEQ + CTX]
                    qk_norm_rope(M, pq, T, kg[:, l:l + 1], dst, cs, cskey, ptag)
                elif oc in (6, 7):
                    m = oc - 6
                    dst = M.upT[:, m, 8 + tq * 512: 8 + (tq + 1) * 512] if not is_ctx else M.upc[:, m, 8:8 + CTX]
                    S.op('act', lambda h, dst=dst, pq=pq, T=T: h.copy(out=dst, in_=pq[:, :T]), reads=[ptag, 'upT', 'upc'], writes=[('up', tq, m)])
                else:
                    m = oc - 8
                    S.op('act', lambda h, m=m, pq=pq, T=T: h.copy(out=M.ufT[:, m, :T], in_=pq[:, :T]), reads=[ptag], writes=[('ufT', m)])
                    if m == 1:
                        for tc in range(T // 128):
                            ch = (tq * 4 + tc) if not is_ctx else 16 + tc

                            def mmab(h, tc=tc):
                                h.matmul(ps[3][:, :], lhsT=M.ufT[:, 0, tc * 128:(tc + 1) * 128], rhs=M.AB[:, 0, :], start=True, stop=False)
                                return h.matmul(ps[3][:, :], lhsT=M.ufT[:, 1, tc * 128:(tc + 1) * 128], rhs=M.AB[:, 1, :], start=False, stop=True)
                            S.op('pe', mmab, reads=[('ufT', 0), ('ufT', 1), 'AB'], writes=[('ps', 3)])
                            S.op('dve', lambda h, ch=ch: h.tensor_copy(out=M.uab[:, ch, :], in_=ps[3][:, :]), reads=[('ps', 3)], writes=[('uab', ch)])
        S.barrier()

    def pool_tile(M, l, up, T, t0, L, mixv):
        W = T + 16
        edge = (t0 == 0) or (t0 + T == L)
        if edge:
            which = 2 if L == CTX else (0 if t0 == 0 else 1)
            S.op('sp', lambda h: h.dma_start(out=M.rcf.rearrange("p a b -> p (a b)"), in_=krcf_d[which]), writes=['rcf'], dma=M.rcfsem)
        for m in range(2):
            u = up[:, m, t0:t0 + W]
            S.op('pool', lambda h, u=u: h.tensor_tensor(out=M.a2[:, 0:W - 1], in0=u[:, 0:W - 1], in1=u[:, 1:W], op=ALU.add),
                 reads=[('up', 'all')], writes=['a2'])
            S.op('pool', lambda h: h.tensor_tensor(out=M.a4[:, 0:W - 3], in0=M.a2[:, 0:W - 3], in1=M.a2[:, 2:W - 1], op=ALU.add),
                 reads=['a2'], writes=['a4'])
            if m == 1:
                S.op('pool', lambda h: h.tensor_tensor(out=M.a8[:, 0:W - 7], in0=M.a4[:, 0:W - 7], in1=M.a4[:, 4:W - 3], op=ALU.add),
                     reads=['a4'], writes=['a8'])
                S.op('pool', lambda h: h.tensor_tensor(out=M.a16[:, 0:W - 15], in0=M.a8[:, 0:W - 15], in1=M.a8[:, 8:W - 7], op=ALU.add),
                     reads=['a8'], writes=['a16'])
                srcs = [(M.a8, 4, 'a8'), (M.a16, 8, 'a16')]
            else:
                srcs = [(M.a2, 1, 'a2'), (M.a4, 2, 'a4')]
            for hh in range(2):
                a, hw, akey = srcs[hh]
                lo, hi = hh * 64, (hh + 1) * 64
                if not edge:
                    S.op('dve', lambda h, a=a, hw=hw, lo=lo, hi=hi, m=m, u=u: h.scalar_tensor_tensor(
                        out=M.pm[lo:hi, m, :T], in0=a[lo:hi, 8 - hw:8 - hw + T], scalar=invw[lo:hi, m:m + 1], in1=u[lo:hi, 8:8 + T],
                        op0=ALU.mult, op1=ALU.subtract), reads=[akey, ('up', 'all'), 'invw'], writes=[('pm', m, hh)])
                else:
                    S.op('dve', lambda h, a=a, hw=hw, lo=lo, hi=hi, m=m: h.tensor_tensor(
                        out=M.rr[lo:hi, :T], in0=a[lo:hi, 8 - hw:8 - hw + T], in1=M.rcf[lo:hi, m, :T], op=ALU.mult),
                        reads=[akey, 'rcf'], writes=[('rr', hh)])
                    S.op('dve', lambda h, lo=lo, hi=hi, m=m, u=u: h.tensor_tensor(
                        out=M.pm[lo:hi, m, :T], in0=M.rr[lo:hi, :T], in1=u[lo:hi, 8:8 + T], op=ALU.subtract),
                        reads=[('rr', hh), ('up', 'all')], writes=[('pm', m, hh)])
            S.op('pe', lambda h, m=m: h.matmul(ps[7][:, :T], lhsT=poolw_bf[:, l, m, :], rhs=M.pm[:, m, :T], start=True, stop=True),
                 reads=[('pm', m, 0), ('pm', m, 1), ('poolw', l)], writes=['ps7'])
            S.op('act', lambda h, m=m: h.activation(out=mixv[:, 4 + m, :T], in_=ps[7][:, :T], func=AF.Identity, scale=pscale[:, l, m:m + 1]),
                 reads=['ps7', 'pscale'], writes=[('mixp', m)])

    def fft_tile(M, l, tq, T, is_ctx, mixv):
        if not is_ctx:
            units = [dft_s[tq * 8 + g].rearrange("p (a c) -> p a c", a=4) for g in range(8)]
            st = Stream(S, 'dft', M.dft, units)
            for g in range(8):
                w, wkey = st.get()
                cs_, lcg = g // 4, g % 4

                def mm(h, w=w, g=g, cs_=cs_, lcg=lcg):
                    ins = None
                    for lc4 in range(4):
                        lc = lcg * 4 + lc4
                        for m in range(2):
                            ins = h.matmul(ps[m][:, :T], lhsT=M.uab[:, lc, cs_ * 256 + m * 128: cs_ * 256 + (m + 1) * 128],
                                           rhs=w[:, lc4, :], start=(g == 0 and lc4 == 0), stop=(g == 7 and lc4 == 3))
                    return ins
                S.op('pe', mm, reads=[wkey] + [('uab', lc) for lc in range(16)], writes=[('ps', 0), ('ps', 1)])
        else:
            def mm(h):
                ins = None
                n = 0
                for cs_ in range(2):
                    for lc in range(2):
                        for m in range(2):
                            ins = h.matmul(ps[m][:, :T], lhsT=M.uab[:, 16 + lc, cs_ * 256 + m * 128: cs_ * 256 + (m + 1) * 128],
                                           rhs=cdft_bf[:, cs_, lc, :], start=(n == 0), stop=(n == 3))
                        n += 1
                return ins
            S.op('pe', mm, reads=['cdft', ('uab', 16), ('uab', 17)], writes=[('ps', 0), ('ps', 1)])
        S.op('act', lambda h: h.copy(out=mixv[:, 6, :T], in_=ps[0][:, :T]), reads=[('ps', 0)], writes=[('mixf', 0)])
        S.op('dve', lambda h: h.tensor_copy(out=mixv[:, 7, :T], in_=ps[1][:, :T]), reads=[('ps', 1)], writes=[('mixf', 1)])

    def attn_tile(M, qsrc, T, kchunks, mixv):
        for c in range(4):
            nk = len(kchunks)
            for i, ch in enumerate(kchunks):
                sbank = pp[(M.p_i) % 2]
                skey = ('pp', M.p_i % 2)
                skeys = [('ps', 2 * (M.p_i % 2)), ('ps', 2 * (M.p_i % 2) + 1)]
                P = M.P[M.p_i % 3]
                pkey = ('P', M.p_i % 3)
                M.p_i += 1

                def mms(h, sbank=sbank, ch=ch, c=c):
                    h.matmul(sbank[:, 0:T], lhsT=M.kT[0:64, ch * 128:(ch + 1) * 128], rhs=qsrc[0:64, c, :], start=True, stop=True)
                    return h.matmul(sbank[:, 512:512 + T], lhsT=M.kT[64:128, ch * 128:(ch + 1) * 128], rhs=qsrc[64:128, c, :], start=True, stop=True)
                S.op('pe', mms, reads=['qk'], writes=skeys)
                if T == 512:
                    S.op('act', lambda h, sbank=sbank, P=P: h.activation(out=P.rearrange("p a b -> p (a b)"), in_=sbank[:, :], func=AF.Exp, scale=0.125),
                         reads=skeys, writes=[pkey])
                else:
                    S.op('act', lambda h, sbank=sbank, P=P: h.activation(out=P[:, :, :T], in_=sbank.rearrange("p (a b) -> p a b", a=2)[:, :, :T], func=AF.Exp, scale=0.125),
                         reads=skeys, writes=[pkey])

                def mmpv(h, P=P, ch=ch, i=i):
                    h.matmul(ps[4][:, :T], lhsT=M.Vx[:, ch, :], rhs=P[:, 0, :T], start=(i == 0), stop=(i == nk - 1))
                    h.matmul(ps[5][:, :T], lhsT=M.Vx[:, ch, :], rhs=P[:, 1, :T], start=(i == 0), stop=(i == nk - 1))
                    h.matmul(ps[6][:, :T], lhsT=esel_bf[:, 0, :], rhs=P[:, 0, :T], start=(i == 0), stop=False)
                    return h.matmul(ps[6][:, :T], lhsT=esel_bf[:, 1, :], rhs=P[:, 1, :T], start=False, stop=(i == nk - 1))
                S.op('pe', mmpv, reads=[pkey, 'Vxall', 'esel'], writes=[('ps', 4), ('ps', 5), 'ps6'])
            S.op('dve', lambda h: h.reciprocal(out=M.rc[:, :T], in_=ps[6][:, :T]), reads=['ps6'], writes=['rc'])
            S.op('dve', lambda h, c=c: h.tensor_tensor(out=mixv[0:64, c, :T], in0=ps[4][0:64, :T], in1=M.rc[0:64, :T], op=ALU.mult),
                 reads=[('ps', 4), 'rc'], writes=[('mixa', c, 0)])
            S.op('dve', lambda h, c=c: h.tensor_tensor(out=mixv[64:128, c, :T], in0=ps[5][64:128, :T], in1=M.rc[64:128, :T], op=ALU.mult),
                 reads=[('ps', 5), 'rc'], writes=[('mixa', c, 1)])

    def wout_tile(M, l, b, xv, T, xkey, mixv, mixkeys):
        units = [wout_s[l][:, dc * 1024:(dc + 1) * 1024].rearrange("p (m c) -> p m c", m=8) for dc in range(8)]
        st = Stream(S, 'wo', M.wo, units)
        for dc in range(8):
            w, wkey = st.get()
            pd = ps[dc % 2]

            def mm(h, w=w, pd=pd):
                ins = None
                for mc in range(8):
                    ins = h.matmul(pd[:, :T], lhsT=w[:, mc, :], rhs=mixv[:, mc, :T], start=(mc == 0), stop=(mc == 7))
                return ins
            S.op('pe', mm, reads=[wkey] + mixkeys, writes=[('ps', dc % 2)])
            S.op('dve', lambda h, dc=dc, pd=pd: h.scalar_tensor_tensor(
                out=xv[:, dc, :], in0=pd[:, :T], scalar=modv(l, 5, dc, b), in1=xv[:, dc, :],
                op0=ALU.mult, op1=ALU.add), reads=[('ps', dc % 2), xkey], writes=[xkey])

    mixb = []

    def mixer(l, b, last):
        if not mixb:
            mixb.append(MixBufs())
        M = mixb[0]
        if mixdbg < 2:
            return
        mixer_m1(M, l, b, last)
        if mixdbg < 3:
            return
        mixkeys = [('mixa', c, hh) for c in range(4) for hh in range(2)] + [('mixp', 0), ('mixp', 1), ('mixf', 0), ('mixf', 1)]
        tiles = [(tq, 512, False) for tq in range(4)]
        if not last:
            tiles.append((0, CTX, True))
        for (tq, T, is_ctx) in tiles:
            mixv = M.mix[M.mix_i % 2]
            M.mix_i += 1
            if not is_ctx:
                if mixdbg >= 3:
                    pool_tile(M, l, M.upT, 512, tq * 512, SEQ, mixv)
                if mixdbg >= 4:
                    fft_tile(M, l, tq, 512, False, mixv)
                if mixdbg >= 5:
                    attn_tile(M, M.qT[:, :, tq * 512:(tq + 1) * 512], 512, list(range(18)), mixv)
                if mixdbg >= 6:
                    wout_tile(M, l, b, xT[:, :, tq * 512:(tq + 1) * 512], 512, 'xT', mixv, mixkeys)
            else:
                if mixdbg >= 3:
                    pool_tile(M, l, M.upc, CTX, 0, CTX, mixv)
                if mixdbg >= 4:
                    fft_tile(M, l, 0, CTX, True, mixv)
                if mixdbg >= 5:
                    attn_tile(M, M.qcT[:, :, :], CTX, [16, 17], mixv)
                if mixdbg >= 6:
                    wout_tile(M, l, 4, cT[:, :, :], CTX, 'cT', mixv, mixkeys)
        S.barrier()

    if stages != 'io':
        prologue_ffn()
        prologue_mod()
        if stages != 'ffn1':
            prologue_mix()
    ffnb = FFNBufs()
    iob = IOBufs(ffnb.end)
    for b in range(nb):
        load_tokens(x_d[b], xT, SEQ, 'xT')
        load_tokens(ctx_d[b], cT, CTX, 'cT')
        if stages != 'io':
            for l in range(depth):
                last = (l == DEPTH - 1)
                for tq in range(4):
                    ffn_tile(l, 0, xT[:, :, tq * 512:(tq + 1) * 512], 512, b, 'xT')
                ffn_tile(l, 0, cT[:, :, :], CTX, 4, 'cT')
                if stages == 'ffn1':
                    continue
                S.barrier()
                mixer(l, b, last)
                if stages == 'mix':
                    continue
                for tq in range(4):
                    ffn_tile(l, 1, xT[:, :, tq * 512:(tq + 1) * 512], 512, b, 'xT')
                if not last:
                    ffn_tile(l, 1, cT[:, :, :], CTX, 4, 'cT')
        store_out(b)
        S.barrier()
        S.new_epoch()
    S.barrier()
    S.op('sp', lambda h: h.nop(), reads=[], writes=[])

    with nc.Block() as block:
        S.emit(block)
    return nc


def make_consts():
    c = {"k_ident": np.eye(128, dtype=np.float32)}
    l = np.arange(2048, dtype=np.int64)
    mm_ = (l[:, None] * l[None, :]) % 2048
    ang = 2.0 * np.pi * mm_.astype(np.float64) / 2048.0
    nrm = 1.0 / np.sqrt(2048.0)
    mats = [np.cos(ang) * nrm, np.sin(ang) * nrm]
    dft = np.zeros((4, 2, 4, 128, 4, 512), np.float32)
    for cs in range(2):
        m = mats[cs].reshape(4, 4, 128, 4, 512)
        dft[:, cs] = m.transpose(3, 0, 2, 1, 4)
    c["k_dft"] = dft.reshape(32, 128, 2048)
    lc_ = np.arange(256, dtype=np.int64)
    angc = 2.0 * np.pi * ((lc_[:, None] * lc_[None, :]) % 256).astype(np.float64) / 256.0
    cm = [np.cos(angc) / 16.0, np.sin(angc) / 16.0]
    cd = np.zeros((128, 2, 2, 256), np.float32)
    for cs in range(2):
        cd[:, cs] = cm[cs].reshape(2, 128, 256).transpose(1, 0, 2)
    c["k_cdft"] = cd.reshape(128, 1024)
    cc = np.arange(64, dtype=np.int64)
    a64 = 2.0 * np.pi * ((cc[:, None] * cc[None, :]) % 64).astype(np.float64) / 64.0
    bd = np.zeros((2, 128, 128), np.float32)
    for o in (0, 64):
        bd[0, o:o + 64, o:o + 64] = np.cos(a64) / 8.0
        bd[1, o:o + 64, o:o + 64] = -np.sin(a64) / 8.0
    c["k_bd"] = bd
    t = np.arange(SEQ)
    row = (t // 64).astype(np.float32)
    col = (t % 64).astype(np.float32)
    inv = (np.float32(10000.0) ** (-(np.arange(16, dtype=np.float32)) / np.float32(16))).astype(np.float32)
    angr = np.concatenate([row[:, None] * inv[None, :], col[:, None] * inv[None, :]], axis=-1).astype(np.float32)
    cosr = np.cos(angr).astype(np.float32).T
    sinr = np.sin(angr).astype(np.float32).T
    rope = np.zeros((2, 128, SEQ), np.float32)
    for p in range(128):
        rope[0, p] = cosr[(p % 64) % 32]
        rope[1, p] = sinr[(p % 64) % 32]
    c["k_rope"] = rope
    perm = np.zeros((128, 128), np.float32)
    for o in (0, 64):
        for m in range(64):
            if m < 32:
                perm[o + m + 32, o + m] = -1.0
            else:
                perm[o + m - 32, o + m] = 1.0
    c["k_perm"] = perm
    sw = np.zeros((128, 128), np.float32)
    for m in range(128):
        sw[(m + 64) % 128, m] = 1.0
    c["k_swap"] = sw
    ws = [2, 4, 8, 16]
    invw = np.zeros((128, 2), np.float32)
    rc = np.zeros((128, 2, 16), np.float32)
    for p in range(128):
        for m in range(2):
            w = ws[2 * m + p // 64]
            invw[p, m] = 1.0 / w
            for i in range(8):
                tt = i
                rc[p, m, i] = 1.0 / ((tt + w // 2) - max(tt - w // 2, 0))
                dd = 8 - i
                rc[p, m, 8 + i] = 1.0 / (min(w // 2, dd) + w // 2)
    c["k_invw"] = invw
    rcf = np.zeros((3, 128, 2, 512), np.float32)
    for p in range(128):
        for m in range(2):
            w = ws[2 * m + p // 64]
            for which, (Lx, t0) in enumerate(((SEQ, 0), (SEQ, SEQ - 512), (CTX, 0))):
                for i in range(512):
                    tt = t0 + i
                    if tt >= Lx:
                        rcf[which, p, m, i] = 1.0 / w
                    else:
                        rcf[which, p, m, i] = 1.0 / (min(tt + w // 2, Lx) - max(tt - w // 2, 0))
    c["k_rcf"] = rcf.reshape(3, 128, 1024)
    c["k_rcnt"] = rc.reshape(128, 32)
    return c


def kernel(**inputs):
    cfg = inputs.pop('_cfg', {})
    nb = cfg.get('nb', NB_CORE)
    ncores = cfg.get('ncores', NCORES)
    nc = build_nc(cfg)
    consts = make_consts()
    in_maps = []
    for i in range(ncores):
        m = {}
        for k, v in inputs.items():
            v = np.asarray(v)
            if k in ('x', 'c', 'ctx'):
                v = np.ascontiguousarray(v[i * nb:(i + 1) * nb])
            m[k] = v
        m.update(consts)
        in_maps.append(m)
    res = run_bass_kernel_spmd(nc, in_maps, core_ids=list(range(ncores)))
    return np.concatenate([np.asarray(r["out"]) for r in res.results], axis=0)
```

```python
import numpy as np
import concourse.bass as bass
import concourse.mybir as mybir
from concourse.bass_utils import run_bass_kernel_spmd

F32 = mybir.dt.float32
BF16 = mybir.dt.bfloat16
AF = mybir.ActivationFunctionType
ALU = mybir.AluOpType

D = 1024
SEQ = 2048
CTX = 256
DEPTH = 4
DFF = 2816
NFC = 22
INW = 1280
EPS = 1e-6
NB_CORE = 4
NCORES = 8
ENGS = ('pe', 'act', 'dve', 'pool', 'sp')


class _Op:
    __slots__ = ('id', 'eng', 'fn', 'deps', 'signal', 'sem', 'val', 'dma', 'epoch')


class _Res:
    __slots__ = ('w', 'r', 'rd')

    def __init__(self):
        self.w = None
        self.r = {}
        self.rd = []


class DmaSem:
    def __init__(self, sem):
        self.sem = sem
        self.count = 0


class Sched:
    def __init__(self, nc):
        self.nc = nc
        self.ops = {e: [] for e in ENGS}
        self.all = []
        self.res = {}
        self.epoch = 0
        self.pending_barrier = {e: set() for e in ENGS}
        self.dma_since_barrier = []
        self.engsem = {}
        self.dmasems = []

    def dma_sem(self, name):
        s = DmaSem(self.nc.alloc_semaphore(name))
        self.dmasems.append(s)
        return s

    def new_epoch(self):
        self.epoch += 1

    def op(self, eng, fn, reads=(), writes=(), dma=None):
        o = _Op()
        o.id = len(self.all)
        o.eng = eng
        o.fn = fn
        o.signal = False
        o.dma = dma
        o.epoch = self.epoch
        o.sem = None
        o.val = None
        deps = set(self.pending_barrier[eng])
        self.pending_barrier[eng] = set()
        for r in reads:
            st = self.res.get(r)
            if st is not None and st.w is not None:
                deps.add(st.w)
        for w in writes:
            st = self.res.get(w)
            if st is not None:
                if st.w is not None:
                    deps.add(st.w)
                deps.update(st.r.values())
                deps.update(st.rd)
        for r in reads:
            st = self.res.setdefault(r, _Res())
            if dma is not None:
                st.rd.append(o.id)
            else:
                st.r[eng] = o.id
        for w in writes:
            st = self.res.setdefault(w, _Res())
            st.w = o.id
            st.r = {}
            st.rd = []
        best = {}
        keep = set()
        for d in deps:
            p = self.all[d]
            if p.dma is not None:
                keep.add(d)
            else:
                k = (p.eng, p.epoch)
                if k not in best or best[k] < d:
                    best[k] = d
        keep.update(best.values())
        o.deps = keep
        if dma is not None:
            dma.count += 16
            o.sem = dma.sem
            o.val = dma.count
            self.dma_since_barrier.append(o.id)
        self.all.append(o)
        self.ops[eng].append(o)
        return o

    def barrier(self):
        last = set()
        for e in ENGS:
            for o in reversed(self.ops[e]):
                if o.dma is None:
                    last.add(o.id)
                    break
        last.update(self.dma_since_barrier)
        self.dma_since_barrier = []
        for e in ENGS:
            self.pending_barrier[e].update(last)

    def emit(self, block):
        nc = self.nc
        for o in self.all:
            for d in o.deps:
                self.all[d].signal = True
        nep = self.epoch + 1
        for e in ENGS:
            if e == 'sp':
                continue
            self.engsem[e] = [nc.alloc_semaphore(f"s_{e}_{i}") for i in range(nep)]
        for e in ENGS:
            cnt = {}
            for o in self.ops[e]:
                if o.dma is None and o.signal:
                    if e == 'sp':
                        raise RuntimeError("non-dma op on sp cannot signal")
                    cnt[o.epoch] = cnt.get(o.epoch, 0) + 1
                    o.val = cnt[o.epoch]
                    o.sem = self.engsem[e][o.epoch]
        allops = self.all

        def run(name, h):
            waited = {}
            for o in self.ops[name]:
                for d in sorted(o.deps):
                    p = allops[d]
                    key = id(p.sem)
                    if waited.get(key, 0) >= p.val:
                        continue
                    h.wait_ge(p.sem, p.val)
                    waited[key] = p.val
                ins = o.fn(h)
                if o.dma is not None:
                    ins.then_inc(o.sem, 16)
                elif o.signal:
                    ins.then_inc(o.sem, 1)

        @block.tensor
        def _(h):
            run('pe', h)

        @block.scalar
        def _(h):
            run('act', h)

        @block.vector
        def _(h):
            run('dve', h)

        @block.gpsimd
        def _(h):
            run('pool', h)

        @block.sync
        def _(h):
            run('sp', h)


class Stream:
    def __init__(self, S, name, slots, units):
        self.S = S
        self.name = name
        self.slots = slots
        self.units = units
        self.issued = 0
        self.got = 0

    def _issue(self):
        i = self.issued
        ap, sem = self.slots[i % len(self.slots)]
        src = self.units[i]
        self.S.op('sp', lambda h, ap=ap, src=src: h.dma_start(out=ap, in_=src),
                  writes=[(self.name, i % len(self.slots))], dma=sem)
        self.issued += 1

    def start(self):
        while self.issued < min(len(self.units), len(self.slots)):
            self._issue()

    def get(self):
        i = self.got
        while self.issued < min(len(self.units), i + len(self.slots)):
            self._issue()
        self.got += 1
        return self.slots[i % len(self.slots)][0], (self.name, i % len(self.slots))


def build_nc(cfg):
    nb = cfg.get('nb', NB_CORE)
    depth = cfg.get('depth', DEPTH)
    stages = cfg.get('stages', 'full')
    final_norm = cfg.get('final_norm', True)
    mixdbg = cfg.get('mixdbg', 9)
    pmdbg = cfg.get('pmdbg', 9)

    nc = bass.Bass("TRN2", target_bir_lowering=False)
    S = Sched(nc)

    def dram_in(name, shape, dt=F32):
        return nc.dram_tensor(name, list(shape), dt, kind="ExternalInput").ap()

    x_d = dram_in("x", [nb, SEQ, D])
    c_d = dram_in("c", [nb, D])
    ctx_d = dram_in("ctx", [nb, CTX, D])
    cctx_d = dram_in("c_ctx", [D])
    adaw_d = dram_in("ada_w", [DEPTH, D, 9 * D])
    adab_d = dram_in("ada_b", [DEPTH, 9 * D])
    wgu_d = [dram_in("ffn1_w_gu", [DEPTH, D, 2 * DFF]), dram_in("ffn2_w_gu", [DEPTH, D, 2 * DFF])]
    wdn_d = [dram_in("ffn1_w_down", [DEPTH, DFF, D]), dram_in("ffn2_w_down", [DEPTH, DFF, D])]
    win_d = dram_in("w_in", [DEPTH, D, INW])
    qg_d = dram_in("q_gain", [DEPTH, 64])
    kg_d = dram_in("k_gain", [DEPTH, 64])
    poolw_d = dram_in("pool_w", [DEPTH, 4, 64, 64])
    pools_d = dram_in("pool_scale", [DEPTH, 256])
    fftw_d = dram_in("fft_w", [DEPTH, 256, 256])
    wout_d = dram_in("w_out", [DEPTH, D, D])
    fg_d = dram_in("final_gain", [D])
    ident_d = dram_in("k_ident", [128, 128])
    kdft_d = dram_in("k_dft", [32, 128, 2048])
    kcdft_d = dram_in("k_cdft", [128, 1024])
    kbd_d = dram_in("k_bd", [2, 128, 128])
    krope_d = dram_in("k_rope", [2, 128, SEQ])
    kperm_d = dram_in("k_perm", [128, 128])
    kswap_d = dram_in("k_swap", [128, 128])
    kinvw_d = dram_in("k_invw", [128, 2])
    krcnt_d = dram_in("k_rcnt", [128, 32])
    krcf_d = dram_in("k_rcf", [3, 128, 1024])
    out_d = nc.dram_tensor("out", [nb, SEQ, D], F32, kind="ExternalOutput").ap()
    win_s = nc.dram_tensor("win_s", [DEPTH, 128, 10 * 8 * 128], BF16, kind="Internal").ap()
    wout_s = nc.dram_tensor("wout_s", [DEPTH, 128, 8 * 8 * 128], BF16, kind="Internal").ap()
    ab_s = nc.dram_tensor("ab_s", [DEPTH, 128, 2 * 512], BF16, kind="Internal").ap()
    dft_s = nc.dram_tensor("dft_s", [32, 128, 2048], BF16, kind="Internal").ap()

    wgu_s = [nc.dram_tensor(f"wgu_s{f}", [DEPTH, 128, NFC * 8 * 256], BF16, kind="Internal").ap() for f in range(2)]
    wdn_s = [nc.dram_tensor(f"wdn_s{f}", [DEPTH, 128, 8 * NFC * 128], BF16, kind="Internal").ap() for f in range(2)]

    xT = nc.alloc_sbuf_tensor("xT", [128, 8, SEQ], F32)
    cT = nc.alloc_sbuf_tensor("cT", [128, 8, CTX], F32)
    mod = nc.alloc_sbuf_tensor("mod", [128, DEPTH, 72, 5], F32)
    ident = nc.alloc_sbuf_tensor("ident", [128, 128], F32)
    ones_bf = nc.alloc_sbuf_tensor("ones_bf", [128, 128], BF16)
    fgain = nc.alloc_sbuf_tensor("fgain", [128, 8], F32)
    perm_bf = nc.alloc_sbuf_tensor("perm_bf", [128, 128], BF16)
    bdones_bf = nc.alloc_sbuf_tensor("bdones_bf", [128, 128], BF16)
    esel_bf = nc.alloc_sbuf_tensor("esel_bf", [128, 2, 128], BF16)
    swap_f = nc.alloc_sbuf_tensor("swap_f", [128, 128], F32)
    cdft_bf = nc.alloc_sbuf_tensor("cdft_bf", [128, 2, 2, 256], BF16)
    poolw_bf = nc.alloc_sbuf_tensor("poolw_bf", [128, DEPTH, 2, 128], BF16)
    pscale = nc.alloc_sbuf_tensor("pscale", [128, DEPTH, 2], F32)
    qg = nc.alloc_sbuf_tensor("qg", [128, DEPTH], F32)
    kg = nc.alloc_sbuf_tensor("kg", [128, DEPTH], F32)
    invw = nc.alloc_sbuf_tensor("invw", [128, 2], F32)
    rcnt = nc.alloc_sbuf_tensor("rcnt", [128, 2, 16], F32)
    WORK_BYTES = 120 * 1024
    work = nc.alloc_sbuf_tensor("work", [128, WORK_BYTES // 2], BF16)

    class Arena:
        def __init__(self):
            self.off = 0

        def alloc(self, shape_free, dt):
            n = int(np.prod(shape_free))
            esz = 4 if dt == F32 else 2
            nbytes = (n * esz + 63) // 64 * 64
            assert self.off + nbytes <= WORK_BYTES, (self.off, nbytes)
            v = work[:, self.off // 2:(self.off + n * esz) // 2]
            self.off += nbytes
            if dt == F32:
                v = v.bitcast(F32)
            if len(shape_free) == 2:
                v = v.rearrange("p (a b) -> p a b", a=shape_free[0])
            elif len(shape_free) == 3:
                v = v.rearrange("p (a b c) -> p a b c", a=shape_free[0], b=shape_free[1])
            return v

    pp = [nc.alloc_psum_tensor(f"pp{i}", [128, 1024], F32) for i in range(4)]
    ps = [pp[i // 2][:, (i % 2) * 512:(i % 2 + 1) * 512] for i in range(8)]

    sem_misc = S.dma_sem("misc")
    S.op('sp', lambda h: h.dma_start(out=ident[:], in_=ident_d[:, :]), writes=['ident'], dma=sem_misc)
    S.op('dve', lambda h: h.memset(ones_bf[:], 1.0), writes=['ones_bf'])
    with nc.allow_non_contiguous_dma(reason="tiny one-time vector loads"):
        pass
    S.op('sp', lambda h: h.dma_start(out=fgain[:], in_=fg_d.rearrange("(k p) -> p k", p=128),
                                     allow_slow_non_contiguous=True), writes=['fgain'], dma=S.dma_sem("fg"))

    def prologue_ffn():
        A = Arena()
        stage = [A.alloc([DFF], F32) for _ in range(2)]
        stsem = [S.dma_sem(f"pst{i}") for i in range(2)]
        big = A.alloc([NFC * 8 * 256], BF16)
        bigsem = S.dma_sem("pbig")
        cnt = 0

        def cp(eng, dst, src, reads, writes):
            if eng == 'act':
                S.op('act', lambda h: h.copy(out=dst, in_=src), reads=reads, writes=writes)
            else:
                S.op('dve', lambda h: h.tensor_copy(out=dst, in_=src), reads=reads, writes=writes)

        for l in range(depth):
            for f in range(2):
                bigv = big.rearrange("p (j k h c) -> p j k h c", j=NFC, k=8, h=2)
                for kc in range(8):
                    for hh in range(2):
                        st = stage[cnt % 2]
                        rs = ('pstage', cnt % 2)
                        S.op('sp', lambda h, st=st, l=l, f=f, kc=kc, hh=hh: h.dma_start(
                            out=st, in_=wgu_d[f][l, kc * 128:(kc + 1) * 128, hh * DFF:(hh + 1) * DFF]),
                            writes=[rs], dma=stsem[cnt % 2])
                        cp('act' if cnt % 2 == 0 else 'dve', bigv[:, :, kc, hh, :],
                           st.rearrange("p (j c) -> p j c", j=NFC), [rs], [('pbig', cnt % 2)])
                        cnt += 1
                S.op('sp', lambda h, l=l, f=f: h.dma_start(out=wgu_s[f][l], in_=big),
                     reads=[('pbig', 0), ('pbig', 1)], writes=[('wgu_s', f, l), ('pbig', 0), ('pbig', 1)], dma=bigsem)
                bigd = big[:, 0:8 * NFC * 128].rearrange("p (d j c) -> p d j c", d=8, j=NFC)
                for q in range(11):
                    st = stage[cnt % 2]
                    rs = ('pstage', cnt % 2)
                    stv = st[:, 0:2048].rearrange("p (j d) -> p j d", j=2)
                    S.op('sp', lambda h, stv=stv, l=l, f=f, q=q: h.dma_start(
                        out=stv, in_=wdn_d[f][l, q * 256:(q + 1) * 256, :].rearrange("(j p) d -> p j d", p=128)),
                        writes=[rs], dma=stsem[cnt % 2])
                    cp('act' if cnt % 2 == 0 else 'dve', bigd[:, :, q * 2:(q + 1) * 2, :],
                       stv.rearrange("p j (d c) -> p d j c", d=8), [rs], [('pbig', cnt % 2)])
                    cnt += 1
                S.op('sp', lambda h, l=l, f=f: h.dma_start(out=wdn_s[f][l], in_=big[:, 0:8 * NFC * 128]),
                     reads=[('pbig', 0), ('pbig', 1)], writes=[('wdn_s', f, l), ('pbig', 0), ('pbig', 1)], dma=bigsem)
        S.barrier()

    def prologue_mod():
        A = Arena()
        ccT = A.alloc([8, 5], F32)
        adab = A.alloc([DEPTH, 72], F32)
        wst = [A.alloc([8, 1152], F32) for _ in range(2)]
        wsem = [S.dma_sem(f"adaw{i}") for i in range(2)]
        sm = S.dma_sem("modmisc")
        for b in range(nb):
            S.op('sp', lambda h, b=b: h.dma_start(out=ccT[:, :, b:b + 1], in_=c_d[b].rearrange("(k p o) -> p k o", p=128, o=1),
                                                  allow_slow_non_contiguous=True), writes=[('ccT', b)], dma=sm)
        S.op('sp', lambda h: h.dma_start(out=ccT[:, :, 4:5], in_=cctx_d.rearrange("(k p o) -> p k o", p=128, o=1),
                                         allow_slow_non_contiguous=True), writes=[('ccT', 4)], dma=sm)
        if nb < 4:
            for b in range(nb, 4):
                S.op('dve', lambda h, b=b: h.memset(ccT[:, :, b:b + 1], 0.0), writes=[('ccT', b)])
        S.op('sp', lambda h: h.dma_start(out=adab, in_=adab_d.rearrange("l (f p) -> p l f", p=128),
                                         allow_slow_non_contiguous=True), writes=['adab'], dma=sm)
        S.op('act', lambda h: h.activation(out=ccT, in_=ccT, func=AF.Silu),
             reads=[('ccT', b) for b in range(5)], writes=['ccS'])
        cnt = 0
        for l in range(depth):
            for piece in range(8):
                st = wst[cnt % 2]
                rs = ('adawst', cnt % 2)
                S.op('sp', lambda h, st=st, l=l, piece=piece: h.dma_start(
                    out=st, in_=adaw_d[l, :, piece * 1152:(piece + 1) * 1152].rearrange("(k p) f -> p k f", p=128)),
                    writes=[rs], dma=wsem[cnt % 2])
                bank = ps[cnt % 2]

                def mm(h, st=st, bank=bank):
                    ins = None
                    for fcl in range(9):
                        for kc in range(8):
                            ins = h.matmul(bank[:, fcl * 8:fcl * 8 + 5], lhsT=st[:, kc, fcl * 128:(fcl + 1) * 128],
                                           rhs=ccT[:, kc, :], start=(kc == 0), stop=(kc == 7))
                    return ins
                S.op('pe', mm, reads=[rs, 'ccS'], writes=[('psb', cnt % 2)])
                S.op('dve', lambda h, bank=bank, l=l, piece=piece: h.tensor_tensor(
                    out=mod[:, l, piece * 9:(piece + 1) * 9, :],
                    in0=bank[:, 0:72].rearrange("p (f e) -> p f e", e=8)[:, :, 0:5],
                    in1=adab[:, l, piece * 9:(piece + 1) * 9].unsqueeze(2).broadcast_to([128, 9, 5]), op=ALU.add),
                    reads=[('psb', cnt % 2), 'adab'], writes=[('mod', l)])
                cnt += 1
            for m in (1, 4, 7):
                S.op('dve', lambda h, l=l, m=m: h.tensor_scalar_add(out=mod[:, l, m * 8:(m + 1) * 8, :], in0=mod[:, l, m * 8:(m + 1) * 8, :], scalar1=1.0),
                     reads=[('mod', l)], writes=[('mod', l)])
            for m in (2, 8):
                S.op('dve', lambda h, l=l, m=m: h.tensor_scalar_mul(out=mod[:, l, m * 8:(m + 1) * 8, :], in0=mod[:, l, m * 8:(m + 1) * 8, :], scalar1=0.5),
                     reads=[('mod', l)], writes=[('mod', l)])
        S.barrier()

    def modv(l, m, kc, b):
        return mod[:, l, m * 8 + kc, b:b + 1]

    class FFNBufs:
        def __init__(self):
            A = Arena()
            self.hT = [A.alloc([8, 512], BF16) for _ in range(2)]
            self.sq = A.alloc([8, 512], BF16)
            self.t = A.alloc([8, 512], F32)
            self.rs = A.alloc([512], F32)
            self.sg = [A.alloc([512], F32) for _ in range(2)]
            self.act = A.alloc([NFC, 512], BF16)
            self.wgu = [(A.alloc([8, 256], BF16), S.dma_sem(f"wgu{i}")) for i in range(4)]
            self.wdn = [(A.alloc([NFC, 128], BF16), S.dma_sem(f"wdn{i}")) for i in range(4)]
            self.end = A.off
            self.tile_i = 0

    class IOBufs:
        def __init__(self, base):
            A = Arena()
            A.off = base
            self.stg = [(A.alloc([1024], F32), S.dma_sem(f"io{i}")) for i in range(2)]
            self.i = 0

    ffnb = None
    iob = None

    def rms_stats(xv, T, scale, key_in, bufs):
        S.op('act', lambda h: h.activation(out=bufs.sq[:, :, :T], in_=xv, func=AF.Square),
             reads=[key_in], writes=['sq'])

        def mm(h):
            ins = None
            for kc in range(8):
                ins = h.matmul(ps[6][:, :T], lhsT=ones_bf[:], rhs=bufs.sq[:, kc, :T], start=(kc == 0), stop=(kc == 7))
            return ins
        S.op('pe', mm, reads=['sq', 'ones_bf'], writes=['ps6'])
        S.op('act', lambda h: h.activation(out=bufs.rs[:, :T], in_=ps[6][:, :T], func=AF.Sqrt, scale=scale, bias=EPS),
             reads=['ps6'], writes=['rs'])
        S.op('dve', lambda h: h.reciprocal(out=bufs.rs[:, :T], in_=bufs.rs[:, :T]), reads=['rs'], writes=['rs'])

    def ffn_tile(l, f, xv, T, b, xkey):
        B = ffnb
        ti = B.tile_i
        B.tile_i += 1
        hT = B.hT[ti % 2]
        hkey = ('hT', ti % 2)
        m0 = 0 if f == 0 else 6
        rms_stats(xv, T, 1.0 / D, xkey, B)
        for kc in range(8):
            S.op('dve', lambda h, kc=kc: h.scalar_tensor_tensor(
                out=B.t[:, kc, :T], in0=xv[:, kc, :], scalar=modv(l, m0 + 1, kc, b), in1=B.rs[:, :T],
                op0=ALU.mult, op1=ALU.mult), reads=[xkey, 'rs'], writes=[('t', kc)])
            S.op('act', lambda h, kc=kc: h.activation(
                out=hT[:, kc, :T], in_=B.t[:, kc, :T], func=AF.Identity, bias=modv(l, m0, kc, b), scale=1.0),
                reads=[('t', kc)], writes=[hkey])
        gu_units = [wgu_s[f][l][:, j * 2048:(j + 1) * 2048].rearrange("p (k c) -> p k c", k=8) for j in range(NFC)]
        st_gu = Stream(S, 'wgu', B.wgu, gu_units)
        dn_units = [wdn_s[f][l][:, dc * NFC * 128:(dc + 1) * NFC * 128].rearrange("p (j c) -> p j c", j=NFC) for dc in range(8)]
        st_dn = Stream(S, 'wdn', B.wdn, dn_units)
        st_gu.start()
        for j in range(NFC):
            w, wkey = st_gu.get()
            pg, pu = ps[j % 2], ps[2 + j % 2]

            def mmg(h, w=w, pg=pg):
                ins = None
                for kc in range(8):
                    ins = h.matmul(pg[:, :T], lhsT=w[:, kc, 0:128], rhs=hT[:, kc, :T], start=(kc == 0), stop=(kc == 7))
                return ins

            def mmu(h, w=w, pu=pu):
                ins = None
                for kc in range(8):
                    ins = h.matmul(pu[:, :T], lhsT=w[:, kc, 128:256], rhs=hT[:, kc, :T], start=(kc == 0), stop=(kc == 7))
                return ins
            S.op('pe', mmg, reads=[wkey, hkey], writes=[('ps', j % 2)])
            S.op('pe', mmu, reads=[wkey, hkey], writes=[('ps', 2 + j % 2)])
            sg = B.sg[j % 2]
            S.op('act', lambda h, sg=sg, pg=pg: h.activation(out=sg[:, :T], in_=pg[:, :T], func=AF.Silu),
                 reads=[('ps', j % 2)], writes=[('sg', j % 2)])
            S.op('dve', lambda h, sg=sg, pu=pu, j=j: h.tensor_tensor(out=B.act[:, j, :T], in0=pu[:, :T], in1=sg[:, :T], op=ALU.mult),
                 reads=[('ps', 2 + j % 2), ('sg', j % 2)], writes=[('act', j)])
            if j == NFC - 6:
                st_dn.start()
        for dc in range(8):
            w, wkey = st_dn.get()
            pd = ps[4 + dc % 2]

            def mmd(h, w=w, pd=pd):
                ins = None
                for fc in range(NFC):
                    ins = h.matmul(pd[:, :T], lhsT=w[:, fc, :], rhs=B.act[:, fc, :T], start=(fc == 0), stop=(fc == NFC - 1))
                return ins
            S.op('pe', mmd, reads=[wkey] + [('act', j) for j in range(NFC)], writes=[('ps', 4 + dc % 2)])
            S.op('dve', lambda h, dc=dc, pd=pd: h.scalar_tensor_tensor(
                out=xv[:, dc, :], in0=pd[:, :T], scalar=modv(l, m0 + 2, dc, b), in1=xv[:, dc, :],
                op0=ALU.mult, op1=ALU.add), reads=[('ps', 4 + dc % 2), xkey], writes=[xkey])

    def load_tokens(src_rows, dstT, ntok, key):
        for tt in range(ntok // 128):
            st, sem = iob.stg[iob.i % 2]
            skey = ('iostg', iob.i % 2)
            iob.i += 1
            S.op('sp', lambda h, st=st, tt=tt: h.dma_start(out=st, in_=src_rows[tt * 128:(tt + 1) * 128, :]),
                 writes=[skey], dma=sem)
            for hf in range(2):
                bank = ps[hf]

                def tr(h, st=st, bank=bank, hf=hf):
                    ins = None
                    for q in range(4):
                        kc = hf * 4 + q
                        ins = h.transpose(out=bank[:, q * 128:(q + 1) * 128], in_=st[:, kc * 128:(kc + 1) * 128], identity=ident[:])
                    return ins
                S.op('pe', tr, reads=[skey, 'ident'], writes=[('ps', hf)])
                eng = 'act' if hf == 0 else 'dve'
                dst = dstT[:, hf * 4:(hf + 1) * 4, tt * 128:(tt + 1) * 128]
                srcv = bank[:, :].rearrange("p (q t) -> p q t", q=4)
                if eng == 'act':
                    S.op('act', lambda h, dst=dst, srcv=srcv: h.copy(out=dst, in_=srcv), reads=[('ps', hf)], writes=[key])
                else:
                    S.op('dve', lambda h, dst=dst, srcv=srcv: h.tensor_copy(out=dst, in_=srcv), reads=[('ps', hf)], writes=[key])

    def store_out(b):
        B = ffnb
        for tq in range(SEQ // 512):
            xv = xT[:, :, tq * 512:(tq + 1) * 512]
            if final_norm:
                rms_stats(xv, 512, 1.0 / D, 'xT', B)
                for kc in range(8):
                    S.op('dve', lambda h, kc=kc, xv=xv: h.scalar_tensor_tensor(
                        out=B.t[:, kc, :], in0=xv[:, kc, :], scalar=fgain[:, kc:kc + 1], in1=B.rs[:, :],
                        op0=ALU.mult, op1=ALU.mult), reads=['xT', 'rs', 'fgain'], writes=[('t', kc)])
                srcT = B.t
                skeys = [('t', kc) for kc in range(8)]
            else:
                srcT = xv
                skeys = ['xT']
            for tt in range(4):
                st, sem = iob.stg[iob.i % 2]
                skey = ('iostg', iob.i % 2)
                iob.i += 1
                for hf in range(2):
                    bank = ps[hf]

                    def tr(h, bank=bank, hf=hf, tt=tt, srcT=srcT):
                        ins = None
                        for q in range(4):
                            kc = hf * 4 + q
                            ins = h.transpose(out=bank[:, q * 128:(q + 1) * 128], in_=srcT[:, kc, tt * 128:(tt + 1) * 128], identity=ident[:])
                        return ins
                    S.op('pe', tr, reads=skeys + ['ident'], writes=[('ps', hf)])
                    dst = st[:, hf * 512:(hf + 1) * 512]
                    if hf == 0:
                        S.op('act', lambda h, dst=dst, bank=bank: h.copy(out=dst, in_=bank[:, :]), reads=[('ps', hf)], writes=[(skey, hf)])
                    else:
                        S.op('dve', lambda h, dst=dst, bank=bank: h.tensor_copy(out=dst, in_=bank[:, :]), reads=[('ps', hf)], writes=[(skey, hf)])
                r0 = tq * 512 + tt * 128
                S.op('sp', lambda h, st=st, r0=r0: h.dma_start(out=out_d[b, r0:r0 + 128, :], in_=st),
                     reads=[(skey, 0), (skey, 1)], writes=[skey], dma=sem)


    def prologue_mix():
        A = Arena()
        st32 = [A.alloc([2048], F32) for _ in range(2)]
        stsem = [S.dma_sem(f"pm{i}") for i in range(2)]
        big = A.alloc([10 * 8 * 128], BF16)
        bigsem = S.dma_sem("pmbig")
        bfr = [A.alloc([2048], BF16) for _ in range(2)]
        bfsem = [S.dma_sem(f"pmbf{i}") for i in range(2)]
        bd32 = A.alloc([2, 128], F32)
        pw32 = A.alloc([2, 128], F32)
        abt = A.alloc([2, 512], BF16)
        sm = S.dma_sem("pmmisc")
        state = {'c': 0}

        def stage_load(dmas):
            i = state['c'] % 2
            state['c'] += 1
            st = st32[i]
            key = ('pmst', i)
            if len(dmas) == 1:
                dv, src = dmas[0]
                S.op('sp', lambda h, d=dv(st), src=src: h.dma_start(out=d, in_=src), writes=[key], dma=stsem[i])
                return st, key
            S.op('dve', lambda h: h.nop(), reads=[], writes=[key])
            keys = []
            for j, (dv, src) in enumerate(dmas):
                S.op('sp', lambda h, d=dv(st), src=src: h.dma_start(out=d, in_=src), reads=[key], writes=[(key, j)], dma=stsem[i])
                keys.append((key, j))
            return st, keys

        def cp(eng, dst, src, reads, writes):
            if eng == 'act':
                S.op('act', lambda h: h.copy(out=dst, in_=src), reads=reads, writes=writes)
            else:
                S.op('dve', lambda h: h.tensor_copy(out=dst, in_=src), reads=reads, writes=writes)

        st, key = stage_load([(lambda st: st[:, 0:128], kperm_d[:, :])])
        cp('dve', perm_bf[:], st[:, 0:128], [key], ['perm_bf'])
        S.op('sp', lambda h: h.dma_start(out=swap_f[:], in_=kswap_d[:, :]), writes=['swap_f'], dma=sm)
        S.op('sp', lambda h: h.dma_start(out=invw[:], in_=kinvw_d[:, :]), writes=['invw'], dma=sm)
        S.op('sp', lambda h: h.dma_start(out=rcnt[:].rearrange("p a b -> p (a b)"), in_=krcnt_d[:, :]), writes=['rcnt'], dma=sm)
        S.op('sp', lambda h: h.dma_start(out=bd32, in_=kbd_d.rearrange("a p c -> p a c")), writes=['bd32'], dma=sm)
        S.op('dve', lambda h: h.memset(esel_bf[:].rearrange('p a b -> p (a b)'), 0.0), writes=['esel'])
        S.op('dve', lambda h: h.memset(esel_bf[:, 0, 0:64], 1.0), reads=['esel'], writes=['esel'])
        S.op('dve', lambda h: h.memset(esel_bf[:, 1, 64:128], 1.0), reads=['esel'], writes=['esel'])
        S.op('dve', lambda h: h.memset(bdones_bf[:], 0.0), writes=['bdones'])
        S.op('dve', lambda h: h.memset(bdones_bf[0:64, 0:64], 1.0), reads=['bdones'], writes=['bdones'])
        S.op('dve', lambda h: h.memset(bdones_bf[64:128, 64:128], 1.0), reads=['bdones'], writes=['bdones'])
        st, key = stage_load([(lambda st: st[:, 0:1024], kcdft_d[:, :])])
        cp('act', cdft_bf[:].rearrange("p a b c -> p (a b c)"), st[:, 0:1024], [key], ['cdft'])
        for hh in range(2):
            S.op('sp', lambda h, hh=hh: h.dma_start(out=qg[hh * 64:(hh + 1) * 64, :], in_=qg_d.rearrange("l e -> e l"),
                                                   allow_slow_non_contiguous=True), writes=[('qg', hh)], dma=sm)
            S.op('sp', lambda h, hh=hh: h.dma_start(out=kg[hh * 64:(hh + 1) * 64, :], in_=kg_d.rearrange("l e -> e l"),
                                                   allow_slow_non_contiguous=True), writes=[('kg', hh)], dma=sm)
        S.op('sp', lambda h: h.dma_start(out=pscale[:], in_=pools_d.rearrange("l (m p) -> p l m", p=128),
                                         allow_slow_non_contiguous=True), writes=['pscale'], dma=sm)
        S.barrier()
        for u in range(32 if pmdbg >= 2 else 0):
            st, key = stage_load([(lambda st: st, kdft_d[u])])
            bf = bfr[u % 2]
            bkey = ('pmbf', u % 2)
            cp('act' if u % 2 == 0 else 'dve', bf, st, [key], [bkey])
            S.op('sp', lambda h, bf=bf, u=u: h.dma_start(out=dft_s[u], in_=bf), reads=[bkey], writes=[bkey, 'dft_s'], dma=bfsem[u % 2])
        for l in range(depth if pmdbg >= 3 else 0):
            S.op('dve', lambda h: h.memset(pw32, 0.0), writes=['pw32'])
            for g in range(4):
                S.op('sp', lambda h, l=l, g=g: h.dma_start(
                    out=pw32[(g % 2) * 64:(g % 2 + 1) * 64, g // 2, (g % 2) * 64:(g % 2 + 1) * 64], in_=poolw_d[l, g]),
                    reads=['pw32'], writes=[('pw32d', g)], dma=sm)
            cp('dve', poolw_bf[:, l], pw32, ['pw32'] + [('pw32d', g) for g in range(4)], [('poolw', l), 'pw32'] + [('pw32d', g) for g in range(4)])
            for k2 in range(2 if pmdbg >= 4 else 0):
                st, key = stage_load([(lambda st: st[:, 0:256], fftw_d[l, k2 * 128:(k2 + 1) * 128, :])])

                def mm(h, st=st):
                    h.matmul(ps[0][:, 0:256], lhsT=bd32[:, 0, :], rhs=st[:, 0:256], start=True, stop=True)
                    return h.matmul(ps[0][:, 256:512], lhsT=bd32[:, 1, :], rhs=st[:, 0:256], start=True, stop=True)
                S.op('pe', mm, reads=[key, 'bd32'], writes=[('ps', 0)])
                cp('act', abt[:, k2, :], ps[0], [('ps', 0)], [('abt', k2)])
            if pmdbg >= 4:
                S.op('sp', lambda h, l=l: h.dma_start(out=ab_s[l], in_=abt.rearrange("p a b -> p (a b)")),
                     reads=[('abt', 0), ('abt', 1)], writes=[('abt', 0), ('abt', 1), ('ab_s', l)], dma=sm)
            if pmdbg < 5:
                continue
            bigv = big.rearrange("p (o k c) -> p o k c", o=10, k=8)
            for kc in range(8):
                qsrc = win_d[l, kc * 128:(kc + 1) * 128, 0:512].rearrange("p (t c e) -> p c t e", t=2, c=4)
                dl = [((lambda st, c=c: st[:, c * 128:(c + 1) * 128].rearrange("p (t e) -> p t e", t=2)), qsrc[:, c]) for c in range(4)]
                dl.append((lambda st: st[:, 512:INW], win_d[l, kc * 128:(kc + 1) * 128, 512:INW]))
                st, keys = stage_load(dl)
                cp('act' if kc % 2 == 0 else 'dve', bigv[:, :, kc, :], st[:, 0:INW].rearrange("p (o c) -> p o c", o=10),
                   keys, [('pmbig', kc % 2), ('pmst', (state['c'] - 1) % 2)])
            S.op('sp', lambda h, l=l: h.dma_start(out=win_s[l], in_=big), reads=[('pmbig', 0), ('pmbig', 1)],
                 writes=[('pmbig', 0), ('pmbig', 1), ('win_s', l)], dma=bigsem)
            if pmdbg < 6:
                continue
            bigo = big[:, 0:8 * 8 * 128].rearrange("p (d m c) -> p d m c", d=8, m=8)
            for mc in range(8):
                if mc < 4:
                    st, key = stage_load([(lambda st: st[0:64, 0:1024], wout_d[l, mc * 64:(mc + 1) * 64, :]),
                                          (lambda st: st[64:128, 0:1024], wout_d[l, (mc + 4) * 64:(mc + 5) * 64, :])])
                else:
                    st, key = stage_load([(lambda st: st[:, 0:1024], wout_d[l, mc * 128:(mc + 1) * 128, :])])
                klist = key if isinstance(key, list) else [key]
                cp('act' if mc % 2 == 0 else 'dve', bigo[:, :, mc, :], st[:, 0:1024].rearrange("p (d c) -> p d c", d=8),
                   klist, [('pmbig', mc % 2), ('pmst', (state['c'] - 1) % 2)])
            S.op('sp', lambda h, l=l: h.dma_start(out=wout_s[l], in_=big[:, 0:8 * 8 * 128]), reads=[('pmbig', 0), ('pmbig', 1)],
                 writes=[('pmbig', 0), ('pmbig', 1), ('wout_s', l)], dma=bigsem)
        S.barrier()

    class MixBufs:
        def __init__(self):
            A = Arena()
            self.qT = A.alloc([4, SEQ], BF16)
            self.qcT = A.alloc([4, CTX], BF16)
            self.kT = A.alloc([SEQ + CTX], BF16)
            self.Vx = A.alloc([18, 128], BF16)
            self.upT = A.alloc([2, SEQ + 16], F32)
            self.upc = A.alloc([2, CTX + 16], F32)
            self.uab = A.alloc([18, 512], BF16)
            base = A.off
            self.hT = A.alloc([8, 512], BF16)
            self.sq = A.alloc([8, 512], BF16)
            self.t = [A.alloc([512], F32) for _ in range(2)]
            self.rs = A.alloc([512], F32)
            self.sq2 = [A.alloc([512], BF16) for _ in range(2)]
            self.rq = [A.alloc([512], F32) for _ in range(2)]
            self.qn = [A.alloc([512], F32) for _ in range(2)]
            self.qnb = [A.alloc([512], BF16) for _ in range(2)]
            self.t1 = [A.alloc([512], F32) for _ in range(2)]
            self.qk_i = 0
            self.ufT = A.alloc([2, 512], BF16)
            self.rope = [(A.alloc([2, 512], F32), S.dma_sem(f"rope{i}")) for i in range(1)]
            self.win = [(A.alloc([8, 128], BF16), S.dma_sem(f"win{i}")) for i in range(2)]
            self.AB = A.alloc([2, 512], BF16)
            self.ABsem = S.dma_sem("ab")
            self.end1 = A.off
            A.off = base
            self.P = [A.alloc([2, 512], BF16) for _ in range(3)]
            self.rr = A.alloc([512], F32)
            self.rc = A.alloc([512], F32)
            self.mix = [A.alloc([8, 512], BF16) for _ in range(2)]
            self.a2 = A.alloc([544], F32)
            self.a4 = A.alloc([544], F32)
            self.a8 = A.alloc([544], F32)
            self.a16 = A.alloc([544], F32)
            self.pm = A.alloc([2, 512], BF16)
            self.rcf = A.alloc([2, 512], F32)
            self.rcfsem = S.dma_sem('rcf')
            self.dft = [(A.alloc([4, 512], BF16), S.dma_sem(f"dft{i}")) for i in range(2)]
            self.wo = [(A.alloc([8, 128], BF16), S.dma_sem(f"wo{i}")) for i in range(2)]
            self.end2 = A.off
            self.mix_i = 0
            self.p_i = 0

    def qk_norm_rope(M, pq, T, gain_ap, dst, cs, cskey, tag):
        par = M.qk_i % 2
        M.qk_i += 1
        sq2, rq, qn, qnb, t1 = M.sq2[par], M.rq[par], M.qn[par], M.qnb[par], M.t1[par]
        pstat, kstat = (ps[6], 'ps6') if par == 0 else (ps[4], ('ps', 4))
        pperm, kperm = (ps[7], 'ps7') if par == 0 else (ps[5], ('ps', 5))
        k = lambda n: (n, par)
        S.op('act', lambda h: h.activation(out=sq2[:, :T], in_=pq[:, :T], func=AF.Square), reads=[tag], writes=[k('sq2')])
        S.op('pe', lambda h: h.matmul(pstat[:, :T], lhsT=bdones_bf[:], rhs=sq2[:, :T], start=True, stop=True),
             reads=[k('sq2'), 'bdones'], writes=[kstat])
        S.op('act', lambda h: h.activation(out=rq[:, :T], in_=pstat[:, :T], func=AF.Sqrt, scale=1.0 / 64, bias=EPS),
             reads=[kstat], writes=[k('rq')])
        S.op('dve', lambda h: h.reciprocal(out=rq[:, :T], in_=rq[:, :T]), reads=[k('rq')], writes=[k('rq')])
        S.op('dve', lambda h: h.scalar_tensor_tensor(out=qn[:, :T], in0=pq[:, :T], scalar=gain_ap, in1=rq[:, :T],
                                                     op0=ALU.mult, op1=ALU.mult), reads=[tag, k('rq')], writes=[k('qn')])
        if cs is None:
            S.op('act', lambda h: h.copy(out=dst, in_=qn[:, :T]), reads=[k('qn')], writes=['qkdst'])
            return
        S.op('act', lambda h: h.copy(out=qnb[:, :T], in_=qn[:, :T]), reads=[k('qn')], writes=[k('qnb')])
        S.op('pe', lambda h: h.matmul(pperm[:, :T], lhsT=perm_bf[:], rhs=qnb[:, :T], start=True, stop=True),
             reads=[k('qnb'), 'perm_bf'], writes=[kperm])
        S.op('pool', lambda h: h.tensor_tensor(out=t1[:, :T], in0=qn[:, :T], in1=cs[:, 0, :T], op=ALU.mult),
             reads=[k('qn'), cskey], writes=[k('t1')])
        S.op('dve', lambda h: h.tensor_tensor(out=rq[:, :T], in0=pperm[:, :T], in1=cs[:, 1, :T], op=ALU.mult),
             reads=[kperm, cskey, k('qn')], writes=[k('rq')])
        S.op('dve', lambda h: h.tensor_tensor(out=dst, in0=rq[:, :T], in1=t1[:, :T], op=ALU.add),
             reads=[k('rq'), k('t1')], writes=['qkdst'])

    def mixer_m1(M, l, b, last):
        S.op('sp', lambda h: h.dma_start(out=M.AB.rearrange("p a b -> p (a b)"), in_=ab_s[l]), reads=[('ab_s', l)], writes=['AB'], dma=M.ABsem)
        S.op('dve', lambda h: h.memset(M.upT.rearrange('p a b -> p (a b)'), 0.0), writes=['upT'])
        S.op('dve', lambda h: h.memset(M.upc.rearrange('p a b -> p (a b)'), 0.0), writes=['upc'])
        tiles = [(xT[:, :, tq * 512:(tq + 1) * 512], 512, b, 'xT', tq) for tq in range(4)] + [(cT[:, :, :], CTX, 4, 'cT', 4)]
        rope_i = 0
        t_i = 0
        for (xv, T, bb, xkey, tq) in tiles:
            is_ctx = (tq == 4)
            rms_stats(xv, T, 1.0 / D, xkey, M)
            for kc in range(8):
                tt = M.t[t_i % 2]
                tk = ('mt', t_i % 2)
                t_i += 1
                S.op('dve', lambda h, kc=kc, tt=tt, xv=xv, T=T, bb=bb: h.scalar_tensor_tensor(
                    out=tt[:, :T], in0=xv[:, kc, :], scalar=modv(l, 4, kc, bb), in1=M.rs[:, :T],
                    op0=ALU.mult, op1=ALU.mult), reads=[xkey, 'rs'], writes=[tk])
                S.op('act', lambda h, kc=kc, tt=tt, T=T, bb=bb: h.activation(
                    out=M.hT[:, kc, :T], in_=tt[:, :T], func=AF.Identity, bias=modv(l, 3, kc, bb), scale=1.0),
                    reads=[tk], writes=['mhT'])
            if not is_ctx:
                cs, csem = M.rope[0]
                cskey = ('rope', 0)
                rope_i += 1
                S.op('sp', lambda h, cs=cs, tq=tq: h.dma_start(out=cs, in_=krope_d[:, :, tq * 512:(tq + 1) * 512].rearrange("a p t -> p a t")),
                     writes=[cskey], dma=csem)
            else:
                cs, cskey = None, None
            if is_ctx and last:
                chunks = [4, 5]
            else:
                chunks = list(range(10))
            units = [win_s[l][:, oc * 1024:(oc + 1) * 1024].rearrange("p (k c) -> p k c", k=8) for oc in chunks]
            st = Stream(S, 'win', M.win, units)
            for oc in chunks:
                w, wkey = st.get()
                if oc == 5:
                    for tc in range(T // 128):
                        ch = (tq * 4 + tc) if not is_ctx else 16 + tc

                        def mmv(h, w=w, tc=tc):
                            ins = None
                            for kc in range(8):
                                ins = h.matmul(ps[1][:, 0:128], lhsT=M.hT[:, kc, tc * 128:(tc + 1) * 128], rhs=w[:, kc, :],
                                               start=(kc == 0), stop=(kc == 7))
                            return ins
                        S.op('pe', mmv, reads=[wkey, 'mhT'], writes=[('ps', 1)])
                        S.op('act', lambda h, ch=ch: h.activation(out=M.Vx[:, ch, :], in_=ps[1][:, 0:128], func=AF.Identity), reads=[('ps', 1)], writes=[('Vxa', ch)])
                    continue
                pq = ps[0] if oc % 2 == 0 else ps[2]
                ptag = ('ps', 0) if oc % 2 == 0 else ('ps', 2)

                def mmq(h, w=w, pq=pq, T=T):
                    ins = None
                    for kc in range(8):
                        ins = h.matmul(pq[:, :T], lhsT=w[:, kc, :], rhs=M.hT[:, kc, :T], start=(kc == 0), stop=(kc == 7))
                    return ins
                S.op('pe', mmq, reads=[wkey, 'mhT'], writes=[ptag])
                if oc < 4:
                    dst = M.qT[:, oc, tq * 512:(tq + 1) * 512] if not is_ctx else M.qcT[:, oc, :]
                    qk_norm_rope(M, pq, T, qg[:, l:l + 1], dst, cs, cskey, ptag)
                elif oc == 4:
                    dst = M.kT[:, tq * 512:(tq + 1) * 512] if not is_ctx else M.kT[:, SEQ:SEQ + CTX]
                    qk_norm_rope(M, pq, T, kg[:, l:l + 1], dst, cs, cskey, ptag)
                elif oc in (6, 7):
                    m = oc - 6
                    dst = M.upT[:, m, 8 + tq * 512: 8 + (tq + 1) * 512] if not is_ctx else M.upc[:, m, 8:8 + CTX]
                    S.op('act', lambda h, dst=dst, pq=pq, T=T: h.copy(out=dst, in_=pq[:, :T]), reads=[ptag, 'upT', 'upc'], writes=[('up', tq, m)])
                else:
                    m = oc - 8
                    S.op('act', lambda h, m=m, pq=pq, T=T: h.copy(out=M.ufT[:, m, :T], in_=pq[:, :T]), reads=[ptag], writes=[('ufT', m)])
                    if m == 1:
                        for tc in range(T // 128):
                            ch = (tq * 4 + tc) if not is_ctx else 16 + tc

                            def mmab(h, tc=tc):
                                h.matmul(ps[3][:, :], lhsT=M.ufT[:, 0, tc * 128:(tc + 1) * 128], rhs=M.AB[:, 0, :], start=True, stop=False)
                                return h.matmul(ps[3][:, :], lhsT=M.ufT[:, 1, tc * 128:(tc + 1) * 128], rhs=M.AB[:, 1, :], start=False, stop=True)
                            S.op('pe', mmab, reads=[('ufT', 0), ('ufT', 1), 'AB'], writes=[('ps', 3)])
                            S.op('dve', lambda h, ch=ch: h.tensor_copy(out=M.uab[:, ch, :], in_=ps[3][:, :]), reads=[('ps', 3)], writes=[('uab', ch)])
        S.barrier()

    def pool_tile(M, l, up, T, t0, L, mixv):
        W = T + 16
        edge = (t0 == 0) or (t0 + T == L)
        if edge:
            which = 2 if L == CTX else (0 if t0 == 0 else 1)
            S.op('sp', lambda h: h.dma_start(out=M.rcf.rearrange("p a b -> p (a b)"), in_=krcf_d[which]), writes=['rcf'], dma=M.rcfsem)
        for m in range(2):
            u = up[:, m, t0:t0 + W]
            S.op('pool', lambda h, u=u: h.tensor_tensor(out=M.a2[:, 0:W - 1], in0=u[:, 0:W - 1], in1=u[:, 1:W], op=ALU.add),
                 reads=[('up', 'all')], writes=['a2'])
            S.op('pool', lambda h: h.tensor_tensor(out=M.a4[:, 0:W - 3], in0=M.a2[:, 0:W - 3], in1=M.a2[:, 2:W - 1], op=ALU.add),
                 reads=['a2'], writes=['a4'])
            if m == 1:
                S.op('pool', lambda h: h.tensor_tensor(out=M.a8[:, 0:W - 7], in0=M.a4[:, 0:W - 7], in1=M.a4[:, 4:W - 3], op=ALU.add),
                     reads=['a4'], writes=['a8'])
                S.op('pool', lambda h: h.tensor_tensor(out=M.a16[:, 0:W - 15], in0=M.a8[:, 0:W - 15], in1=M.a8[:, 8:W - 7], op=ALU.add),
                     reads=['a8'], writes=['a16'])
                srcs = [(M.a8, 4, 'a8'), (M.a16, 8, 'a16')]
            else:
                srcs = [(M.a2, 1, 'a2'), (M.a4, 2, 'a4')]
            for hh in range(2):
                a, hw, akey = srcs[hh]
                lo, hi = hh * 64, (hh + 1) * 64
                if not edge:
                    S.op('dve', lambda h, a=a, hw=hw, lo=lo, hi=hi, m=m, u=u: h.scalar_tensor_tensor(
                        out=M.pm[lo:hi, m, :T], in0=a[lo:hi, 8 - hw:8 - hw + T], scalar=invw[lo:hi, m:m + 1], in1=u[lo:hi, 8:8 + T],
                        op0=ALU.mult, op1=ALU.subtract), reads=[akey, ('up', 'all'), 'invw'], writes=[('pm', m, hh)])
                else:
                    S.op('dve', lambda h, a=a, hw=hw, lo=lo, hi=hi, m=m: h.tensor_tensor(
                        out=M.rr[lo:hi, :T], in0=a[lo:hi, 8 - hw:8 - hw + T], in1=M.rcf[lo:hi, m, :T], op=ALU.mult),
                        reads=[akey, 'rcf'], writes=[('rr', hh)])
                    S.op('dve', lambda h, lo=lo, hi=hi, m=m, u=u: h.tensor_tensor(
                        out=M.pm[lo:hi, m, :T], in0=M.rr[lo:hi, :T], in1=u[lo:hi, 8:8 + T], op=ALU.subtract),
                        reads=[('rr', hh), ('up', 'all')], writes=[('pm', m, hh)])
            S.op('pe', lambda h, m=m: h.matmul(ps[7][:, :T], lhsT=poolw_bf[:, l, m, :], rhs=M.pm[:, m, :T], start=True, stop=True),
                 reads=[('pm', m, 0), ('pm', m, 1), ('poolw', l)], writes=['ps7'])
            S.op('act', lambda h, m=m: h.activation(out=mixv[:, 4 + m, :T], in_=ps[7][:, :T], func=AF.Identity, scale=pscale[:, l, m:m + 1]),
                 reads=['ps7', 'pscale'], writes=[('mixp', m)])

    def fft_tile(M, l, tq, T, is_ctx, mixv):
        if not is_ctx:
            units = [dft_s[tq * 8 + g].rearrange("p (a c) -> p a c", a=4) for g in range(8)]
            st = Stream(S, 'dft', M.dft, units)
            for g in range(8):
                w, wkey = st.get()
                cs_, lcg = g // 4, g % 4

                def mm(h, w=w, g=g, cs_=cs_, lcg=lcg):
                    ins = None
                    for lc4 in range(4):
                        lc = lcg * 4 + lc4
                        for m in range(2):
                            ins = h.matmul(ps[m][:, :T], lhsT=M.uab[:, lc, cs_ * 256 + m * 128: cs_ * 256 + (m + 1) * 128],
                                           rhs=w[:, lc4, :], start=(g == 0 and lc4 == 0), stop=(g == 7 and lc4 == 3))
                    return ins
                S.op('pe', mm, reads=[wkey] + [('uab', lc) for lc in range(16)], writes=[('ps', 0), ('ps', 1)])
        else:
            def mm(h):
                ins = None
                n = 0
                for cs_ in range(2):
                    for lc in range(2):
                        for m in range(2):
                            ins = h.matmul(ps[m][:, :T], lhsT=M.uab[:, 16 + lc, cs_ * 256 + m * 128: cs_ * 256 + (m + 1) * 128],
                                           rhs=cdft_bf[:, cs_, lc, :], start=(n == 0), stop=(n == 3))
                        n += 1
                return ins
            S.op('pe', mm, reads=['cdft', ('uab', 16), ('uab', 17)], writes=[('ps', 0), ('ps', 1)])
        S.op('act', lambda h: h.copy(out=mixv[:, 6, :T], in_=ps[0][:, :T]), reads=[('ps', 0)], writes=[('mixf', 0)])
        S.op('dve', lambda h: h.tensor_copy(out=mixv[:, 7, :T], in_=ps[1][:, :T]), reads=[('ps', 1)], writes=[('mixf', 1)])

    def attn_tile(M, qsrc, T, kchunks, mixv):
        nk = len(kchunks)

        def rec_S(c, i, ch):
            sbank = pp[(M.p_i) % 2]
            skeys = [('ps', 2 * (M.p_i % 2)), ('ps', 2 * (M.p_i % 2) + 1)]
            P = M.P[M.p_i % 3]
            pkey = ('P', M.p_i % 3)
            M.p_i += 1

            def mms(h):
                h.matmul(sbank[:, 0:T], lhsT=M.kT[0:64, ch * 128:(ch + 1) * 128], rhs=qsrc[0:64, c, :], start=True, stop=True)
                return h.matmul(sbank[:, 512:512 + T], lhsT=M.kT[64:128, ch * 128:(ch + 1) * 128], rhs=qsrc[64:128, c, :], start=True, stop=True)
            S.op('pe', mms, reads=['qk'], writes=skeys)
            if T == 512:
                S.op('act', lambda h: h.activation(out=P.rearrange("p a b -> p (a b)"), in_=sbank[:, :], func=AF.Exp, scale=0.125),
                     reads=skeys, writes=[pkey])
            else:
                S.op('act', lambda h: h.activation(out=P[:, :, :T], in_=sbank.rearrange("p (a b) -> p a b", a=2)[:, :, :T], func=AF.Exp, scale=0.125),
                     reads=skeys, writes=[pkey])
            return P, pkey

        def rec_PV(c, i, ch, P, pkey):
            def mmpv(h):
                h.matmul(ps[4][:, :T], lhsT=M.Vx[:, ch, :], rhs=P[:, 0, :T], start=(i == 0), stop=(i == nk - 1))
                h.matmul(ps[5][:, :T], lhsT=M.Vx[:, ch, :], rhs=P[:, 1, :T], start=(i == 0), stop=(i == nk - 1))
                h.matmul(ps[6][:, :T], lhsT=esel_bf[:, 0, :], rhs=P[:, 0, :T], start=(i == 0), stop=False)
                return h.matmul(ps[6][:, :T], lhsT=esel_bf[:, 1, :], rhs=P[:, 1, :T], start=False, stop=(i == nk - 1))
            S.op('pe', mmpv, reads=[pkey, 'Vxall', 'esel'], writes=[('ps', 4), ('ps', 5), 'ps6'])
            if i == nk - 1:
                S.op('dve', lambda h: h.reciprocal(out=M.rc[:, :T], in_=ps[6][:, :T]), reads=['ps6'], writes=['rc'])
                S.op('dve', lambda h: h.tensor_tensor(out=mixv[0:64, c, :T], in0=ps[4][0:64, :T], in1=M.rc[0:64, :T], op=ALU.mult),
                     reads=[('ps', 4), 'rc'], writes=[('mixa', c, 0)])
                S.op('dve', lambda h: h.tensor_tensor(out=mixv[64:128, c, :T], in0=ps[5][64:128, :T], in1=M.rc[64:128, :T], op=ALU.mult),
                     reads=[('ps', 5), 'rc'], writes=[('mixa', c, 1)])

        items = [(c, i, ch) for c in range(4) for i, ch in enumerate(kchunks)]
        prev = None
        for it in items:
            P, pkey = rec_S(*it)
            if prev is not None:
                rec_PV(*prev)
            prev = it + (P, pkey)
        rec_PV(*prev)

    def wout_tile(M, l, b, xv, T, xkey, mixv, mixkeys):
        units = [wout_s[l][:, dc * 1024:(dc + 1) * 1024].rearrange("p (m c) -> p m c", m=8) for dc in range(8)]
        st = Stream(S, 'wo', M.wo, units)
        for dc in range(8):
            w, wkey = st.get()
            pd = ps[dc % 2]

            def mm(h, w=w, pd=pd):
                ins = None
                for mc in range(8):
                    ins = h.matmul(pd[:, :T], lhsT=w[:, mc, :], rhs=mixv[:, mc, :T], start=(mc == 0), stop=(mc == 7))
                return ins
            S.op('pe', mm, reads=[wkey] + mixkeys, writes=[('ps', dc % 2)])
            S.op('dve', lambda h, dc=dc, pd=pd: h.scalar_tensor_tensor(
                out=xv[:, dc, :], in0=pd[:, :T], scalar=modv(l, 5, dc, b), in1=xv[:, dc, :],
                op0=ALU.mult, op1=ALU.add), reads=[('ps', dc % 2), xkey], writes=[xkey])

    mixb = []

    def mixer(l, b, last):
        if not mixb:
            mixb.append(MixBufs())
        M = mixb[0]
        if mixdbg < 2:
            return
        mixer_m1(M, l, b, last)
        if mixdbg < 3:
            return
        mixkeys = [('mixa', c, hh) for c in range(4) for hh in range(2)] + [('mixp', 0), ('mixp', 1), ('mixf', 0), ('mixf', 1)]
        tiles = [(tq, 512, False) for tq in range(4)]
        if not last:
            tiles.append((0, CTX, True))
        for (tq, T, is_ctx) in tiles:
            mixv = M.mix[M.mix_i % 2]
            M.mix_i += 1
            if not is_ctx:
                if mixdbg >= 3:
                    pool_tile(M, l, M.upT, 512, tq * 512, SEQ, mixv)
                if mixdbg >= 4:
                    fft_tile(M, l, tq, 512, False, mixv)
                if mixdbg >= 5:
                    attn_tile(M, M.qT[:, :, tq * 512:(tq + 1) * 512], 512, list(range(18)), mixv)
                if mixdbg >= 6:
                    wout_tile(M, l, b, xT[:, :, tq * 512:(tq + 1) * 512], 512, 'xT', mixv, mixkeys)
            else:
                if mixdbg >= 3:
                    pool_tile(M, l, M.upc, CTX, 0, CTX, mixv)
                if mixdbg >= 4:
                    fft_tile(M, l, 0, CTX, True, mixv)
                if mixdbg >= 5:
                    attn_tile(M, M.qcT[:, :, :], CTX, [16, 17], mixv)
                if mixdbg >= 6:
                    wout_tile(M, l, 4, cT[:, :, :], CTX, 'cT', mixv, mixkeys)
        S.barrier()

    if stages != 'io':
        prologue_ffn()
        prologue_mod()
        if stages != 'ffn1':
            prologue_mix()
    ffnb = FFNBufs()
    iob = IOBufs(ffnb.end)
    for b in range(nb):
        load_tokens(x_d[b], xT, SEQ, 'xT')
        load_tokens(ctx_d[b], cT, CTX, 'cT')
        if stages != 'io':
            for l in range(depth):
                last = (l == DEPTH - 1)
                for tq in range(4):
                    ffn_tile(l, 0, xT[:, :, tq * 512:(tq + 1) * 512], 512, b, 'xT')
                ffn_tile(l, 0, cT[:, :, :], CTX, 4, 'cT')
                if stages == 'ffn1':
                    continue
                S.barrier()
                mixer(l, b, last)
                if stages == 'mix':
                    continue
                for tq in range(4):
                    ffn_tile(l, 1, xT[:, :, tq * 512:(tq + 1) * 512], 512, b, 'xT')
                if not last:
                    ffn_tile(l, 1, cT[:, :, :], CTX, 4, 'cT')
        store_out(b)
        S.barrier()
        S.new_epoch()
    S.barrier()
    S.op('sp', lambda h: h.nop(), reads=[], writes=[])

    with nc.Block() as block:
        S.emit(block)
    return nc


def make_consts():
    c = {"k_ident": np.eye(128, dtype=np.float32)}
    l = np.arange(2048, dtype=np.int64)
    mm_ = (l[:, None] * l[None, :]) % 2048
    ang = 2.0 * np.pi * mm_.astype(np.float64) / 2048.0
    nrm = 1.0 / np.sqrt(2048.0)
    mats = [np.cos(ang) * nrm, np.sin(ang) * nrm]
    dft = np.zeros((4, 2, 4, 128, 4, 512), np.float32)
    for cs in range(2):
        m = mats[cs].reshape(4, 4, 128, 4, 512)
        dft[:, cs] = m.transpose(3, 0, 2, 1, 4)
    c["k_dft"] = dft.reshape(32, 128, 2048)
    lc_ = np.arange(256, dtype=np.int64)
    angc = 2.0 * np.pi * ((lc_[:, None] * lc_[None, :]) % 256).astype(np.float64) / 256.0
    cm = [np.cos(angc) / 16.0, np.sin(angc) / 16.0]
    cd = np.zeros((128, 2, 2, 256), np.float32)
    for cs in range(2):
        cd[:, cs] = cm[cs].reshape(2, 128, 256).transpose(1, 0, 2)
    c["k_cdft"] = cd.reshape(128, 1024)
    cc = np.arange(64, dtype=np.int64)
    a64 = 2.0 * np.pi * ((cc[:, None] * cc[None, :]) % 64).astype(np.float64) / 64.0
    bd = np.zeros((2, 128, 128), np.float32)
    for o in (0, 64):
        bd[0, o:o + 64, o:o + 64] = np.cos(a64) / 8.0
        bd[1, o:o + 64, o:o + 64] = -np.sin(a64) / 8.0
    c["k_bd"] = bd
    t = np.arange(SEQ)
    row = (t // 64).astype(np.float32)
    col = (t % 64).astype(np.float32)
    inv = (np.float32(10000.0) ** (-(np.arange(16, dtype=np.float32)) / np.float32(16))).astype(np.float32)
    angr = np.concatenate([row[:, None] * inv[None, :], col[:, None] * inv[None, :]], axis=-1).astype(np.float32)
    cosr = np.cos(angr).astype(np.float32).T
    sinr = np.sin(angr).astype(np.float32).T
    rope = np.zeros((2, 128, SEQ), np.float32)
    for p in range(128):
        rope[0, p] = cosr[(p % 64) % 32]
        rope[1, p] = sinr[(p % 64) % 32]
    c["k_rope"] = rope
    perm = np.zeros((128, 128), np.float32)
    for o in (0, 64):
        for m in range(64):
            if m < 32:
                perm[o + m + 32, o + m] = -1.0
            else:
                perm[o + m - 32, o + m] = 1.0
    c["k_perm"] = perm
    sw = np.zeros((128, 128), np.float32)
    for m in range(128):
        sw[(m + 64) % 128, m] = 1.0
    c["k_swap"] = sw
    ws = [2, 4, 8, 16]
    invw = np.zeros((128, 2), np.float32)
    rc = np.zeros((128, 2, 16), np.float32)
    for p in range(128):
        for m in range(2):
            w = ws[2 * m + p // 64]
            invw[p, m] = 1.0 / w
            for i in range(8):
                tt = i
                rc[p, m, i] = 1.0 / ((tt + w // 2) - max(tt - w // 2, 0))
                dd = 8 - i
                rc[p, m, 8 + i] = 1.0 / (min(w // 2, dd) + w // 2)
    c["k_invw"] = invw
    rcf = np.zeros((3, 128, 2, 512), np.float32)
    for p in range(128):
        for m in range(2):
            w = ws[2 * m + p // 64]
            for which, (Lx, t0) in enumerate(((SEQ, 0), (SEQ, SEQ - 512), (CTX, 0))):
                for i in range(512):
                    tt = t0 + i
                    if tt >= Lx:
                        rcf[which, p, m, i] = 1.0 / w
                    else:
                        rcf[which, p, m, i] = 1.0 / (min(tt + w // 2, Lx) - max(tt - w // 2, 0))
    c["k_rcf"] = rcf.reshape(3, 128, 1024)
    c["k_rcnt"] = rc.reshape(128, 32)
    return c


def kernel(**inputs):
    cfg = inputs.pop('_cfg', {})
    nb = cfg.get('nb', NB_CORE)
    ncores = cfg.get('ncores', NCORES)
    nc = build_nc(cfg)
    consts = make_consts()
    in_maps = []
    for i in range(ncores):
        m = {}
        for k, v in inputs.items():
            v = np.asarray(v)
            if k in ('x', 'c', 'ctx'):
                v = np.ascontiguousarray(v[i * nb:(i + 1) * nb])
            m[k] = v
        m.update(consts)
        in_maps.append(m)
    res = run_bass_kernel_spmd(nc, in_maps, core_ids=list(range(ncores)))
    return np.concatenate([np.asarray(r["out"]) for r in res.results], axis=0)
```

```python
import numpy as np
import concourse.bass as bass
import concourse.mybir as mybir
from concourse.bass_utils import run_bass_kernel_spmd

F32 = mybir.dt.float32
BF16 = mybir.dt.bfloat16
AF = mybir.ActivationFunctionType
ALU = mybir.AluOpType

D = 1024
SEQ = 2048
CTX = 256
DEPTH = 4
DFF = 2816
NFC = 22
INW = 1280
EPS = 1e-6
NB_CORE = 4
NCORES = 8
ENGS = ('pe', 'act', 'dve', 'pool', 'sp')


class _Op:
    __slots__ = ('id', 'eng', 'fn', 'deps', 'signal', 'sem', 'val', 'dma', 'epoch')


class _Res:
    __slots__ = ('w', 'r', 'rd')

    def __init__(self):
        self.w = None
        self.r = {}
        self.rd = []


class DmaSem:
    def __init__(self, sem):
        self.sem = sem
        self.count = 0


class Sched:
    def __init__(self, nc):
        self.nc = nc
        self.ops = {e: [] for e in ENGS}
        self.all = []
        self.res = {}
        self.epoch = 0
        self.pending_barrier = {e: set() for e in ENGS}
        self.dma_since_barrier = []
        self.engsem = {}
        self.dmasems = []

    def dma_sem(self, name):
        s = DmaSem(self.nc.alloc_semaphore(name))
        self.dmasems.append(s)
        return s

    def new_epoch(self):
        self.epoch += 1

    def op(self, eng, fn, reads=(), writes=(), dma=None):
        o = _Op()
        o.id = len(self.all)
        o.eng = eng
        o.fn = fn
        o.signal = False
        o.dma = dma
        o.epoch = self.epoch
        o.sem = None
        o.val = None
        deps = set(self.pending_barrier[eng])
        self.pending_barrier[eng] = set()
        for r in reads:
            st = self.res.get(r)
            if st is not None and st.w is not None:
                deps.add(st.w)
        for w in writes:
            st = self.res.get(w)
            if st is not None:
                if st.w is not None:
                    deps.add(st.w)
                deps.update(st.r.values())
                deps.update(st.rd)
        for r in reads:
            st = self.res.setdefault(r, _Res())
            if dma is not None:
                st.rd.append(o.id)
            else:
                st.r[eng] = o.id
        for w in writes:
            st = self.res.setdefault(w, _Res())
            st.w = o.id
            st.r = {}
            st.rd = []
        best = {}
        keep = set()
        for d in deps:
            p = self.all[d]
            if p.dma is not None:
                keep.add(d)
            else:
                k = (p.eng, p.epoch)
                if k not in best or best[k] < d:
                    best[k] = d
        keep.update(best.values())
        o.deps = keep
        if dma is not None:
            dma.count += 16
            o.sem = dma.sem
            o.val = dma.count
            self.dma_since_barrier.append(o.id)
        self.all.append(o)
        self.ops[eng].append(o)
        return o

    def barrier(self):
        last = set()
        for e in ENGS:
            for o in reversed(self.ops[e]):
                if o.dma is None:
                    last.add(o.id)
                    break
        last.update(self.dma_since_barrier)
        self.dma_since_barrier = []
        for e in ENGS:
            self.pending_barrier[e].update(last)

    def emit(self, block):
        nc = self.nc
        for o in self.all:
            for d in o.deps:
                self.all[d].signal = True
        nep = self.epoch + 1
        for e in ENGS:
            if e == 'sp':
                continue
            self.engsem[e] = [nc.alloc_semaphore(f"s_{e}_{i}") for i in range(nep)]
        for e in ENGS:
            cnt = {}
            for o in self.ops[e]:
                if o.dma is None and o.signal:
                    if e == 'sp':
                        raise RuntimeError("non-dma op on sp cannot signal")
                    cnt[o.epoch] = cnt.get(o.epoch, 0) + 1
                    o.val = cnt[o.epoch]
                    o.sem = self.engsem[e][o.epoch]
        allops = self.all

        def run(name, h):
            waited = {}
            for o in self.ops[name]:
                for d in sorted(o.deps):
                    p = allops[d]
                    key = id(p.sem)
                    if waited.get(key, 0) >= p.val:
                        continue
                    h.wait_ge(p.sem, p.val)
                    waited[key] = p.val
                ins = o.fn(h)
                if o.dma is not None:
                    ins.then_inc(o.sem, 16)
                elif o.signal:
                    ins.then_inc(o.sem, 1)

        @block.tensor
        def _(h):
            run('pe', h)

        @block.scalar
        def _(h):
            run('act', h)

        @block.vector
        def _(h):
            run('dve', h)

        @block.gpsimd
        def _(h):
            run('pool', h)

        @block.sync
        def _(h):
            run('sp', h)


class Stream:
    def __init__(self, S, name, slots, units):
        self.S = S
        self.name = name
        self.slots = slots
        self.units = units
        self.issued = 0
        self.got = 0

    def _issue(self):
        i = self.issued
        ap, sem = self.slots[i % len(self.slots)]
        src = self.units[i]
        self.S.op('sp', lambda h, ap=ap, src=src: h.dma_start(out=ap, in_=src),
                  writes=[(self.name, i % len(self.slots))], dma=sem)
        self.issued += 1

    def start(self):
        while self.issued < min(len(self.units), len(self.slots)):
            self._issue()

    def get(self):
        i = self.got
        while self.issued < min(len(self.units), i + len(self.slots)):
            self._issue()
        self.got += 1
        return self.slots[i % len(self.slots)][0], (self.name, i % len(self.slots))


def build_nc(cfg):
    nb = cfg.get('nb', NB_CORE)
    depth = cfg.get('depth', DEPTH)
    stages = cfg.get('stages', 'full')
    final_norm = cfg.get('final_norm', True)
    mixdbg = cfg.get('mixdbg', 9)
    pmdbg = cfg.get('pmdbg', 9)

    nc = bass.Bass("TRN2", target_bir_lowering=False)
    S = Sched(nc)

    def dram_in(name, shape, dt=F32):
        return nc.dram_tensor(name, list(shape), dt, kind="ExternalInput").ap()

    x_d = dram_in("x", [nb, SEQ, D])
    c_d = dram_in("c", [nb, D])
    ctx_d = dram_in("ctx", [nb, CTX, D])
    cctx_d = dram_in("c_ctx", [D])
    adaw_d = dram_in("ada_w", [DEPTH, D, 9 * D])
    adab_d = dram_in("ada_b", [DEPTH, 9 * D])
    wgu_d = [dram_in("ffn1_w_gu", [DEPTH, D, 2 * DFF]), dram_in("ffn2_w_gu", [DEPTH, D, 2 * DFF])]
    wdn_d = [dram_in("ffn1_w_down", [DEPTH, DFF, D]), dram_in("ffn2_w_down", [DEPTH, DFF, D])]
    win_d = dram_in("w_in", [DEPTH, D, INW])
    qg_d = dram_in("q_gain", [DEPTH, 64])
    kg_d = dram_in("k_gain", [DEPTH, 64])
    poolw_d = dram_in("pool_w", [DEPTH, 4, 64, 64])
    pools_d = dram_in("pool_scale", [DEPTH, 256])
    fftw_d = dram_in("fft_w", [DEPTH, 256, 256])
    wout_d = dram_in("w_out", [DEPTH, D, D])
    fg_d = dram_in("final_gain", [D])
    ident_d = dram_in("k_ident", [128, 128])
    kdft_d = dram_in("k_dft", [32, 128, 2048])
    kcdft_d = dram_in("k_cdft", [128, 1024])
    kbd_d = dram_in("k_bd", [2, 128, 128])
    krope_d = dram_in("k_rope", [2, 128, SEQ])
    kperm_d = dram_in("k_perm", [128, 128])
    kswap_d = dram_in("k_swap", [128, 128])
    kinvw_d = dram_in("k_invw", [128, 2])
    krcnt_d = dram_in("k_rcnt", [128, 32])
    krcf_d = dram_in("k_rcf", [3, 128, 1024])
    out_d = nc.dram_tensor("out", [nb, SEQ, D], F32, kind="ExternalOutput").ap()
    win_s = nc.dram_tensor("win_s", [DEPTH, 128, 10 * 8 * 128], BF16, kind="Internal").ap()
    wout_s = nc.dram_tensor("wout_s", [DEPTH, 128, 8 * 8 * 128], BF16, kind="Internal").ap()
    ab_s = nc.dram_tensor("ab_s", [DEPTH, 128, 2 * 512], BF16, kind="Internal").ap()
    dft_s = nc.dram_tensor("dft_s", [32, 128, 2048], BF16, kind="Internal").ap()

    wgu_s = [nc.dram_tensor(f"wgu_s{f}", [DEPTH, 128, NFC * 8 * 256], BF16, kind="Internal").ap() for f in range(2)]
    wdn_s = [nc.dram_tensor(f"wdn_s{f}", [DEPTH, 128, 8 * NFC * 128], BF16, kind="Internal").ap() for f in range(2)]

    xT = nc.alloc_sbuf_tensor("xT", [128, 8, SEQ], F32)
    cT = nc.alloc_sbuf_tensor("cT", [128, 8, CTX], F32)
    mod = nc.alloc_sbuf_tensor("mod", [128, DEPTH, 72, 5], F32)
    ident = nc.alloc_sbuf_tensor("ident", [128, 128], F32)
    ones_bf = nc.alloc_sbuf_tensor("ones_bf", [128, 128], BF16)
    fgain = nc.alloc_sbuf_tensor("fgain", [128, 8], F32)
    perm_bf = nc.alloc_sbuf_tensor("perm_bf", [128, 128], BF16)
    bdones_bf = nc.alloc_sbuf_tensor("bdones_bf", [128, 128], BF16)
    esel_bf = nc.alloc_sbuf_tensor("esel_bf", [128, 2, 128], BF16)
    swap_f = nc.alloc_sbuf_tensor("swap_f", [128, 128], F32)
    cdft_bf = nc.alloc_sbuf_tensor("cdft_bf", [128, 2, 2, 256], BF16)
    poolw_bf = nc.alloc_sbuf_tensor("poolw_bf", [128, DEPTH, 2, 128], BF16)
    pscale = nc.alloc_sbuf_tensor("pscale", [128, DEPTH, 2], F32)
    qg = nc.alloc_sbuf_tensor("qg", [128, DEPTH], F32)
    kg = nc.alloc_sbuf_tensor("kg", [128, DEPTH], F32)
    invw = nc.alloc_sbuf_tensor("invw", [128, 2], F32)
    rcnt = nc.alloc_sbuf_tensor("rcnt", [128, 2, 16], F32)
    WORK_BYTES = 120 * 1024
    work = nc.alloc_sbuf_tensor("work", [128, WORK_BYTES // 2], BF16)

    class Arena:
        def __init__(self):
            self.off = 0

        def alloc(self, shape_free, dt):
            n = int(np.prod(shape_free))
            esz = 4 if dt == F32 else 2
            nbytes = (n * esz + 63) // 64 * 64
            assert self.off + nbytes <= WORK_BYTES, (self.off, nbytes)
            v = work[:, self.off // 2:(self.off + n * esz) // 2]
            self.off += nbytes
            if dt == F32:
                v = v.bitcast(F32)
            if len(shape_free) == 2:
                v = v.rearrange("p (a b) -> p a b", a=shape_free[0])
            elif len(shape_free) == 3:
                v = v.rearrange("p (a b c) -> p a b c", a=shape_free[0], b=shape_free[1])
            return v

    pp = [nc.alloc_psum_tensor(f"pp{i}", [128, 1024], F32) for i in range(4)]
    ps = [pp[i // 2][:, (i % 2) * 512:(i % 2 + 1) * 512] for i in range(8)]

    sem_misc = S.dma_sem("misc")
    S.op('sp', lambda h: h.dma_start(out=ident[:], in_=ident_d[:, :]), writes=['ident'], dma=sem_misc)
    S.op('dve', lambda h: h.memset(ones_bf[:], 1.0), writes=['ones_bf'])
    with nc.allow_non_contiguous_dma(reason="tiny one-time vector loads"):
        pass
    S.op('sp', lambda h: h.dma_start(out=fgain[:], in_=fg_d.rearrange("(k p) -> p k", p=128),
                                     allow_slow_non_contiguous=True), writes=['fgain'], dma=S.dma_sem("fg"))

    def prologue_ffn():
        A = Arena()
        stage = [A.alloc([DFF], F32) for _ in range(2)]
        stsem = [S.dma_sem(f"pst{i}") for i in range(2)]
        big = A.alloc([NFC * 8 * 256], BF16)
        bigsem = S.dma_sem("pbig")
        cnt = 0

        def cp(eng, dst, src, reads, writes):
            if eng == 'act':
                S.op('act', lambda h: h.copy(out=dst, in_=src), reads=reads, writes=writes)
            else:
                S.op('dve', lambda h: h.tensor_copy(out=dst, in_=src), reads=reads, writes=writes)

        for l in range(depth):
            for f in range(2):
                bigv = big.rearrange("p (j k h c) -> p j k h c", j=NFC, k=8, h=2)
                for kc in range(8):
                    for hh in range(2):
                        st = stage[cnt % 2]
                        rs = ('pstage', cnt % 2)
                        S.op('sp', lambda h, st=st, l=l, f=f, kc=kc, hh=hh: h.dma_start(
                            out=st, in_=wgu_d[f][l, kc * 128:(kc + 1) * 128, hh * DFF:(hh + 1) * DFF]),
                            writes=[rs], dma=stsem[cnt % 2])
                        cp('act' if cnt % 2 == 0 else 'dve', bigv[:, :, kc, hh, :],
                           st.rearrange("p (j c) -> p j c", j=NFC), [rs], [('pbig', cnt % 2)])
                        cnt += 1
                S.op('sp', lambda h, l=l, f=f: h.dma_start(out=wgu_s[f][l], in_=big),
                     reads=[('pbig', 0), ('pbig', 1)], writes=[('wgu_s', f, l), ('pbig', 0), ('pbig', 1)], dma=bigsem)
                bigd = big[:, 0:8 * NFC * 128].rearrange("p (d j c) -> p d j c", d=8, j=NFC)
                for q in range(11):
                    st = stage[cnt % 2]
                    rs = ('pstage', cnt % 2)
                    stv = st[:, 0:2048].rearrange("p (j d) -> p j d", j=2)
                    S.op('sp', lambda h, stv=stv, l=l, f=f, q=q: h.dma_start(
                        out=stv, in_=wdn_d[f][l, q * 256:(q + 1) * 256, :].rearrange("(j p) d -> p j d", p=128)),
                        writes=[rs], dma=stsem[cnt % 2])
                    cp('act' if cnt % 2 == 0 else 'dve', bigd[:, :, q * 2:(q + 1) * 2, :],
                       stv.rearrange("p j (d c) -> p d j c", d=8), [rs], [('pbig', cnt % 2)])
                    cnt += 1
                S.op('sp', lambda h, l=l, f=f: h.dma_start(out=wdn_s[f][l], in_=big[:, 0:8 * NFC * 128]),
                     reads=[('pbig', 0), ('pbig', 1)], writes=[('wdn_s', f, l), ('pbig', 0), ('pbig', 1)], dma=bigsem)
        S.barrier()

    def prologue_mod():
        A = Arena()
        ccT = A.alloc([8, 5], F32)
        adab = A.alloc([DEPTH, 72], F32)
        wst = [A.alloc([8, 1152], F32) for _ in range(2)]
        wsem = [S.dma_sem(f"adaw{i}") for i in range(2)]
        sm = S.dma_sem("modmisc")
        for b in range(nb):
            S.op('sp', lambda h, b=b: h.dma_start(out=ccT[:, :, b:b + 1], in_=c_d[b].rearrange("(k p o) -> p k o", p=128, o=1),
                                                  allow_slow_non_contiguous=True), writes=[('ccT', b)], dma=sm)
        S.op('sp', lambda h: h.dma_start(out=ccT[:, :, 4:5], in_=cctx_d.rearrange("(k p o) -> p k o", p=128, o=1),
                                         allow_slow_non_contiguous=True), writes=[('ccT', 4)], dma=sm)
        if nb < 4:
            for b in range(nb, 4):
                S.op('dve', lambda h, b=b: h.memset(ccT[:, :, b:b + 1], 0.0), writes=[('ccT', b)])
        S.op('sp', lambda h: h.dma_start(out=adab, in_=adab_d.rearrange("l (f p) -> p l f", p=128),
                                         allow_slow_non_contiguous=True), writes=['adab'], dma=sm)
        S.op('act', lambda h: h.activation(out=ccT, in_=ccT, func=AF.Silu),
             reads=[('ccT', b) for b in range(5)], writes=['ccS'])
        cnt = 0
        for l in range(depth):
            for piece in range(8):
                st = wst[cnt % 2]
                rs = ('adawst', cnt % 2)
                S.op('sp', lambda h, st=st, l=l, piece=piece: h.dma_start(
                    out=st, in_=adaw_d[l, :, piece * 1152:(piece + 1) * 1152].rearrange("(k p) f -> p k f", p=128)),
                    writes=[rs], dma=wsem[cnt % 2])
                bank = ps[cnt % 2]

                def mm(h, st=st, bank=bank):
                    ins = None
                    for fcl in range(9):
                        for kc in range(8):
                            ins = h.matmul(bank[:, fcl * 8:fcl * 8 + 5], lhsT=st[:, kc, fcl * 128:(fcl + 1) * 128],
                                           rhs=ccT[:, kc, :], start=(kc == 0), stop=(kc == 7))
                    return ins
                S.op('pe', mm, reads=[rs, 'ccS'], writes=[('psb', cnt % 2)])
                S.op('dve', lambda h, bank=bank, l=l, piece=piece: h.tensor_tensor(
                    out=mod[:, l, piece * 9:(piece + 1) * 9, :],
                    in0=bank[:, 0:72].rearrange("p (f e) -> p f e", e=8)[:, :, 0:5],
                    in1=adab[:, l, piece * 9:(piece + 1) * 9].unsqueeze(2).broadcast_to([128, 9, 5]), op=ALU.add),
                    reads=[('psb', cnt % 2), 'adab'], writes=[('mod', l)])
                cnt += 1
            for m in (1, 4, 7):
                S.op('dve', lambda h, l=l, m=m: h.tensor_scalar_add(out=mod[:, l, m * 8:(m + 1) * 8, :], in0=mod[:, l, m * 8:(m + 1) * 8, :], scalar1=1.0),
                     reads=[('mod', l)], writes=[('mod', l)])
            for m in (2, 8):
                S.op('dve', lambda h, l=l, m=m: h.tensor_scalar_mul(out=mod[:, l, m * 8:(m + 1) * 8, :], in0=mod[:, l, m * 8:(m + 1) * 8, :], scalar1=0.5),
                     reads=[('mod', l)], writes=[('mod', l)])
        S.barrier()

    def modv(l, m, kc, b):
        return mod[:, l, m * 8 + kc, b:b + 1]

    class FFNBufs:
        def __init__(self):
            A = Arena()
            self.hT = [A.alloc([8, 512], BF16) for _ in range(2)]
            self.sq = A.alloc([8, 512], BF16)
            self.t = A.alloc([8, 512], F32)
            self.rs = A.alloc([512], F32)
            self.sg = [A.alloc([512], F32) for _ in range(2)]
            self.act = A.alloc([NFC, 512], BF16)
            self.wgu = [(A.alloc([8, 256], BF16), S.dma_sem(f"wgu{i}")) for i in range(4)]
            self.wdn = [(A.alloc([NFC, 128], BF16), S.dma_sem(f"wdn{i}")) for i in range(4)]
            self.end = A.off
            self.tile_i = 0

    class IOBufs:
        def __init__(self, base):
            A = Arena()
            A.off = base
            self.stg = [(A.alloc([1024], F32), S.dma_sem(f"io{i}")) for i in range(2)]
            self.i = 0

    ffnb = None
    iob = None

    def rms_stats(xv, T, scale, key_in, bufs):
        S.op('act', lambda h: h.activation(out=bufs.sq[:, :, :T], in_=xv, func=AF.Square),
             reads=[key_in], writes=['sq'])

        def mm(h):
            ins = None
            for kc in range(8):
                ins = h.matmul(ps[6][:, :T], lhsT=ones_bf[:], rhs=bufs.sq[:, kc, :T], start=(kc == 0), stop=(kc == 7))
            return ins
        S.op('pe', mm, reads=['sq', 'ones_bf'], writes=['ps6'])
        S.op('act', lambda h: h.activation(out=bufs.rs[:, :T], in_=ps[6][:, :T], func=AF.Sqrt, scale=scale, bias=EPS),
             reads=['ps6'], writes=['rs'])
        S.op('dve', lambda h: h.reciprocal(out=bufs.rs[:, :T], in_=bufs.rs[:, :T]), reads=['rs'], writes=['rs'])

    def ffn_tile(l, f, xv, T, b, xkey):
        B = ffnb
        ti = B.tile_i
        B.tile_i += 1
        hT = B.hT[ti % 2]
        hkey = ('hT', ti % 2)
        m0 = 0 if f == 0 else 6
        rms_stats(xv, T, 1.0 / D, xkey, B)
        for kc in range(8):
            S.op('dve', lambda h, kc=kc: h.scalar_tensor_tensor(
                out=B.t[:, kc, :T], in0=xv[:, kc, :], scalar=modv(l, m0 + 1, kc, b), in1=B.rs[:, :T],
                op0=ALU.mult, op1=ALU.mult), reads=[xkey, 'rs'], writes=[('t', kc)])
            S.op('act', lambda h, kc=kc: h.activation(
                out=hT[:, kc, :T], in_=B.t[:, kc, :T], func=AF.Identity, bias=modv(l, m0, kc, b), scale=1.0),
                reads=[('t', kc)], writes=[hkey])
        gu_units = [wgu_s[f][l][:, j * 2048:(j + 1) * 2048].rearrange("p (k c) -> p k c", k=8) for j in range(NFC)]
        st_gu = Stream(S, 'wgu', B.wgu, gu_units)
        dn_units = [wdn_s[f][l][:, dc * NFC * 128:(dc + 1) * NFC * 128].rearrange("p (j c) -> p j c", j=NFC) for dc in range(8)]
        st_dn = Stream(S, 'wdn', B.wdn, dn_units)
        st_gu.start()
        for j in range(NFC):
            w, wkey = st_gu.get()
            pg, pu = ps[j % 2], ps[2 + j % 2]

            def mmg(h, w=w, pg=pg):
                ins = None
                for kc in range(8):
                    ins = h.matmul(pg[:, :T], lhsT=w[:, kc, 0:128], rhs=hT[:, kc, :T], start=(kc == 0), stop=(kc == 7))
                return ins

            def mmu(h, w=w, pu=pu):
                ins = None
                for kc in range(8):
                    ins = h.matmul(pu[:, :T], lhsT=w[:, kc, 128:256], rhs=hT[:, kc, :T], start=(kc == 0), stop=(kc == 7))
                return ins
            S.op('pe', mmg, reads=[wkey, hkey], writes=[('ps', j % 2)])
            S.op('pe', mmu, reads=[wkey, hkey], writes=[('ps', 2 + j % 2)])
            sg = B.sg[j % 2]
            S.op('act', lambda h, sg=sg, pg=pg: h.activation(out=sg[:, :T], in_=pg[:, :T], func=AF.Silu),
                 reads=[('ps', j % 2)], writes=[('sg', j % 2)])
            S.op('dve', lambda h, sg=sg, pu=pu, j=j: h.tensor_tensor(out=B.act[:, j, :T], in0=pu[:, :T], in1=sg[:, :T], op=ALU.mult),
                 reads=[('ps', 2 + j % 2), ('sg', j % 2)], writes=[('act', j)])
            if j == NFC - 6:
                st_dn.start()
        for dc in range(8):
            w, wkey = st_dn.get()
            pd = ps[4 + dc % 2]

            def mmd(h, w=w, pd=pd):
                ins = None
                for fc in range(NFC):
                    ins = h.matmul(pd[:, :T], lhsT=w[:, fc, :], rhs=B.act[:, fc, :T], start=(fc == 0), stop=(fc == NFC - 1))
                return ins
            S.op('pe', mmd, reads=[wkey] + [('act', j) for j in range(NFC)], writes=[('ps', 4 + dc % 2)])
            S.op('dve', lambda h, dc=dc, pd=pd: h.scalar_tensor_tensor(
                out=xv[:, dc, :], in0=pd[:, :T], scalar=modv(l, m0 + 2, dc, b), in1=xv[:, dc, :],
                op0=ALU.mult, op1=ALU.add), reads=[('ps', 4 + dc % 2), xkey], writes=[xkey])

    def load_tokens(src_rows, dstT, ntok, key):
        for tt in range(ntok // 128):
            st, sem = iob.stg[iob.i % 2]
            skey = ('iostg', iob.i % 2)
            iob.i += 1
            S.op('sp', lambda h, st=st, tt=tt: h.dma_start(out=st, in_=src_rows[tt * 128:(tt + 1) * 128, :]),
                 writes=[skey], dma=sem)
            for hf in range(2):
                bank = ps[hf]

                def tr(h, st=st, bank=bank, hf=hf):
                    ins = None
                    for q in range(4):
                        kc = hf * 4 + q
                        ins = h.transpose(out=bank[:, q * 128:(q + 1) * 128], in_=st[:, kc * 128:(kc + 1) * 128], identity=ident[:])
                    return ins
                S.op('pe', tr, reads=[skey, 'ident'], writes=[('ps', hf)])
                eng = 'act' if hf == 0 else 'dve'
                dst = dstT[:, hf * 4:(hf + 1) * 4, tt * 128:(tt + 1) * 128]
                srcv = bank[:, :].rearrange("p (q t) -> p q t", q=4)
                if eng == 'act':
                    S.op('act', lambda h, dst=dst, srcv=srcv: h.copy(out=dst, in_=srcv), reads=[('ps', hf)], writes=[key])
                else:
                    S.op('dve', lambda h, dst=dst, srcv=srcv: h.tensor_copy(out=dst, in_=srcv), reads=[('ps', hf)], writes=[key])

    def store_out(b):
        B = ffnb
        for tq in range(SEQ // 512):
            xv = xT[:, :, tq * 512:(tq + 1) * 512]
            if final_norm:
                rms_stats(xv, 512, 1.0 / D, 'xT', B)
                for kc in range(8):
                    S.op('dve', lambda h, kc=kc, xv=xv: h.scalar_tensor_tensor(
                        out=B.t[:, kc, :], in0=xv[:, kc, :], scalar=fgain[:, kc:kc + 1], in1=B.rs[:, :],
                        op0=ALU.mult, op1=ALU.mult), reads=['xT', 'rs', 'fgain'], writes=[('t', kc)])
                srcT = B.t
                skeys = [('t', kc) for kc in range(8)]
            else:
                srcT = xv
                skeys = ['xT']
            for tt in range(4):
                st, sem = iob.stg[iob.i % 2]
                skey = ('iostg', iob.i % 2)
                iob.i += 1
                for hf in range(2):
                    bank = ps[hf]

                    def tr(h, bank=bank, hf=hf, tt=tt, srcT=srcT):
                        ins = None
                        for q in range(4):
                            kc = hf * 4 + q
                            ins = h.transpose(out=bank[:, q * 128:(q + 1) * 128], in_=srcT[:, kc, tt * 128:(tt + 1) * 128], identity=ident[:])
                        return ins
                    S.op('pe', tr, reads=skeys + ['ident'], writes=[('ps', hf)])
                    dst = st[:, hf * 512:(hf + 1) * 512]
                    if hf == 0:
                        S.op('act', lambda h, dst=dst, bank=bank: h.copy(out=dst, in_=bank[:, :]), reads=[('ps', hf)], writes=[(skey, hf)])
                    else:
                        S.op('dve', lambda h, dst=dst, bank=bank: h.tensor_copy(out=dst, in_=bank[:, :]), reads=[('ps', hf)], writes=[(skey, hf)])
                r0 = tq * 512 + tt * 128
                S.op('sp', lambda h, st=st, r0=r0: h.dma_start(out=out_d[b, r0:r0 + 128, :], in_=st),
                     reads=[(skey, 0), (skey, 1)], writes=[skey], dma=sem)


    def prologue_mix():
        A = Arena()
        st32 = [A.alloc([2048], F32) for _ in range(2)]
        stsem = [S.dma_sem(f"pm{i}") for i in range(2)]
        big = A.alloc([10 * 8 * 128], BF16)
        bigsem = S.dma_sem("pmbig")
        bfr = [A.alloc([2048], BF16) for _ in range(2)]
        bfsem = [S.dma_sem(f"pmbf{i}") for i in range(2)]
        bd32 = A.alloc([2, 128], F32)
        pw32 = A.alloc([2, 128], F32)
        abt = A.alloc([2, 512], BF16)
        sm = S.dma_sem("pmmisc")
        state = {'c': 0}

        def stage_load(dmas):
            i = state['c'] % 2
            state['c'] += 1
            st = st32[i]
            key = ('pmst', i)
            if len(dmas) == 1:
                dv, src = dmas[0]
                S.op('sp', lambda h, d=dv(st), src=src: h.dma_start(out=d, in_=src), writes=[key], dma=stsem[i])
                return st, key
            S.op('dve', lambda h: h.nop(), reads=[], writes=[key])
            keys = []
            for j, (dv, src) in enumerate(dmas):
                S.op('sp', lambda h, d=dv(st), src=src: h.dma_start(out=d, in_=src), reads=[key], writes=[(key, j)], dma=stsem[i])
                keys.append((key, j))
            return st, keys

        def cp(eng, dst, src, reads, writes):
            if eng == 'act':
                S.op('act', lambda h: h.copy(out=dst, in_=src), reads=reads, writes=writes)
            else:
                S.op('dve', lambda h: h.tensor_copy(out=dst, in_=src), reads=reads, writes=writes)

        st, key = stage_load([(lambda st: st[:, 0:128], kperm_d[:, :])])
        cp('dve', perm_bf[:], st[:, 0:128], [key], ['perm_bf'])
        S.op('sp', lambda h: h.dma_start(out=swap_f[:], in_=kswap_d[:, :]), writes=['swap_f'], dma=sm)
        S.op('sp', lambda h: h.dma_start(out=invw[:], in_=kinvw_d[:, :]), writes=['invw'], dma=sm)
        S.op('sp', lambda h: h.dma_start(out=rcnt[:].rearrange("p a b -> p (a b)"), in_=krcnt_d[:, :]), writes=['rcnt'], dma=sm)
        S.op('sp', lambda h: h.dma_start(out=bd32, in_=kbd_d.rearrange("a p c -> p a c")), writes=['bd32'], dma=sm)
        S.op('dve', lambda h: h.memset(esel_bf[:].rearrange('p a b -> p (a b)'), 0.0), writes=['esel'])
        S.op('dve', lambda h: h.memset(esel_bf[:, 0, 0:64], 1.0), reads=['esel'], writes=['esel'])
        S.op('dve', lambda h: h.memset(esel_bf[:, 1, 64:128], 1.0), reads=['esel'], writes=['esel'])
        S.op('dve', lambda h: h.memset(bdones_bf[:], 0.0), writes=['bdones'])
        S.op('dve', lambda h: h.memset(bdones_bf[0:64, 0:64], 1.0), reads=['bdones'], writes=['bdones'])
        S.op('dve', lambda h: h.memset(bdones_bf[64:128, 64:128], 1.0), reads=['bdones'], writes=['bdones'])
        st, key = stage_load([(lambda st: st[:, 0:1024], kcdft_d[:, :])])
        cp('act', cdft_bf[:].rearrange("p a b c -> p (a b c)"), st[:, 0:1024], [key], ['cdft'])
        for hh in range(2):
            S.op('sp', lambda h, hh=hh: h.dma_start(out=qg[hh * 64:(hh + 1) * 64, :], in_=qg_d.rearrange("l e -> e l"),
                                                   allow_slow_non_contiguous=True), writes=[('qg', hh)], dma=sm)
            S.op('sp', lambda h, hh=hh: h.dma_start(out=kg[hh * 64:(hh + 1) * 64, :], in_=kg_d.rearrange("l e -> e l"),
                                                   allow_slow_non_contiguous=True), writes=[('kg', hh)], dma=sm)
        S.op('sp', lambda h: h.dma_start(out=pscale[:], in_=pools_d.rearrange("l (m p) -> p l m", p=128),
                                         allow_slow_non_contiguous=True), writes=['pscale'], dma=sm)
        S.barrier()
        for u in range(32 if pmdbg >= 2 else 0):
            st, key = stage_load([(lambda st: st, kdft_d[u])])
            bf = bfr[u % 2]
            bkey = ('pmbf', u % 2)
            cp('act' if u % 2 == 0 else 'dve', bf, st, [key], [bkey])
            S.op('sp', lambda h, bf=bf, u=u: h.dma_start(out=dft_s[u], in_=bf), reads=[bkey], writes=[bkey, 'dft_s'], dma=bfsem[u % 2])
        for l in range(depth if pmdbg >= 3 else 0):
            S.op('dve', lambda h: h.memset(pw32, 0.0), writes=['pw32'])
            for g in range(4):
                S.op('sp', lambda h, l=l, g=g: h.dma_start(
                    out=pw32[(g % 2) * 64:(g % 2 + 1) * 64, g // 2, (g % 2) * 64:(g % 2 + 1) * 64], in_=poolw_d[l, g]),
                    reads=['pw32'], writes=[('pw32d', g)], dma=sm)
            cp('dve', poolw_bf[:, l], pw32, ['pw32'] + [('pw32d', g) for g in range(4)], [('poolw', l), 'pw32'] + [('pw32d', g) for g in range(4)])
            for k2 in range(2 if pmdbg >= 4 else 0):
                st, key = stage_load([(lambda st: st[:, 0:256], fftw_d[l, k2 * 128:(k2 + 1) * 128, :])])

                def mm(h, st=st):
                    h.matmul(ps[0][:, 0:256], lhsT=bd32[:, 0, :], rhs=st[:, 0:256], start=True, stop=True)
                    return h.matmul(ps[0][:, 256:512], lhsT=bd32[:, 1, :], rhs=st[:, 0:256], start=True, stop=True)
                S.op('pe', mm, reads=[key, 'bd32'], writes=[('ps', 0)])
                cp('act', abt[:, k2, :], ps[0], [('ps', 0)], [('abt', k2)])
            if pmdbg >= 4:
                S.op('sp', lambda h, l=l: h.dma_start(out=ab_s[l], in_=abt.rearrange("p a b -> p (a b)")),
                     reads=[('abt', 0), ('abt', 1)], writes=[('abt', 0), ('abt', 1), ('ab_s', l)], dma=sm)
            if pmdbg < 5:
                continue
            bigv = big.rearrange("p (o k c) -> p o k c", o=10, k=8)
            for kc in range(8):
                qsrc = win_d[l, kc * 128:(kc + 1) * 128, 0:512].rearrange("p (t c e) -> p c t e", t=2, c=4)
                dl = [((lambda st, c=c: st[:, c * 128:(c + 1) * 128].rearrange("p (t e) -> p t e", t=2)), qsrc[:, c]) for c in range(4)]
                dl.append((lambda st: st[:, 512:INW], win_d[l, kc * 128:(kc + 1) * 128, 512:INW]))
                st, keys = stage_load(dl)
                cp('act' if kc % 2 == 0 else 'dve', bigv[:, :, kc, :], st[:, 0:INW].rearrange("p (o c) -> p o c", o=10),
                   keys, [('pmbig', kc % 2), ('pmst', (state['c'] - 1) % 2)])
            S.op('sp', lambda h, l=l: h.dma_start(out=win_s[l], in_=big), reads=[('pmbig', 0), ('pmbig', 1)],
                 writes=[('pmbig', 0), ('pmbig', 1), ('win_s', l)], dma=bigsem)
            if pmdbg < 6:
                continue
            bigo = big[:, 0:8 * 8 * 128].rearrange("p (d m c) -> p d m c", d=8, m=8)
            for mc in range(8):
                if mc < 4:
                    st, key = stage_load([(lambda st: st[0:64, 0:1024], wout_d[l, mc * 64:(mc + 1) * 64, :]),
                                          (lambda st: st[64:128, 0:1024], wout_d[l, (mc + 4) * 64:(mc + 5) * 64, :])])
                else:
                    st, key = stage_load([(lambda st: st[:, 0:1024], wout_d[l, mc * 128:(mc + 1) * 128, :])])
                klist = key if isinstance(key, list) else [key]
                cp('act' if mc % 2 == 0 else 'dve', bigo[:, :, mc, :], st[:, 0:1024].rearrange("p (d c) -> p d c", d=8),
                   klist, [('pmbig', mc % 2), ('pmst', (state['c'] - 1) % 2)])
            S.op('sp', lambda h, l=l: h.dma_start(out=wout_s[l], in_=big[:, 0:8 * 8 * 128]), reads=[('pmbig', 0), ('pmbig', 1)],
                 writes=[('pmbig', 0), ('pmbig', 1), ('wout_s', l)], dma=bigsem)
        S.barrier()

    class MixBufs:
        def __init__(self):
            A = Arena()
            self.qT = A.alloc([4, SEQ], BF16)
            self.qcT = A.alloc([4, CTX], BF16)
            self.kT = A.alloc([SEQ + CTX], BF16)
            self.Vx = A.alloc([18, 128], BF16)
            self.upT = A.alloc([2, SEQ + 16], F32)
            self.upc = A.alloc([2, CTX + 16], F32)
            self.uab = A.alloc([18, 512], BF16)
            base = A.off
            self.hT = A.alloc([8, 512], BF16)
            self.sq = A.alloc([8, 512], BF16)
            self.t = [A.alloc([512], F32) for _ in range(2)]
            self.rs = A.alloc([512], F32)
            self.sq2 = [A.alloc([512], BF16) for _ in range(2)]
            self.rq = [A.alloc([512], F32) for _ in range(2)]
            self.qn = [A.alloc([512], F32) for _ in range(2)]
            self.qnb = [A.alloc([512], BF16) for _ in range(2)]
            self.t1 = [A.alloc([512], F32) for _ in range(2)]
            self.qk_i = 0
            self.ufT = A.alloc([2, 512], BF16)
            self.rope = [(A.alloc([2, 512], F32), S.dma_sem(f"rope{i}")) for i in range(1)]
            self.win = [(A.alloc([8, 128], BF16), S.dma_sem(f"win{i}")) for i in range(2)]
            self.AB = A.alloc([2, 512], BF16)
            self.ABsem = S.dma_sem("ab")
            self.end1 = A.off
            A.off = base
            self.P = [A.alloc([2, 512], BF16) for _ in range(3)]
            self.rr = A.alloc([512], F32)
            self.rc = A.alloc([512], F32)
            self.mix = [A.alloc([8, 512], BF16) for _ in range(2)]
            self.a2 = A.alloc([544], F32)
            self.a4 = A.alloc([544], F32)
            self.a8 = A.alloc([544], F32)
            self.a16 = A.alloc([544], F32)
            self.pm = A.alloc([2, 512], BF16)
            self.rcf = A.alloc([2, 512], F32)
            self.rcfsem = S.dma_sem('rcf')
            self.dft = [(A.alloc([4, 512], BF16), S.dma_sem(f"dft{i}")) for i in range(2)]
            self.wo = [(A.alloc([8, 128], BF16), S.dma_sem(f"wo{i}")) for i in range(2)]
            self.end2 = A.off
            self.mix_i = 0
            self.p_i = 0

    def qk_norm_rope(M, pq, T, gain_ap, dst, cs, cskey, tag):
        par = M.qk_i % 2
        M.qk_i += 1
        sq2, rq, qn, qnb, t1 = M.sq2[par], M.rq[par], M.qn[par], M.qnb[par], M.t1[par]
        pstat, kstat = (ps[6], 'ps6') if par == 0 else (ps[4], ('ps', 4))
        pperm, kperm = (ps[7], 'ps7') if par == 0 else (ps[5], ('ps', 5))
        k = lambda n: (n, par)
        S.op('act', lambda h: h.activation(out=sq2[:, :T], in_=pq[:, :T], func=AF.Square), reads=[tag], writes=[k('sq2')])
        S.op('pe', lambda h: h.matmul(pstat[:, :T], lhsT=bdones_bf[:], rhs=sq2[:, :T], start=True, stop=True),
             reads=[k('sq2'), 'bdones'], writes=[kstat])
        S.op('act', lambda h: h.activation(out=rq[:, :T], in_=pstat[:, :T], func=AF.Sqrt, scale=1.0 / 64, bias=EPS),
             reads=[kstat], writes=[k('rq')])
        S.op('dve', lambda h: h.reciprocal(out=rq[:, :T], in_=rq[:, :T]), reads=[k('rq')], writes=[k('rq')])
        S.op('dve', lambda h: h.scalar_tensor_tensor(out=qn[:, :T], in0=pq[:, :T], scalar=gain_ap, in1=rq[:, :T],
                                                     op0=ALU.mult, op1=ALU.mult), reads=[tag, k('rq')], writes=[k('qn')])
        if cs is None:
            S.op('act', lambda h: h.copy(out=dst, in_=qn[:, :T]), reads=[k('qn')], writes=['qkdst'])
            return None
        S.op('act', lambda h: h.copy(out=qnb[:, :T], in_=qn[:, :T]), reads=[k('qn')], writes=[k('qnb')])
        S.op('pool', lambda h: h.tensor_tensor(out=t1[:, :T], in0=qn[:, :T], in1=cs[:, 0, :T], op=ALU.mult),
             reads=[k('qn'), cskey], writes=[k('t1')])

        def partB():
            S.op('pe', lambda h: h.matmul(pperm[:, :T], lhsT=perm_bf[:], rhs=qnb[:, :T], start=True, stop=True),
                 reads=[k('qnb'), 'perm_bf'], writes=[kperm])
            S.op('dve', lambda h: h.tensor_tensor(out=rq[:, :T], in0=pperm[:, :T], in1=cs[:, 1, :T], op=ALU.mult),
                 reads=[kperm, cskey, k('qn')], writes=[k('rq')])
            S.op('dve', lambda h: h.tensor_tensor(out=dst, in0=rq[:, :T], in1=t1[:, :T], op=ALU.add),
                 reads=[k('rq'), k('t1')], writes=['qkdst'])
            return None
        return partB

    def mixer_m1(M, l, b, last):
        S.op('sp', lambda h: h.dma_start(out=M.AB.rearrange("p a b -> p (a b)"), in_=ab_s[l]), reads=[('ab_s', l)], writes=['AB'], dma=M.ABsem)
        S.op('dve', lambda h: h.memset(M.upT.rearrange('p a b -> p (a b)'), 0.0), writes=['upT'])
        S.op('dve', lambda h: h.memset(M.upc.rearrange('p a b -> p (a b)'), 0.0), writes=['upc'])
        tiles = [(xT[:, :, tq * 512:(tq + 1) * 512], 512, b, 'xT', tq) for tq in range(4)] + [(cT[:, :, :], CTX, 4, 'cT', 4)]
        rope_i = 0
        t_i = 0
        for (xv, T, bb, xkey, tq) in tiles:
            is_ctx = (tq == 4)
            rms_stats(xv, T, 1.0 / D, xkey, M)
            for kc in range(8):
                tt = M.t[t_i % 2]
                tk = ('mt', t_i % 2)
                t_i += 1
                S.op('dve', lambda h, kc=kc, tt=tt, xv=xv, T=T, bb=bb: h.scalar_tensor_tensor(
                    out=tt[:, :T], in0=xv[:, kc, :], scalar=modv(l, 4, kc, bb), in1=M.rs[:, :T],
                    op0=ALU.mult, op1=ALU.mult), reads=[xkey, 'rs'], writes=[tk])
                S.op('act', lambda h, kc=kc, tt=tt, T=T, bb=bb: h.activation(
                    out=M.hT[:, kc, :T], in_=tt[:, :T], func=AF.Identity, bias=modv(l, 3, kc, bb), scale=1.0),
                    reads=[tk], writes=['mhT'])
            if not is_ctx:
                cs, csem = M.rope[0]
                cskey = ('rope', 0)
                rope_i += 1
                S.op('sp', lambda h, cs=cs, tq=tq: h.dma_start(out=cs, in_=krope_d[:, :, tq * 512:(tq + 1) * 512].rearrange("a p t -> p a t")),
                     writes=[cskey], dma=csem)
            else:
                cs, cskey = None, None
            if is_ctx and last:
                chunks = [4, 5]
            else:
                chunks = list(range(10))
            units = [win_s[l][:, oc * 1024:(oc + 1) * 1024].rearrange("p (k c) -> p k c", k=8) for oc in chunks]
            st = Stream(S, 'win', M.win, units)
            pipe = {'A': None, 'B': None}

            def step(newA):
                if pipe['B'] is not None:
                    pipe['B']()
                    pipe['B'] = None
                if pipe['A'] is not None:
                    pipe['B'] = pipe['A']()
                pipe['A'] = newA
            for oc in chunks:
                w, wkey = st.get()
                if oc == 5:
                    for tc in range(T // 128):
                        ch = (tq * 4 + tc) if not is_ctx else 16 + tc

                        def mmv(h, w=w, tc=tc):
                            ins = None
                            for kc in range(8):
                                ins = h.matmul(ps[1][:, 0:128], lhsT=M.hT[:, kc, tc * 128:(tc + 1) * 128], rhs=w[:, kc, :],
                                               start=(kc == 0), stop=(kc == 7))
                            return ins
                        S.op('pe', mmv, reads=[wkey, 'mhT'], writes=[('ps', 1)])
                        S.op('act', lambda h, ch=ch: h.activation(out=M.Vx[:, ch, :], in_=ps[1][:, 0:128], func=AF.Identity), reads=[('ps', 1)], writes=[('Vxa', ch)])
                    step(None)
                    continue
                pq = ps[0] if oc % 2 == 0 else ps[2]
                ptag = ('ps', 0) if oc % 2 == 0 else ('ps', 2)

                def mmq(h, w=w, pq=pq, T=T):
                    ins = None
                    for kc in range(8):
                        ins = h.matmul(pq[:, :T], lhsT=w[:, kc, :], rhs=M.hT[:, kc, :T], start=(kc == 0), stop=(kc == 7))
                    return ins
                S.op('pe', mmq, reads=[wkey, 'mhT'], writes=[ptag])
                if oc < 4:
                    dst = M.qT[:, oc, tq * 512:(tq + 1) * 512] if not is_ctx else M.qcT[:, oc, :]
                    step(lambda pq=pq, T=T, dst=dst, cs=cs, cskey=cskey, ptag=ptag: qk_norm_rope(M, pq, T, qg[:, l:l + 1], dst, cs, cskey, ptag))
                elif oc == 4:
                    dst = M.kT[:, tq * 512:(tq + 1) * 512] if not is_ctx else M.kT[:, SEQ:SEQ + CTX]
                    step(lambda pq=pq, T=T, dst=dst, cs=cs, cskey=cskey, ptag=ptag: qk_norm_rope(M, pq, T, kg[:, l:l + 1], dst, cs, cskey, ptag))
                elif oc in (6, 7):
                    m = oc - 6
                    dst = M.upT[:, m, 8 + tq * 512: 8 + (tq + 1) * 512] if not is_ctx else M.upc[:, m, 8:8 + CTX]
                    S.op('act', lambda h, dst=dst, pq=pq, T=T: h.copy(out=dst, in_=pq[:, :T]), reads=[ptag, 'upT', 'upc'], writes=[('up', tq, m)])
                    step(None)
                else:
                    m = oc - 8
                    S.op('act', lambda h, m=m, pq=pq, T=T: h.copy(out=M.ufT[:, m, :T], in_=pq[:, :T]), reads=[ptag], writes=[('ufT', m)])
                    if m == 0:
                        step(None)
                    if m == 1:
                        def abpart(T=T, tq=tq, is_ctx=is_ctx):
                            for tc in range(T // 128):
                                ch = (tq * 4 + tc) if not is_ctx else 16 + tc

                                def mmab(h, tc=tc):
                                    h.matmul(ps[3][:, :], lhsT=M.ufT[:, 0, tc * 128:(tc + 1) * 128], rhs=M.AB[:, 0, :], start=True, stop=False)
                                    return h.matmul(ps[3][:, :], lhsT=M.ufT[:, 1, tc * 128:(tc + 1) * 128], rhs=M.AB[:, 1, :], start=False, stop=True)
                                S.op('pe', mmab, reads=[('ufT', 0), ('ufT', 1), 'AB'], writes=[('ps', 3)])
                                S.op('dve', lambda h, ch=ch: h.tensor_copy(out=M.uab[:, ch, :], in_=ps[3][:, :]), reads=[('ps', 3)], writes=[('uab', ch)])
                            return None
                        step(abpart)
            step(None)
            step(None)
        S.barrier()

    def pool_tile(M, l, up, T, t0, L, mixv):
        W = T + 16
        edge = (t0 == 0) or (t0 + T == L)
        if edge:
            which = 2 if L == CTX else (0 if t0 == 0 else 1)
            S.op('sp', lambda h: h.dma_start(out=M.rcf.rearrange("p a b -> p (a b)"), in_=krcf_d[which]), writes=['rcf'], dma=M.rcfsem)
        for m in range(2):
            u = up[:, m, t0:t0 + W]
            S.op('pool', lambda h, u=u: h.tensor_tensor(out=M.a2[:, 0:W - 1], in0=u[:, 0:W - 1], in1=u[:, 1:W], op=ALU.add),
                 reads=[('up', 'all')], writes=['a2'])
            S.op('pool', lambda h: h.tensor_tensor(out=M.a4[:, 0:W - 3], in0=M.a2[:, 0:W - 3], in1=M.a2[:, 2:W - 1], op=ALU.add),
                 reads=['a2'], writes=['a4'])
            if m == 1:
                S.op('pool', lambda h: h.tensor_tensor(out=M.a8[:, 0:W - 7], in0=M.a4[:, 0:W - 7], in1=M.a4[:, 4:W - 3], op=ALU.add),
                     reads=['a4'], writes=['a8'])
                S.op('pool', lambda h: h.tensor_tensor(out=M.a16[:, 0:W - 15], in0=M.a8[:, 0:W - 15], in1=M.a8[:, 8:W - 7], op=ALU.add),
                     reads=['a8'], writes=['a16'])
                srcs = [(M.a8, 4, 'a8'), (M.a16, 8, 'a16')]
            else:
                srcs = [(M.a2, 1, 'a2'), (M.a4, 2, 'a4')]
            for hh in range(2):
                a, hw, akey = srcs[hh]
                lo, hi = hh * 64, (hh + 1) * 64
                if not edge:
                    S.op('dve', lambda h, a=a, hw=hw, lo=lo, hi=hi, m=m, u=u: h.scalar_tensor_tensor(
                        out=M.pm[lo:hi, m, :T], in0=a[lo:hi, 8 - hw:8 - hw + T], scalar=invw[lo:hi, m:m + 1], in1=u[lo:hi, 8:8 + T],
                        op0=ALU.mult, op1=ALU.subtract), reads=[akey, ('up', 'all'), 'invw'], writes=[('pm', m, hh)])
                else:
                    S.op('dve', lambda h, a=a, hw=hw, lo=lo, hi=hi, m=m: h.tensor_tensor(
                        out=M.rr[lo:hi, :T], in0=a[lo:hi, 8 - hw:8 - hw + T], in1=M.rcf[lo:hi, m, :T], op=ALU.mult),
                        reads=[akey, 'rcf'], writes=[('rr', hh)])
                    S.op('dve', lambda h, lo=lo, hi=hi, m=m, u=u: h.tensor_tensor(
                        out=M.pm[lo:hi, m, :T], in0=M.rr[lo:hi, :T], in1=u[lo:hi, 8:8 + T], op=ALU.subtract),
                        reads=[('rr', hh), ('up', 'all')], writes=[('pm', m, hh)])
            S.op('pe', lambda h, m=m: h.matmul(ps[7][:, :T], lhsT=poolw_bf[:, l, m, :], rhs=M.pm[:, m, :T], start=True, stop=True),
                 reads=[('pm', m, 0), ('pm', m, 1), ('poolw', l)], writes=['ps7'])
            S.op('act', lambda h, m=m: h.activation(out=mixv[:, 4 + m, :T], in_=ps[7][:, :T], func=AF.Identity, scale=pscale[:, l, m:m + 1]),
                 reads=['ps7', 'pscale'], writes=[('mixp', m)])

    def fft_tile(M, l, tq, T, is_ctx, mixv):
        if not is_ctx:
            units = [dft_s[tq * 8 + g].rearrange("p (a c) -> p a c", a=4) for g in range(8)]
            st = Stream(S, 'dft', M.dft, units)
            for g in range(8):
                w, wkey = st.get()
                cs_, lcg = g // 4, g % 4

                def mm(h, w=w, g=g, cs_=cs_, lcg=lcg):
                    ins = None
                    for lc4 in range(4):
                        lc = lcg * 4 + lc4
                        for m in range(2):
                            ins = h.matmul(ps[m][:, :T], lhsT=M.uab[:, lc, cs_ * 256 + m * 128: cs_ * 256 + (m + 1) * 128],
                                           rhs=w[:, lc4, :], start=(g == 0 and lc4 == 0), stop=(g == 7 and lc4 == 3))
                    return ins
                S.op('pe', mm, reads=[wkey] + [('uab', lc) for lc in range(16)], writes=[('ps', 0), ('ps', 1)])
        else:
            def mm(h):
                ins = None
                n = 0
                for cs_ in range(2):
                    for lc in range(2):
                        for m in range(2):
                            ins = h.matmul(ps[m][:, :T], lhsT=M.uab[:, 16 + lc, cs_ * 256 + m * 128: cs_ * 256 + (m + 1) * 128],
                                           rhs=cdft_bf[:, cs_, lc, :], start=(n == 0), stop=(n == 3))
                        n += 1
                return ins
            S.op('pe', mm, reads=['cdft', ('uab', 16), ('uab', 17)], writes=[('ps', 0), ('ps', 1)])
        S.op('act', lambda h: h.copy(out=mixv[:, 6, :T], in_=ps[0][:, :T]), reads=[('ps', 0)], writes=[('mixf', 0)])
        S.op('dve', lambda h: h.tensor_copy(out=mixv[:, 7, :T], in_=ps[1][:, :T]), reads=[('ps', 1)], writes=[('mixf', 1)])

    def attn_tile(M, qsrc, T, kchunks, mixv):
        nk = len(kchunks)

        def rec_S(c, i, ch):
            sbank = pp[(M.p_i) % 2]
            skeys = [('ps', 2 * (M.p_i % 2)), ('ps', 2 * (M.p_i % 2) + 1)]
            P = M.P[M.p_i % 3]
            pkey = ('P', M.p_i % 3)
            M.p_i += 1

            def mms(h):
                h.matmul(sbank[:, 0:T], lhsT=M.kT[0:64, ch * 128:(ch + 1) * 128], rhs=qsrc[0:64, c, :], start=True, stop=True)
                return h.matmul(sbank[:, 512:512 + T], lhsT=M.kT[64:128, ch * 128:(ch + 1) * 128], rhs=qsrc[64:128, c, :], start=True, stop=True)
            S.op('pe', mms, reads=['qk'], writes=skeys)
            if T == 512:
                S.op('act', lambda h: h.activation(out=P.rearrange("p a b -> p (a b)"), in_=sbank[:, :], func=AF.Exp, scale=0.125),
                     reads=skeys, writes=[pkey])
            else:
                S.op('act', lambda h: h.activation(out=P[:, :, :T], in_=sbank.rearrange("p (a b) -> p a b", a=2)[:, :, :T], func=AF.Exp, scale=0.125),
                     reads=skeys, writes=[pkey])
            return P, pkey

        def rec_PV(c, i, ch, P, pkey):
            def mmpv(h):
                h.matmul(ps[4][:, :T], lhsT=M.Vx[:, ch, :], rhs=P[:, 0, :T], start=(i == 0), stop=(i == nk - 1))
                h.matmul(ps[5][:, :T], lhsT=M.Vx[:, ch, :], rhs=P[:, 1, :T], start=(i == 0), stop=(i == nk - 1))
                h.matmul(ps[6][:, :T], lhsT=esel_bf[:, 0, :], rhs=P[:, 0, :T], start=(i == 0), stop=False)
                return h.matmul(ps[6][:, :T], lhsT=esel_bf[:, 1, :], rhs=P[:, 1, :T], start=False, stop=(i == nk - 1))
            S.op('pe', mmpv, reads=[pkey, 'Vxall', 'esel'], writes=[('ps', 4), ('ps', 5), 'ps6'])
            if i == nk - 1:
                S.op('dve', lambda h: h.reciprocal(out=M.rc[:, :T], in_=ps[6][:, :T]), reads=['ps6'], writes=['rc'])
                S.op('dve', lambda h: h.tensor_tensor(out=mixv[0:64, c, :T], in0=ps[4][0:64, :T], in1=M.rc[0:64, :T], op=ALU.mult),
                     reads=[('ps', 4), 'rc'], writes=[('mixa', c, 0)])
                S.op('dve', lambda h: h.tensor_tensor(out=mixv[64:128, c, :T], in0=ps[5][64:128, :T], in1=M.rc[64:128, :T], op=ALU.mult),
                     reads=[('ps', 5), 'rc'], writes=[('mixa', c, 1)])

        items = [(c, i, ch) for c in range(4) for i, ch in enumerate(kchunks)]
        prev = None
        for it in items:
            P, pkey = rec_S(*it)
            if prev is not None:
                rec_PV(*prev)
            prev = it + (P, pkey)
        rec_PV(*prev)

    def wout_tile(M, l, b, xv, T, xkey, mixv, mixkeys):
        units = [wout_s[l][:, dc * 1024:(dc + 1) * 1024].rearrange("p (m c) -> p m c", m=8) for dc in range(8)]
        st = Stream(S, 'wo', M.wo, units)
        for dc in range(8):
            w, wkey = st.get()
            pd = ps[dc % 2]

            def mm(h, w=w, pd=pd):
                ins = None
                for mc in range(8):
                    ins = h.matmul(pd[:, :T], lhsT=w[:, mc, :], rhs=mixv[:, mc, :T], start=(mc == 0), stop=(mc == 7))
                return ins
            S.op('pe', mm, reads=[wkey] + mixkeys, writes=[('ps', dc % 2)])
            S.op('dve', lambda h, dc=dc, pd=pd: h.scalar_tensor_tensor(
                out=xv[:, dc, :], in0=pd[:, :T], scalar=modv(l, 5, dc, b), in1=xv[:, dc, :],
                op0=ALU.mult, op1=ALU.add), reads=[('ps', dc % 2), xkey], writes=[xkey])

    mixb = []

    def mixer(l, b, last):
        if not mixb:
            mixb.append(MixBufs())
        M = mixb[0]
        if mixdbg < 2:
            return
        mixer_m1(M, l, b, last)
        if mixdbg < 3:
            return
        mixkeys = [('mixa', c, hh) for c in range(4) for hh in range(2)] + [('mixp', 0), ('mixp', 1), ('mixf', 0), ('mixf', 1)]
        tiles = [(tq, 512, False) for tq in range(4)]
        if not last:
            tiles.append((0, CTX, True))
        for (tq, T, is_ctx) in tiles:
            mixv = M.mix[M.mix_i % 2]
            M.mix_i += 1
            if not is_ctx:
                if mixdbg >= 3:
                    pool_tile(M, l, M.upT, 512, tq * 512, SEQ, mixv)
                if mixdbg >= 4:
                    fft_tile(M, l, tq, 512, False, mixv)
                if mixdbg >= 5:
                    attn_tile(M, M.qT[:, :, tq * 512:(tq + 1) * 512], 512, list(range(18)), mixv)
                if mixdbg >= 6:
                    wout_tile(M, l, b, xT[:, :, tq * 512:(tq + 1) * 512], 512, 'xT', mixv, mixkeys)
            else:
                if mixdbg >= 3:
                    pool_tile(M, l, M.upc, CTX, 0, CTX, mixv)
                if mixdbg >= 4:
                    fft_tile(M, l, 0, CTX, True, mixv)
                if mixdbg >= 5:
                    attn_tile(M, M.qcT[:, :, :], CTX, [16, 17], mixv)
                if mixdbg >= 6:
                    wout_tile(M, l, 4, cT[:, :, :], CTX, 'cT', mixv, mixkeys)
        S.barrier()

    if stages != 'io':
        prologue_ffn()
        prologue_mod()
        if stages != 'ffn1':
            prologue_mix()
    ffnb = FFNBufs()
    iob = IOBufs(ffnb.end)
    for b in range(nb):
        load_tokens(x_d[b], xT, SEQ, 'xT')
        load_tokens(ctx_d[b], cT, CTX, 'cT')
        if stages != 'io':
            for l in range(depth):
                last = (l == DEPTH - 1)
                for tq in range(4):
                    ffn_tile(l, 0, xT[:, :, tq * 512:(tq + 1) * 512], 512, b, 'xT')
                ffn_tile(l, 0, cT[:, :, :], CTX, 4, 'cT')
                if stages == 'ffn1':
                    continue
                S.barrier()
                mixer(l, b, last)
                if stages == 'mix':
                    continue
                for tq in range(4):
                    ffn_tile(l, 1, xT[:, :, tq * 512:(tq + 1) * 512], 512, b, 'xT')
                if not last:
                    ffn_tile(l, 1, cT[:, :, :], CTX, 4, 'cT')
        store_out(b)
        S.barrier()
        S.new_epoch()
    S.barrier()
    S.op('sp', lambda h: h.nop(), reads=[], writes=[])

    with nc.Block() as block:
        S.emit(block)
    return nc


def make_consts():
    c = {"k_ident": np.eye(128, dtype=np.float32)}
    l = np.arange(2048, dtype=np.int64)
    mm_ = (l[:, None] * l[None, :]) % 2048
    ang = 2.0 * np.pi * mm_.astype(np.float64) / 2048.0
    nrm = 1.0 / np.sqrt(2048.0)
    mats = [np.cos(ang) * nrm, np.sin(ang) * nrm]
    dft = np.zeros((4, 2, 4, 128, 4, 512), np.float32)
    for cs in range(2):
        m = mats[cs].reshape(4, 4, 128, 4, 512)
        dft[:, cs] = m.transpose(3, 0, 2, 1, 4)
    c["k_dft"] = dft.reshape(32, 128, 2048)
    lc_ = np.arange(256, dtype=np.int64)
    angc = 2.0 * np.pi * ((lc_[:, None] * lc_[None, :]) % 256).astype(np.float64) / 256.0
    cm = [np.cos(angc) / 16.0, np.sin(angc) / 16.0]
    cd = np.zeros((128, 2, 2, 256), np.float32)
    for cs in range(2):
        cd[:, cs] = cm[cs].reshape(2, 128, 256).transpose(1, 0, 2)
    c["k_cdft"] = cd.reshape(128, 1024)
    cc = np.arange(64, dtype=np.int64)
    a64 = 2.0 * np.pi * ((cc[:, None] * cc[None, :]) % 64).astype(np.float64) / 64.0
    bd = np.zeros((2, 128, 128), np.float32)
    for o in (0, 64):
        bd[0, o:o + 64, o:o + 64] = np.cos(a64) / 8.0
        bd[1, o:o + 64, o:o + 64] = -np.sin(a64) / 8.0
    c["k_bd"] = bd
    t = np.arange(SEQ)
    row = (t // 64).astype(np.float32)
    col = (t % 64).astype(np.float32)
    inv = (np.float32(10000.0) ** (-(np.arange(16, dtype=np.float32)) / np.float32(16))).astype(np.float32)
    angr = np.concatenate([row[:, None] * inv[None, :], col[:, None] * inv[None, :]], axis=-1).astype(np.float32)
    cosr = np.cos(angr).astype(np.float32).T
    sinr = np.sin(angr).astype(np.float32).T
    rope = np.zeros((2, 128, SEQ), np.float32)
    for p in range(128):
        rope[0, p] = cosr[(p % 64) % 32]
        rope[1, p] = sinr[(p % 64) % 32]
    c["k_rope"] = rope
    perm = np.zeros((128, 128), np.float32)
    for o in (0, 64):
        for m in range(64):
            if m < 32:
                perm[o + m + 32, o + m] = -1.0
            else:
                perm[o + m - 32, o + m] = 1.0
    c["k_perm"] = perm
    sw = np.zeros((128, 128), np.float32)
    for m in range(128):
        sw[(m + 64) % 128, m] = 1.0
    c["k_swap"] = sw
    ws = [2, 4, 8, 16]
    invw = np.zeros((128, 2), np.float32)
    rc = np.zeros((128, 2, 16), np.float32)
    for p in range(128):
        for m in range(2):
            w = ws[2 * m + p // 64]
            invw[p, m] = 1.0 / w
            for i in range(8):
                tt = i
                rc[p, m, i] = 1.0 / ((tt + w // 2) - max(tt - w // 2, 0))
                dd = 8 - i
                rc[p, m, 8 + i] = 1.0 / (min(w // 2, dd) + w // 2)
    c["k_invw"] = invw
    rcf = np.zeros((3, 128, 2, 512), np.float32)
    for p in range(128):
        for m in range(2):
            w = ws[2 * m + p // 64]
            for which, (Lx, t0) in enumerate(((SEQ, 0), (SEQ, SEQ - 512), (CTX, 0))):
                for i in range(512):
                    tt = t0 + i
                    if tt >= Lx:
                        rcf[which, p, m, i] = 1.0 / w
                    else:
                        rcf[which, p, m, i] = 1.0 / (min(tt + w // 2, Lx) - max(tt - w // 2, 0))
    c["k_rcf"] = rcf.reshape(3, 128, 1024)
    c["k_rcnt"] = rc.reshape(128, 32)
    return c


def kernel(**inputs):
    cfg = inputs.pop('_cfg', {})
    nb = cfg.get('nb', NB_CORE)
    ncores = cfg.get('ncores', NCORES)
    nc = build_nc(cfg)
    consts = make_consts()
    in_maps = []
    for i in range(ncores):
        m = {}
        for k, v in inputs.items():
            v = np.asarray(v)
            if k in ('x', 'c', 'ctx'):
                v = np.ascontiguousarray(v[i * nb:(i + 1) * nb])
            m[k] = v
        m.update(consts)
        in_maps.append(m)
    res = run_bass_kernel_spmd(nc, in_maps, core_ids=list(range(ncores)))
    return np.concatenate([np.asarray(r["out"]) for r in res.results], axis=0)
```

```python
import numpy as np
import concourse.bass as bass
import concourse.mybir as mybir
from concourse.bass_utils import run_bass_kernel_spmd

F32 = mybir.dt.float32
BF16 = mybir.dt.bfloat16
AF = mybir.ActivationFunctionType
ALU = mybir.AluOpType

D = 1024
SEQ = 2048
CTX = 256
DEPTH = 4
DFF = 2816
NFC = 22
INW = 1280
EPS = 1e-6
NB_CORE = 4
NCORES = 8
ENGS = ('pe', 'act', 'dve', 'pool', 'sp')


class _Op:
    __slots__ = ('id', 'eng', 'fn', 'deps', 'signal', 'sem', 'val', 'dma', 'epoch')


class _Res:
    __slots__ = ('w', 'r', 'rd')

    def __init__(self):
        self.w = None
        self.r = {}
        self.rd = []


class DmaSem:
    def __init__(self, sem):
        self.sem = sem
        self.count = 0


class Sched:
    def __init__(self, nc):
        self.nc = nc
        self.ops = {e: [] for e in ENGS}
        self.all = []
        self.res = {}
        self.epoch = 0
        self.pending_barrier = {e: set() for e in ENGS}
        self.dma_since_barrier = []
        self.engsem = {}
        self.dmasems = []

    def dma_sem(self, name):
        s = DmaSem(self.nc.alloc_semaphore(name))
        self.dmasems.append(s)
        return s

    def new_epoch(self):
        self.epoch += 1

    def op(self, eng, fn, reads=(), writes=(), dma=None):
        o = _Op()
        o.id = len(self.all)
        o.eng = eng
        o.fn = fn
        o.signal = False
        o.dma = dma
        o.epoch = self.epoch
        o.sem = None
        o.val = None
        def _flat(ks):
            out = []
            for k_ in ks:
                if isinstance(k_, list):
                    out.extend(k_)
                else:
                    out.append(k_)
            return out
        reads = _flat(reads)
        writes = _flat(writes)
        deps = set(self.pending_barrier[eng])
        self.pending_barrier[eng] = set()
        for r in reads:
            st = self.res.get(r)
            if st is not None and st.w is not None:
                deps.add(st.w)
        for w in writes:
            st = self.res.get(w)
            if st is not None:
                if st.w is not None:
                    deps.add(st.w)
                deps.update(st.r.values())
                deps.update(st.rd)
        for r in reads:
            st = self.res.setdefault(r, _Res())
            if dma is not None:
                st.rd.append(o.id)
            else:
                st.r[eng] = o.id
        for w in writes:
            st = self.res.setdefault(w, _Res())
            st.w = o.id
            st.r = {}
            st.rd = []
        best = {}
        keep = set()
        for d in deps:
            p = self.all[d]
            if p.dma is not None:
                keep.add(d)
            else:
                k = (p.eng, p.epoch)
                if k not in best or best[k] < d:
                    best[k] = d
        keep.update(best.values())
        o.deps = keep
        if dma is not None:
            dma.count += 16
            o.sem = dma.sem
            o.val = dma.count
            self.dma_since_barrier.append(o.id)
        self.all.append(o)
        self.ops[eng].append(o)
        return o

    def barrier(self):
        last = set()
        for e in ENGS:
            for o in reversed(self.ops[e]):
                if o.dma is None:
                    last.add(o.id)
                    break
        last.update(self.dma_since_barrier)
        self.dma_since_barrier = []
        for e in ENGS:
            self.pending_barrier[e].update(last)

    def emit(self, block):
        nc = self.nc
        for o in self.all:
            for d in o.deps:
                self.all[d].signal = True
        nep = self.epoch + 1
        for e in ENGS:
            if e == 'sp':
                continue
            self.engsem[e] = [nc.alloc_semaphore(f"s_{e}_{i}") for i in range(nep)]
        for e in ENGS:
            cnt = {}
            for o in self.ops[e]:
                if o.dma is None and o.signal:
                    if e == 'sp':
                        raise RuntimeError("non-dma op on sp cannot signal")
                    cnt[o.epoch] = cnt.get(o.epoch, 0) + 1
                    o.val = cnt[o.epoch]
                    o.sem = self.engsem[e][o.epoch]
        allops = self.all

        def run(name, h):
            waited = {}
            for o in self.ops[name]:
                for d in sorted(o.deps):
                    p = allops[d]
                    key = id(p.sem)
                    if waited.get(key, 0) >= p.val:
                        continue
                    h.wait_ge(p.sem, p.val)
                    waited[key] = p.val
                ins = o.fn(h)
                if o.dma is not None:
                    ins.then_inc(o.sem, 16)
                elif o.signal:
                    ins.then_inc(o.sem, 1)

        @block.tensor
        def _(h):
            run('pe', h)

        @block.scalar
        def _(h):
            run('act', h)

        @block.vector
        def _(h):
            run('dve', h)

        @block.gpsimd
        def _(h):
            run('pool', h)

        @block.sync
        def _(h):
            run('sp', h)


class Stream:
    def __init__(self, S, name, slots, units):
        self.S = S
        self.name = name
        self.slots = slots
        self.units = units
        self.issued = 0
        self.got = 0

    def _issue(self):
        i = self.issued
        ap, sem = self.slots[i % len(self.slots)]
        src = self.units[i]
        self.S.op('sp', lambda h, ap=ap, src=src: h.dma_start(out=ap, in_=src),
                  writes=[(self.name, i % len(self.slots))], dma=sem)
        self.issued += 1

    def start(self):
        while self.issued < min(len(self.units), len(self.slots)):
            self._issue()

    def get(self):
        i = self.got
        while self.issued < min(len(self.units), i + len(self.slots)):
            self._issue()
        self.got += 1
        return self.slots[i % len(self.slots)][0], (self.name, i % len(self.slots))


def build_nc(cfg):
    nb = cfg.get('nb', NB_CORE)
    depth = cfg.get('depth', DEPTH)
    stages = cfg.get('stages', 'full')
    final_norm = cfg.get('final_norm', True)
    mixdbg = cfg.get('mixdbg', 9)
    pmdbg = cfg.get('pmdbg', 9)

    nc = bass.Bass("TRN2", target_bir_lowering=False)
    S = Sched(nc)

    def dram_in(name, shape, dt=F32):
        return nc.dram_tensor(name, list(shape), dt, kind="ExternalInput").ap()

    x_d = dram_in("x", [nb, SEQ, D])
    c_d = dram_in("c", [nb, D])
    ctx_d = dram_in("ctx", [nb, CTX, D])
    cctx_d = dram_in("c_ctx", [D])
    adaw_d = dram_in("ada_w", [DEPTH, D, 9 * D])
    adab_d = dram_in("ada_b", [DEPTH, 9 * D])
    wgu_d = [dram_in("ffn1_w_gu", [DEPTH, D, 2 * DFF]), dram_in("ffn2_w_gu", [DEPTH, D, 2 * DFF])]
    wdn_d = [dram_in("ffn1_w_down", [DEPTH, DFF, D]), dram_in("ffn2_w_down", [DEPTH, DFF, D])]
    win_d = dram_in("w_in", [DEPTH, D, INW])
    qg_d = dram_in("q_gain", [DEPTH, 64])
    kg_d = dram_in("k_gain", [DEPTH, 64])
    poolw_d = dram_in("pool_w", [DEPTH, 4, 64, 64])
    pools_d = dram_in("pool_scale", [DEPTH, 256])
    fftw_d = dram_in("fft_w", [DEPTH, 256, 256])
    wout_d = dram_in("w_out", [DEPTH, D, D])
    fg_d = dram_in("final_gain", [D])
    ident_d = dram_in("k_ident", [128, 128])
    kdft_d = dram_in("k_dft", [32, 128, 2048])
    kcdft_d = dram_in("k_cdft", [128, 1024])
    kbd_d = dram_in("k_bd", [2, 128, 128])
    krope_d = dram_in("k_rope", [2, 128, SEQ])
    kperm_d = dram_in("k_perm", [128, 128])
    kswap_d = dram_in("k_swap", [128, 128])
    kinvw_d = dram_in("k_invw", [128, 2])
    krcnt_d = dram_in("k_rcnt", [128, 32])
    krcf_d = dram_in("k_rcf", [3, 128, 1024])
    out_d = nc.dram_tensor("out", [nb, SEQ, D], F32, kind="ExternalOutput").ap()
    win_s = nc.dram_tensor("win_s", [DEPTH, 128, 10 * 8 * 128], BF16, kind="Internal").ap()
    wout_s = nc.dram_tensor("wout_s", [DEPTH, 128, 8 * 8 * 128], BF16, kind="Internal").ap()
    ab_s = nc.dram_tensor("ab_s", [DEPTH, 128, 2 * 512], BF16, kind="Internal").ap()
    dft_s = nc.dram_tensor("dft_s", [32, 128, 2048], BF16, kind="Internal").ap()

    wgu_s = [nc.dram_tensor(f"wgu_s{f}", [DEPTH, 128, NFC * 8 * 256], BF16, kind="Internal").ap() for f in range(2)]
    wdn_s = [nc.dram_tensor(f"wdn_s{f}", [DEPTH, 128, 8 * NFC * 128], BF16, kind="Internal").ap() for f in range(2)]

    xT = nc.alloc_sbuf_tensor("xT", [128, 8, SEQ], F32)
    cT = nc.alloc_sbuf_tensor("cT", [128, 8, CTX], F32)
    mod = nc.alloc_sbuf_tensor("mod", [128, DEPTH, 72, 5], F32)
    ident = nc.alloc_sbuf_tensor("ident", [128, 128], F32)
    ones_bf = nc.alloc_sbuf_tensor("ones_bf", [128, 128], BF16)
    fgain = nc.alloc_sbuf_tensor("fgain", [128, 8], F32)
    perm_bf = nc.alloc_sbuf_tensor("perm_bf", [128, 128], BF16)
    bdones_bf = nc.alloc_sbuf_tensor("bdones_bf", [128, 128], BF16)
    esel_bf = nc.alloc_sbuf_tensor("esel_bf", [128, 2, 128], BF16)
    swap_f = nc.alloc_sbuf_tensor("swap_f", [128, 128], F32)
    cdft_bf = nc.alloc_sbuf_tensor("cdft_bf", [128, 2, 2, 256], BF16)
    poolw_bf = nc.alloc_sbuf_tensor("poolw_bf", [128, DEPTH, 2, 128], BF16)
    pscale = nc.alloc_sbuf_tensor("pscale", [128, DEPTH, 2], F32)
    qg = nc.alloc_sbuf_tensor("qg", [128, DEPTH], F32)
    kg = nc.alloc_sbuf_tensor("kg", [128, DEPTH], F32)
    invw = nc.alloc_sbuf_tensor("invw", [128, 2], F32)
    rcnt = nc.alloc_sbuf_tensor("rcnt", [128, 2, 16], F32)
    WORK_BYTES = 120 * 1024
    work = nc.alloc_sbuf_tensor("work", [128, WORK_BYTES // 2], BF16)

    class Arena:
        def __init__(self):
            self.off = 0

        def alloc(self, shape_free, dt):
            n = int(np.prod(shape_free))
            esz = 4 if dt == F32 else 2
            nbytes = (n * esz + 63) // 64 * 64
            assert self.off + nbytes <= WORK_BYTES, (self.off, nbytes)
            v = work[:, self.off // 2:(self.off + n * esz) // 2]
            self.off += nbytes
            if dt == F32:
                v = v.bitcast(F32)
            if len(shape_free) == 2:
                v = v.rearrange("p (a b) -> p a b", a=shape_free[0])
            elif len(shape_free) == 3:
                v = v.rearrange("p (a b c) -> p a b c", a=shape_free[0], b=shape_free[1])
            return v

    pp = [nc.alloc_psum_tensor(f"pp{i}", [128, 1024], F32) for i in range(4)]
    ps = [pp[i // 2][:, (i % 2) * 512:(i % 2 + 1) * 512] for i in range(8)]

    sem_misc = S.dma_sem("misc")
    S.op('sp', lambda h: h.dma_start(out=ident[:], in_=ident_d[:, :]), writes=['ident'], dma=sem_misc)
    S.op('dve', lambda h: h.memset(ones_bf[:], 1.0), writes=['ones_bf'])
    with nc.allow_non_contiguous_dma(reason="tiny one-time vector loads"):
        pass
    S.op('sp', lambda h: h.dma_start(out=fgain[:], in_=fg_d.rearrange("(k p) -> p k", p=128),
                                     allow_slow_non_contiguous=True), writes=['fgain'], dma=S.dma_sem("fg"))

    def prologue_ffn():
        A = Arena()
        stage = [A.alloc([DFF], F32) for _ in range(2)]
        stsem = [S.dma_sem(f"pst{i}") for i in range(2)]
        big = A.alloc([NFC * 8 * 256], BF16)
        bigsem = S.dma_sem("pbig")
        cnt = 0

        def cp(eng, dst, src, reads, writes):
            if eng == 'act':
                S.op('act', lambda h: h.copy(out=dst, in_=src), reads=reads, writes=writes)
            else:
                S.op('dve', lambda h: h.tensor_copy(out=dst, in_=src), reads=reads, writes=writes)

        for l in range(depth):
            for f in range(2):
                bigv = big.rearrange("p (j k h c) -> p j k h c", j=NFC, k=8, h=2)
                for kc in range(8):
                    for hh in range(2):
                        st = stage[cnt % 2]
                        rs = ('pstage', cnt % 2)
                        S.op('sp', lambda h, st=st, l=l, f=f, kc=kc, hh=hh: h.dma_start(
                            out=st, in_=wgu_d[f][l, kc * 128:(kc + 1) * 128, hh * DFF:(hh + 1) * DFF]),
                            writes=[rs], dma=stsem[cnt % 2])
                        cp('act' if cnt % 2 == 0 else 'dve', bigv[:, :, kc, hh, :],
                           st.rearrange("p (j c) -> p j c", j=NFC), [rs], [('pbig', cnt % 2)])
                        cnt += 1
                S.op('sp', lambda h, l=l, f=f: h.dma_start(out=wgu_s[f][l], in_=big),
                     reads=[('pbig', 0), ('pbig', 1)], writes=[('wgu_s', f, l), ('pbig', 0), ('pbig', 1)], dma=bigsem)
                bigd = big[:, 0:8 * NFC * 128].rearrange("p (d j c) -> p d j c", d=8, j=NFC)
                for q in range(11):
                    st = stage[cnt % 2]
                    rs = ('pstage', cnt % 2)
                    stv = st[:, 0:2048].rearrange("p (j d) -> p j d", j=2)
                    S.op('sp', lambda h, stv=stv, l=l, f=f, q=q: h.dma_start(
                        out=stv, in_=wdn_d[f][l, q * 256:(q + 1) * 256, :].rearrange("(j p) d -> p j d", p=128)),
                        writes=[rs], dma=stsem[cnt % 2])
                    cp('act' if cnt % 2 == 0 else 'dve', bigd[:, :, q * 2:(q + 1) * 2, :],
                       stv.rearrange("p j (d c) -> p d j c", d=8), [rs], [('pbig', cnt % 2)])
                    cnt += 1
                S.op('sp', lambda h, l=l, f=f: h.dma_start(out=wdn_s[f][l], in_=big[:, 0:8 * NFC * 128]),
                     reads=[('pbig', 0), ('pbig', 1)], writes=[('wdn_s', f, l), ('pbig', 0), ('pbig', 1)], dma=bigsem)
        S.barrier()

    def prologue_mod():
        A = Arena()
        ccT = A.alloc([8, 5], F32)
        adab = A.alloc([DEPTH, 72], F32)
        wst = [A.alloc([8, 1152], F32) for _ in range(2)]
        wsem = [S.dma_sem(f"adaw{i}") for i in range(2)]
        sm = S.dma_sem("modmisc")
        for b in range(nb):
            S.op('sp', lambda h, b=b: h.dma_start(out=ccT[:, :, b:b + 1], in_=c_d[b].rearrange("(k p o) -> p k o", p=128, o=1),
                                                  allow_slow_non_contiguous=True), writes=[('ccT', b)], dma=sm)
        S.op('sp', lambda h: h.dma_start(out=ccT[:, :, 4:5], in_=cctx_d.rearrange("(k p o) -> p k o", p=128, o=1),
                                         allow_slow_non_contiguous=True), writes=[('ccT', 4)], dma=sm)
        if nb < 4:
            for b in range(nb, 4):
                S.op('dve', lambda h, b=b: h.memset(ccT[:, :, b:b + 1], 0.0), writes=[('ccT', b)])
        S.op('sp', lambda h: h.dma_start(out=adab, in_=adab_d.rearrange("l (f p) -> p l f", p=128),
                                         allow_slow_non_contiguous=True), writes=['adab'], dma=sm)
        S.op('act', lambda h: h.activation(out=ccT, in_=ccT, func=AF.Silu),
             reads=[('ccT', b) for b in range(5)], writes=['ccS'])
        cnt = 0
        for l in range(depth):
            for piece in range(8):
                st = wst[cnt % 2]
                rs = ('adawst', cnt % 2)
                S.op('sp', lambda h, st=st, l=l, piece=piece: h.dma_start(
                    out=st, in_=adaw_d[l, :, piece * 1152:(piece + 1) * 1152].rearrange("(k p) f -> p k f", p=128)),
                    writes=[rs], dma=wsem[cnt % 2])
                bank = ps[cnt % 2]

                def mm(h, st=st, bank=bank):
                    ins = None
                    for fcl in range(9):
                        for kc in range(8):
                            ins = h.matmul(bank[:, fcl * 8:fcl * 8 + 5], lhsT=st[:, kc, fcl * 128:(fcl + 1) * 128],
                                           rhs=ccT[:, kc, :], start=(kc == 0), stop=(kc == 7))
                    return ins
                S.op('pe', mm, reads=[rs, 'ccS'], writes=[('psb', cnt % 2)])
                S.op('dve', lambda h, bank=bank, l=l, piece=piece: h.tensor_tensor(
                    out=mod[:, l, piece * 9:(piece + 1) * 9, :],
                    in0=bank[:, 0:72].rearrange("p (f e) -> p f e", e=8)[:, :, 0:5],
                    in1=adab[:, l, piece * 9:(piece + 1) * 9].unsqueeze(2).broadcast_to([128, 9, 5]), op=ALU.add),
                    reads=[('psb', cnt % 2), 'adab'], writes=[('mod', l)])
                cnt += 1
            for m in (1, 4, 7):
                S.op('dve', lambda h, l=l, m=m: h.tensor_scalar_add(out=mod[:, l, m * 8:(m + 1) * 8, :], in0=mod[:, l, m * 8:(m + 1) * 8, :], scalar1=1.0),
                     reads=[('mod', l)], writes=[('mod', l)])
            for m in (2, 8):
                S.op('dve', lambda h, l=l, m=m: h.tensor_scalar_mul(out=mod[:, l, m * 8:(m + 1) * 8, :], in0=mod[:, l, m * 8:(m + 1) * 8, :], scalar1=0.5),
                     reads=[('mod', l)], writes=[('mod', l)])
        S.barrier()

    def xk(name, tile):
        return [(name, tile), (name, tile, 0), (name, tile, 1)]

    def modv(l, m, kc, b):
        return mod[:, l, m * 8 + kc, b:b + 1]

    class FFNBufs:
        def __init__(self):
            A = Arena()
            self.hT = [A.alloc([8, 512], BF16) for _ in range(2)]
            self.sq = A.alloc([8, 512], BF16)
            self.t = A.alloc([8, 512], F32)
            self.rs = A.alloc([512], F32)
            self.sg = [A.alloc([512], F32) for _ in range(2)]
            self.act = A.alloc([NFC, 512], BF16)
            self.wgu = [(A.alloc([8, 256], BF16), S.dma_sem(f"wgu{i}")) for i in range(4)]
            self.wdn = [(A.alloc([NFC, 128], BF16), S.dma_sem(f"wdn{i}")) for i in range(4)]
            self.end = A.off
            self.tile_i = 0

    class IOBufs:
        def __init__(self, base):
            A = Arena()
            A.off = base
            self.stg = [(A.alloc([1024], F32), S.dma_sem(f"io{i}")) for i in range(2)]
            self.i = 0

    ffnb = None
    iob = None

    def rms_stats(xv, T, scale, key_in, bufs):
        S.op('act', lambda h: h.activation(out=bufs.sq[:, :, :T], in_=xv, func=AF.Square),
             reads=[key_in], writes=['sq'])

        def mm(h):
            ins = None
            for kc in range(8):
                ins = h.matmul(ps[6][:, :T], lhsT=ones_bf[:], rhs=bufs.sq[:, kc, :T], start=(kc == 0), stop=(kc == 7))
            return ins
        S.op('pe', mm, reads=['sq', 'ones_bf'], writes=['ps6'])
        S.op('act', lambda h: h.activation(out=bufs.rs[:, :T], in_=ps[6][:, :T], func=AF.Sqrt, scale=scale, bias=EPS),
             reads=['ps6'], writes=['rs'])
        S.op('dve', lambda h: h.reciprocal(out=bufs.rs[:, :T], in_=bufs.rs[:, :T]), reads=['rs'], writes=['rs'])

    def ffn_tile(l, f, xv, T, b, xkey):
        B = ffnb
        ti = B.tile_i
        B.tile_i += 1
        hT = B.hT[ti % 2]
        hkey = ('hT', ti % 2)
        m0 = 0 if f == 0 else 6
        rms_stats(xv, T, 1.0 / D, xkey, B)
        for kc in range(8):
            S.op('dve', lambda h, kc=kc: h.scalar_tensor_tensor(
                out=B.t[:, kc, :T], in0=xv[:, kc, :], scalar=modv(l, m0 + 1, kc, b), in1=B.rs[:, :T],
                op0=ALU.mult, op1=ALU.mult), reads=[xkey, 'rs'], writes=[('t', kc)])
            S.op('act', lambda h, kc=kc: h.activation(
                out=hT[:, kc, :T], in_=B.t[:, kc, :T], func=AF.Identity, bias=modv(l, m0, kc, b), scale=1.0),
                reads=[('t', kc)], writes=[hkey])
        gu_units = [wgu_s[f][l][:, j * 2048:(j + 1) * 2048].rearrange("p (k c) -> p k c", k=8) for j in range(NFC)]
        st_gu = Stream(S, 'wgu', B.wgu, gu_units)
        dn_units = [wdn_s[f][l][:, dc * NFC * 128:(dc + 1) * NFC * 128].rearrange("p (j c) -> p j c", j=NFC) for dc in range(8)]
        st_dn = Stream(S, 'wdn', B.wdn, dn_units)
        st_gu.start()
        for j in range(NFC):
            w, wkey = st_gu.get()
            pg, pu = ps[j % 2], ps[2 + j % 2]

            def mmg(h, w=w, pg=pg):
                ins = None
                for kc in range(8):
                    ins = h.matmul(pg[:, :T], lhsT=w[:, kc, 0:128], rhs=hT[:, kc, :T], start=(kc == 0), stop=(kc == 7))
                return ins

            def mmu(h, w=w, pu=pu):
                ins = None
                for kc in range(8):
                    ins = h.matmul(pu[:, :T], lhsT=w[:, kc, 128:256], rhs=hT[:, kc, :T], start=(kc == 0), stop=(kc == 7))
                return ins
            S.op('pe', mmg, reads=[wkey, hkey], writes=[('ps', j % 2)])
            S.op('pe', mmu, reads=[wkey, hkey], writes=[('ps', 2 + j % 2)])
            sg = B.sg[j % 2]
            S.op('act', lambda h, sg=sg, pg=pg: h.activation(out=sg[:, :T], in_=pg[:, :T], func=AF.Silu),
                 reads=[('ps', j % 2)], writes=[('sg', j % 2)])
            S.op('dve', lambda h, sg=sg, pu=pu, j=j: h.tensor_tensor(out=B.act[:, j, :T], in0=pu[:, :T], in1=sg[:, :T], op=ALU.mult),
                 reads=[('ps', 2 + j % 2), ('sg', j % 2)], writes=[('act', j)])
            if j == NFC - 6:
                st_dn.start()
        for dc in range(8):
            w, wkey = st_dn.get()
            pd = ps[4 + dc % 2]

            def mmd(h, w=w, pd=pd):
                ins = None
                for fc in range(NFC):
                    ins = h.matmul(pd[:, :T], lhsT=w[:, fc, :], rhs=B.act[:, fc, :T], start=(fc == 0), stop=(fc == NFC - 1))
                return ins
            S.op('pe', mmd, reads=[wkey] + [('act', j) for j in range(NFC)], writes=[('ps', 4 + dc % 2)])
            S.op('dve', lambda h, dc=dc, pd=pd: h.scalar_tensor_tensor(
                out=xv[:, dc, :], in0=pd[:, :T], scalar=modv(l, m0 + 2, dc, b), in1=xv[:, dc, :],
                op0=ALU.mult, op1=ALU.add), reads=[('ps', 4 + dc % 2), xkey], writes=[xkey])

    def load_tokens(src_rows, dstT, ntok, key):
        for tt in range(ntok // 128):
            st, sem = iob.stg[iob.i % 2]
            skey = ('iostg', iob.i % 2)
            iob.i += 1
            S.op('sp', lambda h, st=st, tt=tt: h.dma_start(out=st, in_=src_rows[tt * 128:(tt + 1) * 128, :]),
                 writes=[skey], dma=sem)
            for hf in range(2):
                bank = ps[hf]

                def tr(h, st=st, bank=bank, hf=hf):
                    ins = None
                    for q in range(4):
                        kc = hf * 4 + q
                        ins = h.transpose(out=bank[:, q * 128:(q + 1) * 128], in_=st[:, kc * 128:(kc + 1) * 128], identity=ident[:])
                    return ins
                S.op('pe', tr, reads=[skey, 'ident'], writes=[('ps', hf)])
                eng = 'act' if hf == 0 else 'dve'
                dst = dstT[:, hf * 4:(hf + 1) * 4, tt * 128:(tt + 1) * 128]
                srcv = bank[:, :].rearrange("p (q t) -> p q t", q=4)
                kk = (key, tt // 4, hf)
                if eng == 'act':
                    S.op('act', lambda h, dst=dst, srcv=srcv: h.copy(out=dst, in_=srcv), reads=[('ps', hf)], writes=[kk])
                else:
                    S.op('dve', lambda h, dst=dst, srcv=srcv: h.tensor_copy(out=dst, in_=srcv), reads=[('ps', hf)], writes=[kk])

    def store_out(b):
        B = ffnb
        for tq in range(SEQ // 512):
            xv = xT[:, :, tq * 512:(tq + 1) * 512]
            if final_norm:
                rms_stats(xv, 512, 1.0 / D, xk('xT', tq), B)
                for kc in range(8):
                    S.op('dve', lambda h, kc=kc, xv=xv: h.scalar_tensor_tensor(
                        out=B.t[:, kc, :], in0=xv[:, kc, :], scalar=fgain[:, kc:kc + 1], in1=B.rs[:, :],
                        op0=ALU.mult, op1=ALU.mult), reads=[xk('xT', tq), 'rs', 'fgain'], writes=[('t', kc)])
                srcT = B.t
                skeys = [('t', kc) for kc in range(8)]
            else:
                srcT = xv
                skeys = [xk('xT', tq)]
            for tt in range(4):
                st, sem = iob.stg[iob.i % 2]
                skey = ('iostg', iob.i % 2)
                iob.i += 1
                for hf in range(2):
                    bank = ps[hf]

                    def tr(h, bank=bank, hf=hf, tt=tt, srcT=srcT):
                        ins = None
                        for q in range(4):
                            kc = hf * 4 + q
                            ins = h.transpose(out=bank[:, q * 128:(q + 1) * 128], in_=srcT[:, kc, tt * 128:(tt + 1) * 128], identity=ident[:])
                        return ins
                    S.op('pe', tr, reads=skeys + ['ident'], writes=[('ps', hf)])
                    dst = st[:, hf * 512:(hf + 1) * 512]
                    if hf == 0:
                        S.op('act', lambda h, dst=dst, bank=bank: h.copy(out=dst, in_=bank[:, :]), reads=[('ps', hf)], writes=[(skey, hf)])
                    else:
                        S.op('dve', lambda h, dst=dst, bank=bank: h.tensor_copy(out=dst, in_=bank[:, :]), reads=[('ps', hf)], writes=[(skey, hf)])
                r0 = tq * 512 + tt * 128
                S.op('sp', lambda h, st=st, r0=r0: h.dma_start(out=out_d[b, r0:r0 + 128, :], in_=st),
                     reads=[(skey, 0), (skey, 1)], writes=[skey], dma=sem)


    def prologue_mix():
        A = Arena()
        st32 = [A.alloc([2048], F32) for _ in range(2)]
        stsem = [S.dma_sem(f"pm{i}") for i in range(2)]
        big = A.alloc([10 * 8 * 128], BF16)
        bigsem = S.dma_sem("pmbig")
        bfr = [A.alloc([2048], BF16) for _ in range(2)]
        bfsem = [S.dma_sem(f"pmbf{i}") for i in range(2)]
        bd32 = A.alloc([2, 128], F32)
        pw32 = A.alloc([2, 128], F32)
        abt = A.alloc([2, 512], BF16)
        sm = S.dma_sem("pmmisc")
        state = {'c': 0}

        def stage_load(dmas):
            i = state['c'] % 2
            state['c'] += 1
            st = st32[i]
            key = ('pmst', i)
            if len(dmas) == 1:
                dv, src = dmas[0]
                S.op('sp', lambda h, d=dv(st), src=src: h.dma_start(out=d, in_=src), writes=[key], dma=stsem[i])
                return st, key
            S.op('dve', lambda h: h.nop(), reads=[], writes=[key])
            keys = []
            for j, (dv, src) in enumerate(dmas):
                S.op('sp', lambda h, d=dv(st), src=src: h.dma_start(out=d, in_=src), reads=[key], writes=[(key, j)], dma=stsem[i])
                keys.append((key, j))
            return st, keys

        def cp(eng, dst, src, reads, writes):
            if eng == 'act':
                S.op('act', lambda h: h.copy(out=dst, in_=src), reads=reads, writes=writes)
            else:
                S.op('dve', lambda h: h.tensor_copy(out=dst, in_=src), reads=reads, writes=writes)

        st, key = stage_load([(lambda st: st[:, 0:128], kperm_d[:, :])])
        cp('dve', perm_bf[:], st[:, 0:128], [key], ['perm_bf'])
        S.op('sp', lambda h: h.dma_start(out=swap_f[:], in_=kswap_d[:, :]), writes=['swap_f'], dma=sm)
        S.op('sp', lambda h: h.dma_start(out=invw[:], in_=kinvw_d[:, :]), writes=['invw'], dma=sm)
        S.op('sp', lambda h: h.dma_start(out=rcnt[:].rearrange("p a b -> p (a b)"), in_=krcnt_d[:, :]), writes=['rcnt'], dma=sm)
        S.op('sp', lambda h: h.dma_start(out=bd32, in_=kbd_d.rearrange("a p c -> p a c")), writes=['bd32'], dma=sm)
        S.op('dve', lambda h: h.memset(esel_bf[:].rearrange('p a b -> p (a b)'), 0.0), writes=['esel'])
        S.op('dve', lambda h: h.memset(esel_bf[:, 0, 0:64], 1.0), reads=['esel'], writes=['esel'])
        S.op('dve', lambda h: h.memset(esel_bf[:, 1, 64:128], 1.0), reads=['esel'], writes=['esel'])
        S.op('dve', lambda h: h.memset(bdones_bf[:], 0.0), writes=['bdones'])
        S.op('dve', lambda h: h.memset(bdones_bf[0:64, 0:64], 1.0), reads=['bdones'], writes=['bdones'])
        S.op('dve', lambda h: h.memset(bdones_bf[64:128, 64:128], 1.0), reads=['bdones'], writes=['bdones'])
        st, key = stage_load([(lambda st: st[:, 0:1024], kcdft_d[:, :])])
        cp('act', cdft_bf[:].rearrange("p a b c -> p (a b c)"), st[:, 0:1024], [key], ['cdft'])
        for hh in range(2):
            S.op('sp', lambda h, hh=hh: h.dma_start(out=qg[hh * 64:(hh + 1) * 64, :], in_=qg_d.rearrange("l e -> e l"),
                                                   allow_slow_non_contiguous=True), writes=[('qg', hh)], dma=sm)
            S.op('sp', lambda h, hh=hh: h.dma_start(out=kg[hh * 64:(hh + 1) * 64, :], in_=kg_d.rearrange("l e -> e l"),
                                                   allow_slow_non_contiguous=True), writes=[('kg', hh)], dma=sm)
        S.op('sp', lambda h: h.dma_start(out=pscale[:], in_=pools_d.rearrange("l (m p) -> p l m", p=128),
                                         allow_slow_non_contiguous=True), writes=['pscale'], dma=sm)
        S.barrier()
        for u in range(32 if pmdbg >= 2 else 0):
            st, key = stage_load([(lambda st: st, kdft_d[u])])
            bf = bfr[u % 2]
            bkey = ('pmbf', u % 2)
            cp('act' if u % 2 == 0 else 'dve', bf, st, [key], [bkey])
            S.op('sp', lambda h, bf=bf, u=u: h.dma_start(out=dft_s[u], in_=bf), reads=[bkey], writes=[bkey, 'dft_s'], dma=bfsem[u % 2])
        for l in range(depth if pmdbg >= 3 else 0):
            S.op('dve', lambda h: h.memset(pw32, 0.0), writes=['pw32'])
            for g in range(4):
                S.op('sp', lambda h, l=l, g=g: h.dma_start(
                    out=pw32[(g % 2) * 64:(g % 2 + 1) * 64, g // 2, (g % 2) * 64:(g % 2 + 1) * 64], in_=poolw_d[l, g]),
                    reads=['pw32'], writes=[('pw32d', g)], dma=sm)
            cp('dve', poolw_bf[:, l], pw32, ['pw32'] + [('pw32d', g) for g in range(4)], [('poolw', l), 'pw32'] + [('pw32d', g) for g in range(4)])
            for k2 in range(2 if pmdbg >= 4 else 0):
                st, key = stage_load([(lambda st: st[:, 0:256], fftw_d[l, k2 * 128:(k2 + 1) * 128, :])])

                def mm(h, st=st):
                    h.matmul(ps[0][:, 0:256], lhsT=bd32[:, 0, :], rhs=st[:, 0:256], start=True, stop=True)
                    return h.matmul(ps[0][:, 256:512], lhsT=bd32[:, 1, :], rhs=st[:, 0:256], start=True, stop=True)
                S.op('pe', mm, reads=[key, 'bd32'], writes=[('ps', 0)])
                cp('act', abt[:, k2, :], ps[0], [('ps', 0)], [('abt', k2)])
            if pmdbg >= 4:
                S.op('sp', lambda h, l=l: h.dma_start(out=ab_s[l], in_=abt.rearrange("p a b -> p (a b)")),
                     reads=[('abt', 0), ('abt', 1)], writes=[('abt', 0), ('abt', 1), ('ab_s', l)], dma=sm)
            if pmdbg < 5:
                continue
            bigv = big.rearrange("p (o k c) -> p o k c", o=10, k=8)
            for kc in range(8):
                qsrc = win_d[l, kc * 128:(kc + 1) * 128, 0:512].rearrange("p (t c e) -> p c t e", t=2, c=4)
                dl = [((lambda st, c=c: st[:, c * 128:(c + 1) * 128].rearrange("p (t e) -> p t e", t=2)), qsrc[:, c]) for c in range(4)]
                dl.append((lambda st: st[:, 512:INW], win_d[l, kc * 128:(kc + 1) * 128, 512:INW]))
                st, keys = stage_load(dl)
                cp('act' if kc % 2 == 0 else 'dve', bigv[:, :, kc, :], st[:, 0:INW].rearrange("p (o c) -> p o c", o=10),
                   keys, [('pmbig', kc % 2), ('pmst', (state['c'] - 1) % 2)])
            S.op('sp', lambda h, l=l: h.dma_start(out=win_s[l], in_=big), reads=[('pmbig', 0), ('pmbig', 1)],
                 writes=[('pmbig', 0), ('pmbig', 1), ('win_s', l)], dma=bigsem)
            if pmdbg < 6:
                continue
            bigo = big[:, 0:8 * 8 * 128].rearrange("p (d m c) -> p d m c", d=8, m=8)
            for mc in range(8):
                if mc < 4:
                    st, key = stage_load([(lambda st: st[0:64, 0:1024], wout_d[l, mc * 64:(mc + 1) * 64, :]),
                                          (lambda st: st[64:128, 0:1024], wout_d[l, (mc + 4) * 64:(mc + 5) * 64, :])])
                else:
                    st, key = stage_load([(lambda st: st[:, 0:1024], wout_d[l, mc * 128:(mc + 1) * 128, :])])
                klist = key if isinstance(key, list) else [key]
                cp('act' if mc % 2 == 0 else 'dve', bigo[:, :, mc, :], st[:, 0:1024].rearrange("p (d c) -> p d c", d=8),
                   klist, [('pmbig', mc % 2), ('pmst', (state['c'] - 1) % 2)])
            S.op('sp', lambda h, l=l: h.dma_start(out=wout_s[l], in_=big[:, 0:8 * 8 * 128]), reads=[('pmbig', 0), ('pmbig', 1)],
                 writes=[('pmbig', 0), ('pmbig', 1), ('wout_s', l)], dma=bigsem)
        S.barrier()

    class MixBufs:
        def __init__(self):
            A = Arena()
            self.qT = A.alloc([4, SEQ], BF16)
            self.qcT = A.alloc([4, CTX], BF16)
            self.kT = A.alloc([SEQ + CTX], BF16)
            self.Vx = A.alloc([18, 128], BF16)
            self.upT = A.alloc([2, SEQ + 16], F32)
            self.upc = A.alloc([2, CTX + 16], F32)
            self.uab = A.alloc([18, 512], BF16)
            base = A.off
            self.hT = A.alloc([8, 512], BF16)
            self.sq = A.alloc([8, 512], BF16)
            self.t = [A.alloc([512], F32) for _ in range(2)]
            self.rs = A.alloc([512], F32)
            self.sq2 = [A.alloc([512], BF16) for _ in range(2)]
            self.rq = [A.alloc([512], F32) for _ in range(2)]
            self.qn = [A.alloc([512], F32) for _ in range(2)]
            self.qnb = [A.alloc([512], BF16) for _ in range(2)]
            self.t1 = [A.alloc([512], F32) for _ in range(2)]
            self.qk_i = 0
            self.ufT = A.alloc([2, 512], BF16)
            self.rope = [(A.alloc([2, 512], F32), S.dma_sem(f"rope{i}")) for i in range(1)]
            self.win = [(A.alloc([8, 128], BF16), S.dma_sem(f"win{i}")) for i in range(2)]
            self.AB = A.alloc([2, 512], BF16)
            self.ABsem = S.dma_sem("ab")
            self.end1 = A.off
            A.off = base
            self.P = [A.alloc([2, 512], BF16) for _ in range(3)]
            self.rr = A.alloc([512], F32)
            self.rc = A.alloc([512], F32)
            self.mix = [A.alloc([8, 512], BF16) for _ in range(2)]
            self.a2 = A.alloc([544], F32)
            self.a4 = A.alloc([544], F32)
            self.a8 = A.alloc([544], F32)
            self.a16 = A.alloc([544], F32)
            self.pm = A.alloc([2, 512], BF16)
            self.rcf = A.alloc([2, 512], F32)
            self.rcfsem = S.dma_sem('rcf')
            self.dft = [(A.alloc([4, 512], BF16), S.dma_sem(f"dft{i}")) for i in range(2)]
            self.wo = [(A.alloc([8, 128], BF16), S.dma_sem(f"wo{i}")) for i in range(2)]
            self.end2 = A.off
            self.mix_i = 0
            self.p_i = 0

    def qk_norm_rope(M, pq, T, gain_ap, dst, cs, cskey, tag):
        par = M.qk_i % 2
        M.qk_i += 1
        sq2, rq, qn, qnb, t1 = M.sq2[par], M.rq[par], M.qn[par], M.qnb[par], M.t1[par]
        pstat, kstat = (ps[6], 'ps6') if par == 0 else (ps[4], ('ps', 4))
        pperm, kperm = (ps[7], 'ps7') if par == 0 else (ps[5], ('ps', 5))
        k = lambda n: (n, par)
        S.op('act', lambda h: h.activation(out=sq2[:, :T], in_=pq[:, :T], func=AF.Square), reads=[tag], writes=[k('sq2')])
        S.op('pe', lambda h: h.matmul(pstat[:, :T], lhsT=bdones_bf[:], rhs=sq2[:, :T], start=True, stop=True),
             reads=[k('sq2'), 'bdones'], writes=[kstat])
        S.op('act', lambda h: h.activation(out=rq[:, :T], in_=pstat[:, :T], func=AF.Sqrt, scale=1.0 / 64, bias=EPS),
             reads=[kstat], writes=[k('rq')])
        S.op('dve', lambda h: h.reciprocal(out=rq[:, :T], in_=rq[:, :T]), reads=[k('rq')], writes=[k('rq')])
        S.op('dve', lambda h: h.scalar_tensor_tensor(out=qn[:, :T], in0=pq[:, :T], scalar=gain_ap, in1=rq[:, :T],
                                                     op0=ALU.mult, op1=ALU.mult), reads=[tag, k('rq')], writes=[k('qn')])
        if cs is None:
            S.op('act', lambda h: h.copy(out=dst, in_=qn[:, :T]), reads=[k('qn')], writes=['qkdst'])
            return None
        S.op('act', lambda h: h.copy(out=qnb[:, :T], in_=qn[:, :T]), reads=[k('qn')], writes=[k('qnb')])
        S.op('pool', lambda h: h.tensor_tensor(out=t1[:, :T], in0=qn[:, :T], in1=cs[:, 0, :T], op=ALU.mult),
             reads=[k('qn'), cskey], writes=[k('t1')])

        def partB():
            S.op('pe', lambda h: h.matmul(pperm[:, :T], lhsT=perm_bf[:], rhs=qnb[:, :T], start=True, stop=True),
                 reads=[k('qnb'), 'perm_bf'], writes=[kperm])
            S.op('dve', lambda h: h.tensor_tensor(out=rq[:, :T], in0=pperm[:, :T], in1=cs[:, 1, :T], op=ALU.mult),
                 reads=[kperm, cskey, k('qn')], writes=[k('rq')])
            S.op('dve', lambda h: h.tensor_tensor(out=dst, in0=rq[:, :T], in1=t1[:, :T], op=ALU.add),
                 reads=[k('rq'), k('t1')], writes=['qkdst'])
            return None
        return partB

    def mixer_m1(M, l, b, last):
        S.op('sp', lambda h: h.dma_start(out=M.AB.rearrange("p a b -> p (a b)"), in_=ab_s[l]), reads=[('ab_s', l)], writes=['AB'], dma=M.ABsem)
        S.op('dve', lambda h: h.memset(M.upT.rearrange('p a b -> p (a b)'), 0.0), writes=['upT'])
        S.op('dve', lambda h: h.memset(M.upc.rearrange('p a b -> p (a b)'), 0.0), writes=['upc'])
        tiles = [(xT[:, :, tq * 512:(tq + 1) * 512], 512, b, xk('xT', tq), tq) for tq in range(4)] + [(cT[:, :, :], CTX, 4, xk('cT', 0), 4)]
        rope_i = 0
        t_i = 0
        for (xv, T, bb, xkey, tq) in tiles:
            is_ctx = (tq == 4)
            rms_stats(xv, T, 1.0 / D, xkey, M)
            for kc in range(8):
                tt = M.t[t_i % 2]
                tk = ('mt', t_i % 2)
                t_i += 1
                S.op('dve', lambda h, kc=kc, tt=tt, xv=xv, T=T, bb=bb: h.scalar_tensor_tensor(
                    out=tt[:, :T], in0=xv[:, kc, :], scalar=modv(l, 4, kc, bb), in1=M.rs[:, :T],
                    op0=ALU.mult, op1=ALU.mult), reads=[xkey, 'rs'], writes=[tk])
                S.op('act', lambda h, kc=kc, tt=tt, T=T, bb=bb: h.activation(
                    out=M.hT[:, kc, :T], in_=tt[:, :T], func=AF.Identity, bias=modv(l, 3, kc, bb), scale=1.0),
                    reads=[tk], writes=['mhT'])
            if not is_ctx:
                cs, csem = M.rope[0]
                cskey = ('rope', 0)
                rope_i += 1
                S.op('sp', lambda h, cs=cs, tq=tq: h.dma_start(out=cs, in_=krope_d[:, :, tq * 512:(tq + 1) * 512].rearrange("a p t -> p a t")),
                     writes=[cskey], dma=csem)
            else:
                cs, cskey = None, None
            if is_ctx and last:
                chunks = [4, 5]
            else:
                chunks = list(range(10))
            units = [win_s[l][:, oc * 1024:(oc + 1) * 1024].rearrange("p (k c) -> p k c", k=8) for oc in chunks]
            st = Stream(S, 'win', M.win, units)
            pipe = {'A': None, 'B': None}

            def step(newA):
                if pipe['B'] is not None:
                    pipe['B']()
                    pipe['B'] = None
                if pipe['A'] is not None:
                    pipe['B'] = pipe['A']()
                pipe['A'] = newA
            for oc in chunks:
                w, wkey = st.get()
                if oc == 5:
                    for tc in range(T // 128):
                        ch = (tq * 4 + tc) if not is_ctx else 16 + tc

                        def mmv(h, w=w, tc=tc):
                            ins = None
                            for kc in range(8):
                                ins = h.matmul(ps[1][:, 0:128], lhsT=M.hT[:, kc, tc * 128:(tc + 1) * 128], rhs=w[:, kc, :],
                                               start=(kc == 0), stop=(kc == 7))
                            return ins
                        S.op('pe', mmv, reads=[wkey, 'mhT'], writes=[('ps', 1)])
                        S.op('act', lambda h, ch=ch: h.activation(out=M.Vx[:, ch, :], in_=ps[1][:, 0:128], func=AF.Identity), reads=[('ps', 1)], writes=[('Vxa', ch)])
                    step(None)
                    continue
                pq = ps[0] if oc % 2 == 0 else ps[2]
                ptag = ('ps', 0) if oc % 2 == 0 else ('ps', 2)

                def mmq(h, w=w, pq=pq, T=T):
                    ins = None
                    for kc in range(8):
                        ins = h.matmul(pq[:, :T], lhsT=w[:, kc, :], rhs=M.hT[:, kc, :T], start=(kc == 0), stop=(kc == 7))
                    return ins
                S.op('pe', mmq, reads=[wkey, 'mhT'], writes=[ptag])
                if oc < 4:
                    dst = M.qT[:, oc, tq * 512:(tq + 1) * 512] if not is_ctx else M.qcT[:, oc, :]
                    step(lambda pq=pq, T=T, dst=dst, cs=cs, cskey=cskey, ptag=ptag: qk_norm_rope(M, pq, T, qg[:, l:l + 1], dst, cs, cskey, ptag))
                elif oc == 4:
                    dst = M.kT[:, tq * 512:(tq + 1) * 512] if not is_ctx else M.kT[:, SEQ:SEQ + CTX]
                    step(lambda pq=pq, T=T, dst=dst, cs=cs, cskey=cskey, ptag=ptag: qk_norm_rope(M, pq, T, kg[:, l:l + 1], dst, cs, cskey, ptag))
                elif oc in (6, 7):
                    m = oc - 6
                    dst = M.upT[:, m, 8 + tq * 512: 8 + (tq + 1) * 512] if not is_ctx else M.upc[:, m, 8:8 + CTX]
                    S.op('act', lambda h, dst=dst, pq=pq, T=T: h.copy(out=dst, in_=pq[:, :T]), reads=[ptag, 'upT', 'upc'], writes=[('up', tq, m)])
                    step(None)
                else:
                    m = oc - 8
                    S.op('act', lambda h, m=m, pq=pq, T=T: h.copy(out=M.ufT[:, m, :T], in_=pq[:, :T]), reads=[ptag], writes=[('ufT', m)])
                    if m == 0:
                        step(None)
                    if m == 1:
                        def abpart(T=T, tq=tq, is_ctx=is_ctx):
                            for tc in range(T // 128):
                                ch = (tq * 4 + tc) if not is_ctx else 16 + tc

                                def mmab(h, tc=tc):
                                    h.matmul(ps[3][:, :], lhsT=M.ufT[:, 0, tc * 128:(tc + 1) * 128], rhs=M.AB[:, 0, :], start=True, stop=False)
                                    return h.matmul(ps[3][:, :], lhsT=M.ufT[:, 1, tc * 128:(tc + 1) * 128], rhs=M.AB[:, 1, :], start=False, stop=True)
                                S.op('pe', mmab, reads=[('ufT', 0), ('ufT', 1), 'AB'], writes=[('ps', 3)])
                                S.op('dve', lambda h, ch=ch: h.tensor_copy(out=M.uab[:, ch, :], in_=ps[3][:, :]), reads=[('ps', 3)], writes=[('uab', ch)])
                            return None
                        step(abpart)
            step(None)
            step(None)
        S.barrier()

    def pool_tile(M, l, up, T, t0, L, mixv):
        W = T + 16
        edge = (t0 == 0) or (t0 + T == L)
        if edge:
            which = 2 if L == CTX else (0 if t0 == 0 else 1)
            S.op('sp', lambda h: h.dma_start(out=M.rcf.rearrange("p a b -> p (a b)"), in_=krcf_d[which]), writes=['rcf'], dma=M.rcfsem)
        for m in range(2):
            u = up[:, m, t0:t0 + W]
            S.op('pool', lambda h, u=u: h.tensor_tensor(out=M.a2[:, 0:W - 1], in0=u[:, 0:W - 1], in1=u[:, 1:W], op=ALU.add),
                 reads=[('up', 'all')], writes=['a2'])
            S.op('pool', lambda h: h.tensor_tensor(out=M.a4[:, 0:W - 3], in0=M.a2[:, 0:W - 3], in1=M.a2[:, 2:W - 1], op=ALU.add),
                 reads=['a2'], writes=['a4'])
            if m == 1:
                S.op('pool', lambda h: h.tensor_tensor(out=M.a8[:, 0:W - 7], in0=M.a4[:, 0:W - 7], in1=M.a4[:, 4:W - 3], op=ALU.add),
                     reads=['a4'], writes=['a8'])
                S.op('pool', lambda h: h.tensor_tensor(out=M.a16[:, 0:W - 15], in0=M.a8[:, 0:W - 15], in1=M.a8[:, 8:W - 7], op=ALU.add),
                     reads=['a8'], writes=['a16'])
                srcs = [(M.a8, 4, 'a8'), (M.a16, 8, 'a16')]
            else:
                srcs = [(M.a2, 1, 'a2'), (M.a4, 2, 'a4')]
            for hh in range(2):
                a, hw, akey = srcs[hh]
                lo, hi = hh * 64, (hh + 1) * 64
                if not edge:
                    S.op('dve', lambda h, a=a, hw=hw, lo=lo, hi=hi, m=m, u=u: h.scalar_tensor_tensor(
                        out=M.pm[lo:hi, m, :T], in0=a[lo:hi, 8 - hw:8 - hw + T], scalar=invw[lo:hi, m:m + 1], in1=u[lo:hi, 8:8 + T],
                        op0=ALU.mult, op1=ALU.subtract), reads=[akey, ('up', 'all'), 'invw'], writes=[('pm', m, hh)])
                else:
                    S.op('dve', lambda h, a=a, hw=hw, lo=lo, hi=hi, m=m: h.tensor_tensor(
                        out=M.rr[lo:hi, :T], in0=a[lo:hi, 8 - hw:8 - hw + T], in1=M.rcf[lo:hi, m, :T], op=ALU.mult),
                        reads=[akey, 'rcf'], writes=[('rr', hh)])
                    S.op('dve', lambda h, lo=lo, hi=hi, m=m, u=u: h.tensor_tensor(
                        out=M.pm[lo:hi, m, :T], in0=M.rr[lo:hi, :T], in1=u[lo:hi, 8:8 + T], op=ALU.subtract),
                        reads=[('rr', hh), ('up', 'all')], writes=[('pm', m, hh)])
            S.op('pe', lambda h, m=m: h.matmul(ps[7][:, :T], lhsT=poolw_bf[:, l, m, :], rhs=M.pm[:, m, :T], start=True, stop=True),
                 reads=[('pm', m, 0), ('pm', m, 1), ('poolw', l)], writes=['ps7'])
            S.op('act', lambda h, m=m: h.activation(out=mixv[:, 4 + m, :T], in_=ps[7][:, :T], func=AF.Identity, scale=pscale[:, l, m:m + 1]),
                 reads=['ps7', 'pscale'], writes=[('mixp', m)])

    def fft_tile(M, l, tq, T, is_ctx, mixv):
        if not is_ctx:
            units = [dft_s[tq * 8 + g].rearrange("p (a c) -> p a c", a=4) for g in range(8)]
            st = Stream(S, 'dft', M.dft, units)
            for g in range(8):
                w, wkey = st.get()
                cs_, lcg = g // 4, g % 4

                def mm(h, w=w, g=g, cs_=cs_, lcg=lcg):
                    ins = None
                    for lc4 in range(4):
                        lc = lcg * 4 + lc4
                        for m in range(2):
                            ins = h.matmul(ps[m][:, :T], lhsT=M.uab[:, lc, cs_ * 256 + m * 128: cs_ * 256 + (m + 1) * 128],
                                           rhs=w[:, lc4, :], start=(g == 0 and lc4 == 0), stop=(g == 7 and lc4 == 3))
                    return ins
                S.op('pe', mm, reads=[wkey] + [('uab', lc) for lc in range(16)], writes=[('ps', 0), ('ps', 1)])
        else:
            def mm(h):
                ins = None
                n = 0
                for cs_ in range(2):
                    for lc in range(2):
                        for m in range(2):
                            ins = h.matmul(ps[m][:, :T], lhsT=M.uab[:, 16 + lc, cs_ * 256 + m * 128: cs_ * 256 + (m + 1) * 128],
                                           rhs=cdft_bf[:, cs_, lc, :], start=(n == 0), stop=(n == 3))
                        n += 1
                return ins
            S.op('pe', mm, reads=['cdft', ('uab', 16), ('uab', 17)], writes=[('ps', 0), ('ps', 1)])
        S.op('act', lambda h: h.copy(out=mixv[:, 6, :T], in_=ps[0][:, :T]), reads=[('ps', 0)], writes=[('mixf', 0)])
        S.op('dve', lambda h: h.tensor_copy(out=mixv[:, 7, :T], in_=ps[1][:, :T]), reads=[('ps', 1)], writes=[('mixf', 1)])

    def attn_tile(M, qsrc, T, kchunks, mixv):
        nk = len(kchunks)

        def rec_S(c, i, ch):
            sbank = pp[(M.p_i) % 2]
            skeys = [('ps', 2 * (M.p_i % 2)), ('ps', 2 * (M.p_i % 2) + 1)]
            P = M.P[M.p_i % 3]
            pkey = ('P', M.p_i % 3)
            M.p_i += 1

            def mms(h):
                h.matmul(sbank[:, 0:T], lhsT=M.kT[0:64, ch * 128:(ch + 1) * 128], rhs=qsrc[0:64, c, :], start=True, stop=True)
                return h.matmul(sbank[:, 512:512 + T], lhsT=M.kT[64:128, ch * 128:(ch + 1) * 128], rhs=qsrc[64:128, c, :], start=True, stop=True)
            S.op('pe', mms, reads=['qk'], writes=skeys)
            if T == 512:
                S.op('act', lambda h: h.activation(out=P.rearrange("p a b -> p (a b)"), in_=sbank[:, :], func=AF.Exp, scale=0.125),
                     reads=skeys, writes=[pkey])
            else:
                S.op('act', lambda h: h.activation(out=P[:, :, :T], in_=sbank.rearrange("p (a b) -> p a b", a=2)[:, :, :T], func=AF.Exp, scale=0.125),
                     reads=skeys, writes=[pkey])
            return P, pkey

        def rec_PV(c, i, ch, P, pkey):
            def mmpv(h):
                h.matmul(ps[4][:, :T], lhsT=M.Vx[:, ch, :], rhs=P[:, 0, :T], start=(i == 0), stop=(i == nk - 1))
                h.matmul(ps[5][:, :T], lhsT=M.Vx[:, ch, :], rhs=P[:, 1, :T], start=(i == 0), stop=(i == nk - 1))
                h.matmul(ps[6][:, :T], lhsT=esel_bf[:, 0, :], rhs=P[:, 0, :T], start=(i == 0), stop=False)
                return h.matmul(ps[6][:, :T], lhsT=esel_bf[:, 1, :], rhs=P[:, 1, :T], start=False, stop=(i == nk - 1))
            S.op('pe', mmpv, reads=[pkey, 'Vxall', 'esel'], writes=[('ps', 4), ('ps', 5), 'ps6'])
            if i == nk - 1:
                S.op('dve', lambda h: h.reciprocal(out=M.rc[:, :T], in_=ps[6][:, :T]), reads=['ps6'], writes=['rc'])
                S.op('dve', lambda h: h.tensor_tensor(out=mixv[0:64, c, :T], in0=ps[4][0:64, :T], in1=M.rc[0:64, :T], op=ALU.mult),
                     reads=[('ps', 4), 'rc'], writes=[('mixa', c, 0)])
                S.op('dve', lambda h: h.tensor_tensor(out=mixv[64:128, c, :T], in0=ps[5][64:128, :T], in1=M.rc[64:128, :T], op=ALU.mult),
                     reads=[('ps', 5), 'rc'], writes=[('mixa', c, 1)])

        items = [(c, i, ch) for c in range(4) for i, ch in enumerate(kchunks)]
        prev = None
        for it in items:
            P, pkey = rec_S(*it)
            if prev is not None:
                rec_PV(*prev)
            prev = it + (P, pkey)
        rec_PV(*prev)

    def wout_tile(M, l, b, xv, T, xkey, mixv, mixkeys):
        units = [wout_s[l][:, dc * 1024:(dc + 1) * 1024].rearrange("p (m c) -> p m c", m=8) for dc in range(8)]
        st = Stream(S, 'wo', M.wo, units)
        for dc in range(8):
            w, wkey = st.get()
            pd = ps[dc % 2]

            def mm(h, w=w, pd=pd):
                ins = None
                for mc in range(8):
                    ins = h.matmul(pd[:, :T], lhsT=w[:, mc, :], rhs=mixv[:, mc, :T], start=(mc == 0), stop=(mc == 7))
                return ins
            S.op('pe', mm, reads=[wkey] + mixkeys, writes=[('ps', dc % 2)])
            S.op('dve', lambda h, dc=dc, pd=pd: h.scalar_tensor_tensor(
                out=xv[:, dc, :], in0=pd[:, :T], scalar=modv(l, 5, dc, b), in1=xv[:, dc, :],
                op0=ALU.mult, op1=ALU.add), reads=[('ps', dc % 2), xkey], writes=[xkey])

    mixb = []

    def mixer(l, b, last):
        if not mixb:
            mixb.append(MixBufs())
        M = mixb[0]
        if mixdbg < 2:
            return
        mixer_m1(M, l, b, last)
        if mixdbg < 3:
            return
        mixkeys = [('mixa', c, hh) for c in range(4) for hh in range(2)] + [('mixp', 0), ('mixp', 1), ('mixf', 0), ('mixf', 1)]
        tiles = [(tq, 512, False) for tq in range(4)]
        if not last:
            tiles.append((0, CTX, True))
        for (tq, T, is_ctx) in tiles:
            mixv = M.mix[M.mix_i % 2]
            M.mix_i += 1
            if not is_ctx:
                if mixdbg >= 3:
                    pool_tile(M, l, M.upT, 512, tq * 512, SEQ, mixv)
                if mixdbg >= 4:
                    fft_tile(M, l, tq, 512, False, mixv)
                if mixdbg >= 5:
                    attn_tile(M, M.qT[:, :, tq * 512:(tq + 1) * 512], 512, list(range(18)), mixv)
                if mixdbg >= 6:
                    wout_tile(M, l, b, xT[:, :, tq * 512:(tq + 1) * 512], 512, xk('xT', tq), mixv, mixkeys)
            else:
                if mixdbg >= 3:
                    pool_tile(M, l, M.upc, CTX, 0, CTX, mixv)
                if mixdbg >= 4:
                    fft_tile(M, l, 0, CTX, True, mixv)
                if mixdbg >= 5:
                    attn_tile(M, M.qcT[:, :, :], CTX, [16, 17], mixv)
                if mixdbg >= 6:
                    wout_tile(M, l, 4, cT[:, :, :], CTX, xk('cT', 0), mixv, mixkeys)
        S.barrier()

    if stages != 'io':
        prologue_ffn()
        prologue_mod()
        if stages != 'ffn1':
            prologue_mix()
    ffnb = FFNBufs()
    iob = IOBufs(ffnb.end)
    for b in range(nb):
        load_tokens(x_d[b], xT, SEQ, 'xT')
        load_tokens(ctx_d[b], cT, CTX, 'cT')
        if stages != 'io':
            for l in range(depth):
                last = (l == DEPTH - 1)
                for tq in range(4):
                    ffn_tile(l, 0, xT[:, :, tq * 512:(tq + 1) * 512], 512, b, xk('xT', tq))
                ffn_tile(l, 0, cT[:, :, :], CTX, 4, xk('cT', 0))
                if stages == 'ffn1':
                    continue
                S.barrier()
                mixer(l, b, last)
                if stages == 'mix':
                    continue
                for tq in range(4):
                    ffn_tile(l, 1, xT[:, :, tq * 512:(tq + 1) * 512], 512, b, xk('xT', tq))
                if not last:
                    ffn_tile(l, 1, cT[:, :, :], CTX, 4, xk('cT', 0))
        store_out(b)
        S.barrier()
        S.new_epoch()
    S.barrier()
    S.op('sp', lambda h: h.nop(), reads=[], writes=[])

    with nc.Block() as block:
        S.emit(block)
    return nc


def make_consts():
    c = {"k_ident": np.eye(128, dtype=np.float32)}
    l = np.arange(2048, dtype=np.int64)
    mm_ = (l[:, None] * l[None, :]) % 2048
    ang = 2.0 * np.pi * mm_.astype(np.float64) / 2048.0
    nrm = 1.0 / np.sqrt(2048.0)
    mats = [np.cos(ang) * nrm, np.sin(ang) * nrm]
    dft = np.zeros((4, 2, 4, 128, 4, 512), np.float32)
    for cs in range(2):
        m = mats[cs].reshape(4, 4, 128, 4, 512)
        dft[:, cs] = m.transpose(3, 0, 2, 1, 4)
    c["k_dft"] = dft.reshape(32, 128, 2048)
    lc_ = np.arange(256, dtype=np.int64)
    angc = 2.0 * np.pi * ((lc_[:, None] * lc_[None, :]) % 256).astype(np.float64) / 256.0
    cm = [np.cos(angc) / 16.0, np.sin(angc) / 16.0]
    cd = np.zeros((128, 2, 2, 256), np.float32)
    for cs in range(2):
        cd[:, cs] = cm[cs].reshape(2, 128, 256).transpose(1, 0, 2)
    c["k_cdft"] = cd.reshape(128, 1024)
    cc = np.arange(64, dtype=np.int64)
    a64 = 2.0 * np.pi * ((cc[:, None] * cc[None, :]) % 64).astype(np.float64) / 64.0
    bd = np.zeros((2, 128, 128), np.float32)
    for o in (0, 64):
        bd[0, o:o + 64, o:o + 64] = np.cos(a64) / 8.0
        bd[1, o:o + 64, o:o + 64] = -np.sin(a64) / 8.0
    c["k_bd"] = bd
    t = np.arange(SEQ)
    row = (t // 64).astype(np.float32)
    col = (t % 64).astype(np.float32)
    inv = (np.float32(10000.0) ** (-(np.arange(16, dtype=np.float32)) / np.float32(16))).astype(np.float32)
    angr = np.concatenate([row[:, None] * inv[None, :], col[:, None] * inv[None, :]], axis=-1).astype(np.float32)
    cosr = np.cos(angr).astype(np.float32).T
    sinr = np.sin(angr).astype(np.float32).T
    rope = np.zeros((2, 128, SEQ), np.float32)
    for p in range(128):
        rope[0, p] = cosr[(p % 64) % 32]
        rope[1, p] = sinr[(p % 64) % 32]
    c["k_rope"] = rope
    perm = np.zeros((128, 128), np.float32)
    for o in (0, 64):
        for m in range(64):
            if m < 32:
                perm[o + m + 32, o + m] = -1.0
            else:
                perm[o + m - 32, o + m] = 1.0
    c["k_perm"] = perm
    sw = np.zeros((128, 128), np.float32)
    for m in range(128):
        sw[(m + 64) % 128, m] = 1.0
    c["k_swap"] = sw
    ws = [2, 4, 8, 16]
    invw = np.zeros((128, 2), np.float32)
    rc = np.zeros((128, 2, 16), np.float32)
    for p in range(128):
        for m in range(2):
            w = ws[2 * m + p // 64]
            invw[p, m] = 1.0 / w
            for i in range(8):
                tt = i
                rc[p, m, i] = 1.0 / ((tt + w // 2) - max(tt - w // 2, 0))
                dd = 8 - i
                rc[p, m, 8 + i] = 1.0 / (min(w // 2, dd) + w // 2)
    c["k_invw"] = invw
    rcf = np.zeros((3, 128, 2, 512), np.float32)
    for p in range(128):
        for m in range(2):
            w = ws[2 * m + p // 64]
            for which, (Lx, t0) in enumerate(((SEQ, 0), (SEQ, SEQ - 512), (CTX, 0))):
                for i in range(512):
                    tt = t0 + i
                    if tt >= Lx:
                        rcf[which, p, m, i] = 1.0 / w
                    else:
                        rcf[which, p, m, i] = 1.0 / (min(tt + w // 2, Lx) - max(tt - w // 2, 0))
    c["k_rcf"] = rcf.reshape(3, 128, 1024)
    c["k_rcnt"] = rc.reshape(128, 32)
    return c


def kernel(**inputs):
    cfg = inputs.pop('_cfg', {})
    nb = cfg.get('nb', NB_CORE)
    ncores = cfg.get('ncores', NCORES)
    nc = build_nc(cfg)
    consts = make_consts()
    in_maps = []
    for i in range(ncores):
        m = {}
        for k, v in inputs.items():
            v = np.asarray(v)
            if k in ('x', 'c', 'ctx'):
                v = np.ascontiguousarray(v[i * nb:(i + 1) * nb])
            m[k] = v
        m.update(consts)
        in_maps.append(m)
    res = run_bass_kernel_spmd(nc, in_maps, core_ids=list(range(ncores)))
    return np.concatenate([np.asarray(r["out"]) for r in res.results], axis=0)
```
